# Optimizing a Trainium2 kernel written in Bass

```python
import functools
import jax, jax.numpy as jnp
from jax import lax
import numpy as np

D_MODEL = 2048
BATCH = 2
SEQ = 16384
DEPTH = 1
DEC_BATCH = 32
DEC_SEQ = 16
PAST_LEN = 2048

CHUNK = 64
SG_CHUNK = 128
SG_GROUPS = 16
SG_WIDTH = D_MODEL
SG_GROUP_DIM = SG_WIDTH // SG_GROUPS
N_HEADS = 16
Q_LORA = 512
KV_LORA = 512
QK_NOPE = 128
QK_ROPE = 64
V_HEAD = 128
QK_HEAD = QK_NOPE + QK_ROPE
ROPE_BASE = 10000.0
ATTN_SCALE = QK_HEAD ** -0.5
Q_BLOCK = 128
MLP_HIDDEN = 4 * D_MODEL
EPS = 1e-6
OFF_Q = 2 * SG_WIDTH
OFF_KV = OFF_Q + Q_LORA
OFF_GATE = OFF_KV + KV_LORA + QK_ROPE
IN_COLS = OFF_GATE + 2 * D_MODEL

kernel_name = 'hybrid_gmlp_mla_adaln_stream_step'


def rmsnorm(x, g):
    xf = x.astype(jnp.float32)
    y = xf * lax.rsqrt(jnp.mean(xf * xf, axis=-1, keepdims=True) + EPS)
    return (y * g.astype(jnp.float32)).astype(x.dtype)


def rope_tables(pos):
    inv = jnp.float32(ROPE_BASE) ** (-jnp.arange(0, QK_ROPE, 2, dtype=jnp.float32) / QK_ROPE)
    ang = pos.astype(jnp.float32)[:, None] * inv[None, :]
    return jnp.cos(ang), jnp.sin(ang)


def apply_rope(x, cos, sin):
    half = QK_ROPE // 2
    xf = x.astype(jnp.float32)
    x1, x2 = xf[..., :half], xf[..., half:]
    return jnp.concatenate([x1 * cos - x2 * sin, x2 * cos + x1 * sin], axis=-1).astype(x.dtype)


def ada_params(c, w_ada, b_ada):
    mod = jax.nn.silu(c) @ w_ada + b_ada
    return jnp.split(mod[:, None, :], 6, axis=-1)


def sg_inputs(z_uv, g_sg):
    a = jax.nn.gelu(z_uv, approximate=False)
    return a[..., :SG_WIDTH], rmsnorm(a[..., SG_WIDTH:], g_sg)


def spatial_gate(u, v, w_s, b_s):
    B, N, T, _ = v.shape
    w = (w_s * jnp.tril(jnp.ones((SG_CHUNK, SG_CHUNK), w_s.dtype)))[:, :T, :T]
    vg = v.reshape(B, N, T, SG_GROUPS, SG_GROUP_DIM)
    mix = jnp.einsum('gij,bnjgc->bnigc', w, vg) + b_s[:, :T].T[None, None, :, :, None]
    return u * mix.reshape(B, N, T, SG_WIDTH)


def mla_project(z_q, z_kv, pos, p):
    B, S = z_q.shape[:2]
    cos, sin = rope_tables(pos)
    q = (rmsnorm(z_q, p['g_q_a']) @ p['w_uq']).reshape(B, S, N_HEADS, QK_HEAD)
    q_nope = rmsnorm(q[..., :QK_NOPE], p['g_q_nope'])
    q_rope = apply_rope(rmsnorm(q[..., QK_NOPE:], p['g_q_rope']), cos[:, None, :], sin[:, None, :])
    c_kv = rmsnorm(z_kv[..., :KV_LORA], p['g_kv_a'])
    k_rope = apply_rope(rmsnorm(z_kv[..., KV_LORA:], p['g_k_rope']), cos, sin)
    return q_nope, q_rope, c_kv, k_rope


def expand_keys(c_kv, p):
    return rmsnorm(jnp.einsum('bsc,chd->bshd', c_kv, p['w_uk']), p['g_k_nope'])


def attn_scores(q_nope, q_rope, k_nope, k_rope):
    s = jnp.einsum('bqhd,bkhd->bhqk', q_nope, k_nope, preferred_element_type=jnp.float32)
    s = s + jnp.einsum('bqhr,bkr->bhqk', q_rope, k_rope, preferred_element_type=jnp.float32)
    return s * ATTN_SCALE


def attn_values(prob, c_kv, p):
    o_lat = jnp.einsum('bhqk,bkc->bqhc', prob.astype(c_kv.dtype), c_kv)
    return jnp.einsum('bqhc,chd->bqhd', o_lat, p['w_uv'])


def mla_attend_prompt(q_nope, q_rope, c_kv, k_rope, p):
    B, S = c_kv.shape[:2]
    k_nope = expand_keys(c_kv, p)
    nblk = S // Q_BLOCK
    qn = q_nope.reshape(B, nblk, Q_BLOCK, N_HEADS, QK_NOPE).transpose(1, 0, 2, 3, 4)
    qr = q_rope.reshape(B, nblk, Q_BLOCK, N_HEADS, QK_ROPE).transpose(1, 0, 2, 3, 4)
    k_chunk = jnp.arange(S) // CHUNK

    def one_block(args):
        qn_b, qr_b, blk = args
        s = attn_scores(qn_b, qr_b, k_nope, k_rope)
        q_chunk = (blk * Q_BLOCK + jnp.arange(Q_BLOCK)) // CHUNK
        s = jnp.where(k_chunk[None, :] <= q_chunk[:, None], s, -jnp.inf)
        return attn_values(jax.nn.softmax(s, axis=-1), c_kv, p)

    o = lax.map(one_block, (qn, qr, jnp.arange(nblk)))
    return o.transpose(1, 0, 2, 3, 4).reshape(B, S, N_HEADS * V_HEAD)


def mla_attend_sample(q_nope, q_rope, c_kv_all, k_rope_all, p):
    B, T = q_nope.shape[:2]
    k_nope = expand_keys(c_kv_all, p)
    s = attn_scores(q_nope, q_rope, k_nope, k_rope_all)
    return attn_values(jax.nn.softmax(s, axis=-1), c_kv_all, p).reshape(B, T, N_HEADS * V_HEAD)


def merge(o_sg, o_mla, z_g, p):
    g = jax.nn.sigmoid(z_g)
    m = g[..., :D_MODEL] * (o_sg @ p['w_pa']) + g[..., D_MODEL:] * (o_mla @ p['w_pb'])
    return m @ p['w_o']


def mixer_prompt(h, p):
    B, S, _ = h.shape
    z = h @ p['w_in']
    u, v = sg_inputs(z[..., :OFF_Q], p['g_sg'])
    n = S // SG_CHUNK
    o_sg = spatial_gate(u.reshape(B, n, SG_CHUNK, SG_WIDTH), v.reshape(B, n, SG_CHUNK, SG_WIDTH),
                        p['w_s'], p['b_s']).reshape(B, S, SG_WIDTH)
    q_nope, q_rope, c_kv, k_rope = mla_project(z[..., OFF_Q:OFF_KV], z[..., OFF_KV:OFF_GATE], jnp.arange(S), p)
    o_mla = mla_attend_prompt(q_nope, q_rope, c_kv, k_rope, p)
    return merge(o_sg, o_mla, z[..., OFF_GATE:], p), c_kv, k_rope


def mixer_sample(h, cache_lat, cache_kr, p):
    B, T, _ = h.shape
    z = h @ p['w_in']
    u, v = sg_inputs(z[..., :OFF_Q], p['g_sg'])
    o_sg = spatial_gate(u[:, None], v[:, None], p['w_s'], p['b_s'])[:, 0]
    P = cache_lat.shape[1]
    q_nope, q_rope, c_kv, k_rope = mla_project(z[..., OFF_Q:OFF_KV], z[..., OFF_KV:OFF_GATE], P + jnp.arange(T), p)
    o_mla = mla_attend_sample(q_nope, q_rope, jnp.concatenate([cache_lat, c_kv], axis=1),
                              jnp.concatenate([cache_kr, k_rope], axis=1), p)
    return merge(o_sg, o_mla, z[..., OFF_GATE:], p), c_kv, k_rope, v


def trunk_layer(x, c, mixer, p):
    sh1, sc1, g1, sh2, sc2, g2 = ada_params(c, p['w_ada'], p['b_ada'])
    mix_out, *state = mixer(rmsnorm(x, p['g_norm1']) * (1 + sc1) + sh1)
    x = x + g1 * mix_out
    h = rmsnorm(x, p['g_norm2']) * (1 + sc2) + sh2
    x = x + g2 * (jnp.square(jax.nn.relu(h @ p['w_up'])) @ p['w_down'])
    return x, state


def setup_inputs(seed: int = 0) -> dict:
    key = jax.random.key(seed)
    ks = iter(jax.random.split(key, 40))
    f32 = jnp.float32

    def nrm(shape, scale=1.0):
        return jax.random.normal(next(ks), shape, f32) * scale

    def gain(n):
        return 1.0 + nrm((DEPTH, n), 0.02)

    return {
        'x_prompt': nrm((BATCH, SEQ, D_MODEL)),
        'x_sample': nrm((DEC_BATCH, DEC_SEQ, D_MODEL)),
        'cache_kv_latent': nrm((DEPTH, DEC_BATCH, PAST_LEN, KV_LORA)),
        'cache_k_rope': nrm((DEPTH, DEC_BATCH, PAST_LEN, QK_ROPE)),
        'c_prompt': nrm((BATCH, D_MODEL)),
        'c_sample': nrm((DEC_BATCH, D_MODEL)),
        'w_ada': nrm((DEPTH, D_MODEL, 6 * D_MODEL), 0.5 * D_MODEL ** -0.5),
        'b_ada': nrm((DEPTH, 6 * D_MODEL), 0.01),
        'g_norm1': gain(D_MODEL),
        'g_norm2': gain(D_MODEL),
        'w_in': nrm((DEPTH, D_MODEL, IN_COLS), D_MODEL ** -0.5),
        'g_sg': gain(SG_WIDTH),
        'w_s': nrm((DEPTH, SG_GROUPS, SG_CHUNK, SG_CHUNK), SG_CHUNK ** -0.5),
        'b_s': 1.0 + nrm((DEPTH, SG_GROUPS, SG_CHUNK), 0.1),
        'g_q_a': gain(Q_LORA),
        'w_uq': nrm((DEPTH, Q_LORA, N_HEADS * QK_HEAD), Q_LORA ** -0.5),
        'g_q_nope': gain(QK_NOPE),
        'g_q_rope': gain(QK_ROPE),
        'g_kv_a': gain(KV_LORA),
        'g_k_rope': gain(QK_ROPE),
        'w_uk': nrm((DEPTH, KV_LORA, N_HEADS, QK_NOPE), KV_LORA ** -0.5),
        'g_k_nope': gain(QK_NOPE),
        'w_uv': nrm((DEPTH, KV_LORA, N_HEADS, V_HEAD), KV_LORA ** -0.5),
        'w_pa': nrm((DEPTH, SG_WIDTH, D_MODEL), SG_WIDTH ** -0.5),
        'w_pb': nrm((DEPTH, N_HEADS * V_HEAD, D_MODEL), (N_HEADS * V_HEAD) ** -0.5),
        'w_o': nrm((DEPTH, D_MODEL, D_MODEL), D_MODEL ** -0.5),
        'w_up': nrm((DEPTH, D_MODEL, MLP_HIDDEN), D_MODEL ** -0.5),
        'w_down': nrm((DEPTH, MLP_HIDDEN, D_MODEL), MLP_HIDDEN ** -0.5),
    }


def reference(x_prompt, x_sample, cache_kv_latent, cache_k_rope, c_prompt, c_sample,
              w_ada, b_ada, g_norm1, g_norm2, w_in, g_sg, w_s, b_s,
              g_q_a, w_uq, g_q_nope, g_q_rope, g_kv_a, g_k_rope, w_uk, g_k_nope, w_uv,
              w_pa, w_pb, w_o, w_up, w_down):
    y_p, y_s = x_prompt, x_sample
    lat_p, kr_p, lat_s, kr_s, v_s = [], [], [], [], []
    for l in range(DEPTH):
        p = {'w_ada': w_ada[l], 'b_ada': b_ada[l], 'g_norm1': g_norm1[l], 'g_norm2': g_norm2[l],
             'w_in': w_in[l], 'g_sg': g_sg[l], 'w_s': w_s[l], 'b_s': b_s[l],
             'g_q_a': g_q_a[l], 'w_uq': w_uq[l], 'g_q_nope': g_q_nope[l], 'g_q_rope': g_q_rope[l],
             'g_kv_a': g_kv_a[l], 'g_k_rope': g_k_rope[l], 'w_uk': w_uk[l], 'g_k_nope': g_k_nope[l],
             'w_uv': w_uv[l], 'w_pa': w_pa[l], 'w_pb': w_pb[l], 'w_o': w_o[l],
             'w_up': w_up[l], 'w_down': w_down[l]}
        y_p, (lp, kp) = trunk_layer(y_p, c_prompt, functools.partial(mixer_prompt, p=p), p)
        y_s, (ls, ksm, vs) = trunk_layer(
            y_s, c_sample,
            functools.partial(mixer_sample, cache_lat=cache_kv_latent[l], cache_kr=cache_k_rope[l], p=p), p)
        lat_p.append(lp); kr_p.append(kp); lat_s.append(ls); kr_s.append(ksm); v_s.append(vs)
    return (y_p, y_s, jnp.stack(lat_p), jnp.stack(kr_p), jnp.stack(lat_s), jnp.stack(kr_s), jnp.stack(v_s))
```

```python
import numpy as np
import concourse.bass as bass
import concourse.mybir as mybir
from concourse.bass_utils import run_bass_kernel_spmd

F32 = mybir.dt.float32
BF16 = mybir.dt.bfloat16
AF = mybir.ActivationFunctionType
ALU = mybir.AluOpType

D = 2048
NH = 16
QL = 512
KVL = 512
ROPE = 64
QKH = 192
OFF_Q = 4096
OFF_KV = OFF_Q + QL
OFF_GATE = OFF_KV + KVL + ROPE
IN_COLS = OFF_GATE + 2 * D
HID = 4 * D
EPS = 1e-6
SCALE = QKH ** -0.5
TS = 16
SB = 4


class _Rec:
    def __init__(self):
        self.calls = []

    def __getattr__(self, name):
        def f(*a, **kw):
            self.calls.append((name, a, kw))
            return None
        return f


class Sem:
    def __init__(self, h):
        self.h = h
        self.val = 0


class Res:
    def __init__(self, t, dsem=None):
        self.t = t
        self.w = None
        self.r = {}
        self.dsem = dsem


class Eng:
    def __init__(self, name, sem, self_sync):
        self.name = name
        self.sem = sem
        self.ops = []
        self.seen = {}
        self.self_sync = self_sync

    def wait(self, tok):
        if tok is None:
            return
        sem, val = tok
        if sem is self.sem and not self.self_sync:
            return
        if self.seen.get(id(sem), 0) >= val:
            return
        self.seen[id(sem)] = val
        self.ops.append(("w", sem.h, val))


class Prog:
    def __init__(self, nc, stack):
        self.nc = nc
        mk = lambda n: Sem(stack.enter_context(nc.semaphore(n)))
        self.pe = Eng("pe", mk("s_pe"), False)
        self.act = Eng("act", mk("s_act"), True)
        self.dve = Eng("dve", mk("s_dve"), True)
        self.pool = Eng("pool", mk("s_pool"), True)
        self.sp = Eng("sp", mk("s_sp"), True)
        self.engs = [self.pe, self.act, self.dve, self.pool, self.sp]
        self.dsems = [mk("d%d" % i) for i in range(90)]
        self.dnext = 0
        self.outstanding = {}

    def new_dsem(self):
        s = self.dsems[self.dnext % len(self.dsems)]
        self.dnext += 1
        return s

    def deps(self, E, reads, writes):
        for r in reads:
            E.wait(r.w)
        for w in writes:
            E.wait(w.w)
            for tok in list(w.r.values()):
                E.wait(tok)

    def mark(self, tok, reads, writes):
        for r in reads:
            r.r[id(tok[0])] = tok
        for w in writes:
            w.w = tok
            w.r = {}

    def op(self, E, fns, reads=(), writes=()):
        if callable(fns):
            fns = [fns]
        self.deps(E, reads, writes)
        E.sem.val += 1
        tok = (E.sem, E.sem.val)
        calls = []
        for f in fns:
            rec = _Rec()
            f(rec)
            assert len(rec.calls) == 1
            calls.append(rec.calls[0])
        for c in calls[:-1]:
            E.ops.append(("i", c, None))
        E.ops.append(("i", calls[-1], E.sem.h))
        self.mark(tok, reads, writes)

    def dma(self, Q, out, in_, reads=(), writes=()):
        self.deps(Q, reads, writes)
        sem = None
        for x in list(writes) + list(reads):
            if x.dsem is not None:
                sem = x.dsem
                break
        assert sem is not None
        sem.val += 16
        tok = (sem, sem.val)
        Q.ops.append(("d", out, in_, sem.h))
        self.mark(tok, reads, writes)
        self.outstanding[id(sem)] = tok

    def barrier(self):
        toks = [(E.sem, E.sem.val) for E in self.engs if E.sem.val > 0] + list(self.outstanding.values())
        for E in self.engs:
            for t in toks:
                E.wait(t)
        self.outstanding = {}

    def replay(self, E, e):
        for o in E.ops:
            if o[0] == "w":
                e.wait_ge(o[1], o[2])
            elif o[0] == "i":
                name, a, kw = o[1]
                ins = getattr(e, name)(*a, **kw)
                if o[2] is not None:
                    ins.then_inc(o[2], 1)
            else:
                e.dma_start(out=o[1], in_=o[2]).then_inc(o[3], 16)


class Arena:
    def __init__(self, big, nwords, prog):
        self.big = big
        self.cap = nwords
        self.off = 0
        self.prog = prog

    def reset(self, to=0):
        self.off = to

    def raw(self, n, dt):
        words = (n * (4 if dt == F32 else 2) + 3) // 4
        words = (words + 7) // 8 * 8
        assert self.off + words <= self.cap, "SBUF arena overflow %d+%d>%d" % (self.off, words, self.cap)
        v = self.big[:, self.off:self.off + words]
        self.off += words
        if dt == BF16:
            v = v.bitcast(BF16)
        return v[:, 0:n]

    def tile(self, shape, dt, dsem=True):
        n = int(np.prod(shape))
        v = self.raw(n, dt)
        if len(shape) == 2:
            v = v.rearrange("p (a b) -> p a b", a=shape[0], b=shape[1])
        elif len(shape) == 3:
            v = v.rearrange("p (a b c) -> p a b c", a=shape[0], b=shape[1], c=shape[2])
        return Res(v, self.prog.new_dsem() if dsem else None)


class Rot:
    def __init__(self, items):
        self.items = items
        self.i = 0

    def get(self):
        x = self.items[self.i % len(self.items)]
        self.i += 1
        return x


def build_program(SEQ, PAST):
    import contextlib
    nc = bass.Bass("TRN2", target_bir_lowering=False)
    NB = SEQ // 512
    NJ = NB // 4
    NOWN = NJ * 512
    NS = SB * TS
    NTOK = NOWN + NS
    NKT = PAST // 128
    PASTP = PAST + TS
    NT = SEQ // 128

    def din(name, shape, dt=F32):
        return nc.dram_tensor(name, list(shape), dt, kind="ExternalInput").ap()

    def dout(name, shape):
        return nc.dram_tensor(name, list(shape), F32, kind="ExternalOutput").ap()

    def dtmp(name, shape, dt):
        return nc.dram_tensor(name, list(shape), dt, kind="Internal").ap()

    xb = din("xb", [SEQ, D])
    xo = din("xo", [NTOK, D])
    clat = din("clat", [SB, PAST, KVL])
    ckr = din("ckr", [SB, PAST, ROPE])
    cvec = din("cvec", [5, D])
    w_ada = din("w_ada", [D, 6 * D])
    b_ada = din("b_ada", [1, 6 * D])
    g_norm1 = din("g_norm1", [1, D])
    g_norm2 = din("g_norm2", [1, D])
    w_in = din("w_in", [D, IN_COLS])
    g_sg = din("g_sg", [1, D])
    w_s = din("w_s", [NH, 128, 128])
    b_s = din("b_s", [1, NH * 128])
    g_q_a = din("g_q_a", [1, QL])
    w_uq = din("w_uq", [QL, NH * QKH])
    g_q_nope = din("g_q_nope", [1, 128])
    g_q_rope = din("g_q_rope", [1, ROPE])
    g_kv_a = din("g_kv_a", [1, KVL])
    g_k_rope = din("g_k_rope", [1, ROPE])
    w_uk = din("w_uk", [KVL, NH * 128])
    g_k_nope = din("g_k_nope", [1, 128])
    w_uv = din("w_uv", [KVL, NH * 128])
    w_pa = din("w_pa", [D, D])
    w_pb = din("w_pb", [D, D])
    w_o = din("w_o", [D, D])
    w_up = din("w_up", [D, HID])
    w_down = din("w_down", [HID, D])
    ropetm = din("ropetm", [SEQ, ROPE])
    ropeo = din("ropeo", [NTOK, ROPE])
    town = din("town", [128, NTOK])
    maskc = din("maskc", [128, 16 * 8])
    identd = din("ident", [128, 128])
    trild = din("tril", [128, 128])

    y_all = dout("y_all", [NTOK, D])
    lat_all = dout("lat_all", [NTOK, KVL])
    kr_all = dout("kr_all", [NTOK, ROPE])
    v_all = dout("v_all", [NS, D])

    mod_d = dtmp("mod_d", [5, 6 * D], F32)
    KT_d = dtmp("KT_d", [NH, 128, SEQ], BF16)
    KrT_d = dtmp("KrT_d", [128, SEQ], BF16)
    V_d = dtmp("V_d", [NH, 128, NT, 128], BF16)
    KTs_d = dtmp("KTs_d", [SB, NH, 128, PASTP], BF16)
    KrTs_d = dtmp("KrTs_d", [SB, 128, PASTP], BF16)
    Vs_d = dtmp("Vs_d", [SB, NH, 128, NKT + 1, 128], BF16)
    hT_d = dtmp("hT_d", [128, 16, NTOK], BF16)
    uT_d = dtmp("uT_d", [128, 16, NTOK], BF16)
    v_d = dtmp("v_d", [NTOK, D], BF16)
    QT_d = dtmp("QT_d", [NH, 128, NTOK], BF16)
    QrT_d = dtmp("QrT_d", [NH, 128, NTOK], BF16)
    gaT_d = dtmp("gaT_d", [128, 16, NTOK], BF16)
    gbT_d = dtmp("gbT_d", [128, 16, NTOK], BF16)
    osgT_d = dtmp("osgT_d", [128, 16, NTOK], BF16)
    omlaT_d = dtmp("omlaT_d", [128, 16, NTOK], BF16)
    mT_d = dtmp("mT_d", [128, 16, NTOK], BF16)
    h2T_d = dtmp("h2T_d", [128, 16, NTOK], BF16)
    x1_d = dtmp("x1_d", [NTOK, D], F32)
    hidT_d = dtmp("hidT_d", [128, 64, NTOK], BF16)

    AW = 51200
    with contextlib.ExitStack() as stack:
        big = stack.enter_context(nc.sbuf_tensor("arena", [128, AW], F32))
        psb = [stack.enter_context(nc.psum_tensor("ps%d" % i, [128, 512], F32)) for i in range(8)]
        P = Prog(nc, stack)
        A = Arena(big, AW, P)
        PS = [Res(psb[i][:, :]) for i in range(8)]
        pe, act, dve, pool, sp = P.pe, P.act, P.dve, P.pool, P.sp

        def psbf(res):
            return res.t.bitcast(BF16)

        identf = A.tile([128], F32)
        identb = A.tile([128], BF16, dsem=False)
        ones_b = A.tile([128], BF16, dsem=False)
        ones_f = A.tile([128], F32, dsem=False)
        P.dma(sp, identf.t, identd[:, :], writes=[identf])
        P.op(dve, lambda e: e.tensor_copy(out=identb.t, in_=identf.t), reads=[identf], writes=[identb])
        P.op(dve, lambda e: e.memset(ones_b.t, 1.0), writes=[ones_b])
        P.op(dve, lambda e: e.memset(ones_f.t, 1.0), writes=[ones_f])
        PERSIST = A.off

        def blocks():
            for j in range(NJ):
                yield (j, 512 * j, 512, 4, 128, False)
            yield (NJ, NOWN, NS, 1, NS, True)

        def wview(w, c0, c1):
            return w.rearrange("(k p) n -> p k n", p=128)[:, :, c0:c1]

        def load_rows_prompt_or_sample(tile, src_prompt_row, src_sample_rows, is_s, n):
            if not is_s:
                P.dma(sp, tile.t, src_prompt_row.partition_broadcast(128), writes=[tile])
            else:
                for i in range(SB):
                    P.dma(sp, tile.t[16 * i:16 * i + 16, :], src_sample_rows[i].partition_broadcast(16), writes=[tile])

        def rstd_col(ss, rs, R, n):
            P.op(act, lambda e: e.activation(out=rs.t[0:R, :], in_=ss.t[0:R, :], func=AF.Sqrt, scale=1.0 / n, bias=EPS),
                 reads=[ss], writes=[rs])
            P.op(dve, lambda e: e.reciprocal(out=rs.t[0:R, :], in_=rs.t[0:R, :]), reads=[rs], writes=[rs])

        def rstd_tile(ps, rs, T, n, parts=128):
            P.op(act, lambda e: e.activation(out=rs.t[0:parts, 0:T], in_=ps.t[0:parts, 0:T], func=AF.Sqrt, scale=1.0 / n, bias=EPS),
                 reads=[ps], writes=[rs])
            P.op(dve, lambda e: e.reciprocal(out=rs.t[0:parts, 0:T], in_=rs.t[0:parts, 0:T]), reads=[rs], writes=[rs])

        psrot = Rot(PS)

        def transpose_to(dst_fn, src, R, nk, reads, writes_res):
            for k0 in range(0, nk, 8):
                kk = min(8, nk - k0)
                ps = psrot.get()
                pv = psbf(ps)
                fns = []
                for k in range(kk):
                    fns.append(lambda e, k=k: e.transpose(pv[:, k * 128:k * 128 + R], src.t[0:R, (k0 + k) * 128:(k0 + k + 1) * 128], identb.t[0:R, 0:R]))
                P.op(pe, fns, reads=[src, identb] + reads, writes=[ps])
                yield k0, kk, ps, pv

        c5 = A.tile([D], F32)
        c5b = A.tile([D], BF16, dsem=False)
        cT = A.tile([16, 8], BF16, dsem=False)
        bada = A.tile([6 * D], F32)
        wb = [A.tile([16, 512], BF16) for _ in range(2)]
        msb = [A.tile([512], F32) for _ in range(2)]
        P.dma(sp, c5.t[0:5, :], cvec[:, :], writes=[c5])
        P.dma(sp, bada.t[0:5, :], b_ada[0:1, :].partition_broadcast(5), writes=[bada])
        P.op(act, lambda e: e.activation(out=c5b.t[0:5, :], in_=c5.t[0:5, :], func=AF.Silu), reads=[c5], writes=[c5b])
        for k0 in (0, 8):
            ps = psrot.get()
            pv = psbf(ps)
            P.op(pe, [lambda e, k=k, pv=pv: e.transpose(pv[:, k * 8:k * 8 + 5], c5b.t[0:5, (k0 + k) * 128:(k0 + k + 1) * 128], identb.t[0:5, 0:5]) for k in range(8)],
                 reads=[c5b, identb], writes=[ps])
            P.op(dve, lambda e, pv=pv, k0=k0: e.tensor_copy(out=cT.t[:, k0:k0 + 8, 0:5], in_=pv[:, 0:64].rearrange("p (k n) -> p k n", k=8, n=8)[:, :, 0:5]),
                 reads=[ps], writes=[cT])
        for cc in range(24):
            w = wb[cc % 2]
            m = msb[cc % 2]
            P.dma(pool, w.t, wview(w_ada, cc * 512, (cc + 1) * 512), writes=[w])
            ps = psrot.get()
            P.op(pe, [lambda e, k=k, w=w, ps=ps: e.matmul(ps.t[0:5, :], cT.t[:, k, 0:5], w.t[:, k, :], start=(k == 0), stop=(k == 15)) for k in range(16)],
                 reads=[cT, w], writes=[ps])
            P.op(dve, lambda e, ps=ps, m=m, cc=cc: e.tensor_tensor(out=m.t[0:5, :], in0=ps.t[0:5, :], in1=bada.t[0:5, cc * 512:(cc + 1) * 512], op=ALU.add),
                 reads=[ps, bada], writes=[m])
            P.dma(pool, mod_d[:, cc * 512:(cc + 1) * 512], m.t[0:5, :], reads=[m])
        P.barrier()
        A.reset(PERSIST)

        def make_front_rows(gn_d, off_sh, off_sc, is_s, gn):
            G = A.tile([D], F32)
            S = A.tile([D], F32)
            load_rows_prompt_or_sample(S, mod_d[0:1, off_sh:off_sh + D], [mod_d[1 + i:2 + i, off_sh:off_sh + D] for i in range(SB)], is_s, D)
            load_rows_prompt_or_sample(G, mod_d[0:1, off_sc:off_sc + D], [mod_d[1 + i:2 + i, off_sc:off_sc + D] for i in range(SB)], is_s, D)
            P.dma(sp, gn.t, gn_d[0:1, :].partition_broadcast(128), writes=[gn])
            P.op(dve, lambda e: e.scalar_tensor_tensor(out=G.t, in0=G.t, scalar=1.0, in1=gn.t, op0=ALU.add, op1=ALU.mult),
                 reads=[G, gn], writes=[G])
            return G, S

        def reload_front_rows(G, S, gn_d, off_sh, off_sc, gn):
            load_rows_prompt_or_sample(S, None, [mod_d[1 + i:2 + i, off_sh:off_sh + D] for i in range(SB)], True, D)
            load_rows_prompt_or_sample(G, None, [mod_d[1 + i:2 + i, off_sc:off_sc + D] for i in range(SB)], True, D)
            P.op(dve, lambda e: e.scalar_tensor_tensor(out=G.t[0:NS, :], in0=G.t[0:NS, :], scalar=1.0, in1=gn.t[0:NS, :], op0=ALU.add, op1=ALU.mult),
                 reads=[G, gn], writes=[G])

        class FrontBufs:
            def __init__(self):
                self.junk = A.tile([D], BF16, dsem=False)
                self.tmp = A.tile([D], F32)
                self.hb = [A.tile([D], BF16, dsem=False) for _ in range(2)]
                self.ss = [A.tile([1], F32, dsem=False) for _ in range(2)]
                self.rs = [A.tile([1], F32, dsem=False) for _ in range(2)]
                self.i = 0

        def front_tile(fb, xt, R, G, S, hT, col0):
            i = fb.i
            fb.i += 1
            ss, rs, hb = fb.ss[i % 2], fb.rs[i % 2], fb.hb[i % 2]
            P.op(dve, lambda e: e.memset(ss.t[0:R, :], 0.0), writes=[ss])
            P.op(act, lambda e: e.activation(out=fb.junk.t[0:R, :], in_=xt.t[0:R, :], func=AF.Square, accum_out=ss.t[0:R, :]),
                 reads=[xt, ss], writes=[fb.junk, ss])
            rstd_col(ss, rs, R, D)
            P.op(dve, lambda e: e.scalar_tensor_tensor(out=fb.tmp.t[0:R, :], in0=xt.t[0:R, :], scalar=rs.t[0:R, :], in1=G.t[0:R, :], op0=ALU.mult, op1=ALU.mult),
                 reads=[xt, rs, G], writes=[fb.tmp])
            P.op(dve, lambda e: e.tensor_tensor(out=hb.t[0:R, :], in0=fb.tmp.t[0:R, :], in1=S.t[0:R, :], op=ALU.add),
                 reads=[fb.tmp, S], writes=[hb])
            for k0, kk, ps, pv in transpose_to(None, hb, R, 16, [], None):
                P.op(act, lambda e, k0=k0, kk=kk, pv=pv: e.copy(out=hT.t[:, k0:k0 + kk, col0:col0 + R],
                                                               in_=pv.rearrange("p (k n) -> p k n", k=8, n=128)[:, 0:kk, 0:R]),
                     reads=[ps], writes=[hT])

        class KvBufs:
            def __init__(self):
                self.gkv = A.tile([KVL], F32)
                self.gkr = A.tile([ROPE], F32)
                P.dma(sp, self.gkv.t, g_kv_a[0:1, :].partition_broadcast(128), writes=[self.gkv])
                P.dma(sp, self.gkr.t, g_k_rope[0:1, :].partition_broadcast(128), writes=[self.gkr])
                self.junk = A.tile([KVL], BF16, dsem=False)
                self.ss = [A.tile([2], F32, dsem=False) for _ in range(2)]
                self.rs = [A.tile([2], F32, dsem=False) for _ in range(2)]
                self.kn = [A.tile([ROPE], F32, dsem=False) for _ in range(2)]
                self.t4 = [A.tile([4, 32], F32, dsem=False) for _ in range(2)]
                self.cs = [A.tile([ROPE], F32) for _ in range(2)]
                self.i = 0

        def zkv_tile(kb, Wkv, hT, col0, R, rope_rows_ap, ckv_out, kr_out):
            i = kb.i
            kb.i += 1
            ss, rs, kn, t4, cs = kb.ss[i % 2], kb.rs[i % 2], kb.kn[i % 2], kb.t4[i % 2], kb.cs[i % 2]
            P.dma(sp, cs.t[0:R, :], rope_rows_ap, writes=[cs])
            psc = psrot.get()
            P.op(pe, [lambda e, k=k: e.matmul(psc.t[0:R, :], hT.t[:, k, col0:col0 + R], Wkv.t[:, k, 0:KVL], start=(k == 0), stop=(k == 15)) for k in range(16)],
                 reads=[hT, Wkv], writes=[psc])
            psr = psrot.get()
            P.op(pe, [lambda e, k=k: e.matmul(psr.t[0:R, 0:ROPE], hT.t[:, k, col0:col0 + R], Wkv.t[:, k, KVL:KVL + ROPE], start=(k == 0), stop=(k == 15)) for k in range(16)],
                 reads=[hT, Wkv], writes=[psr])
            P.op(dve, lambda e: e.memset(ss.t[0:R, :], 0.0), writes=[ss])
            P.op(act, lambda e: e.activation(out=kb.junk.t[0:R, :], in_=psc.t[0:R, :], func=AF.Square, accum_out=ss.t[0:R, 0:1]),
                 reads=[psc, ss], writes=[kb.junk, ss])
            P.op(act, lambda e: e.activation(out=kb.junk.t[0:R, 0:ROPE], in_=psr.t[0:R, 0:ROPE], func=AF.Square, accum_out=ss.t[0:R, 1:2]),
                 reads=[psr, ss], writes=[kb.junk, ss])
            P.op(act, lambda e: e.activation(out=rs.t[0:R, 0:1], in_=ss.t[0:R, 0:1], func=AF.Sqrt, scale=1.0 / KVL, bias=EPS), reads=[ss], writes=[rs])
            P.op(act, lambda e: e.activation(out=rs.t[0:R, 1:2], in_=ss.t[0:R, 1:2], func=AF.Sqrt, scale=1.0 / ROPE, bias=EPS), reads=[ss], writes=[rs])
            P.op(dve, lambda e: e.reciprocal(out=rs.t[0:R, :], in_=rs.t[0:R, :]), reads=[rs], writes=[rs])
            P.op(dve, lambda e: e.scalar_tensor_tensor(out=ckv_out.t[0:R, :], in0=psc.t[0:R, :], scalar=rs.t[0:R, 0:1], in1=kb.gkv.t[0:R, :], op0=ALU.mult, op1=ALU.mult),
                 reads=[psc, rs, kb.gkv], writes=[ckv_out])
            P.op(dve, lambda e: e.scalar_tensor_tensor(out=kn.t[0:R, :], in0=psr.t[0:R, 0:ROPE], scalar=rs.t[0:R, 1:2], in1=kb.gkr.t[0:R, :], op0=ALU.mult, op1=ALU.mult),
                 reads=[psr, rs, kb.gkr], writes=[kn])
            x1, x2 = kn.t[0:R, 0:32], kn.t[0:R, 32:64]
            co, si = cs.t[0:R, 0:32], cs.t[0:R, 32:64]
            P.op(dve, [lambda e: e.tensor_tensor(out=t4.t[0:R, 0, :], in0=x1, in1=co, op=ALU.mult),
                       lambda e: e.tensor_tensor(out=t4.t[0:R, 1, :], in0=x2, in1=si, op=ALU.mult),
                       lambda e: e.tensor_tensor(out=t4.t[0:R, 2, :], in0=x2, in1=co, op=ALU.mult),
                       lambda e: e.tensor_tensor(out=t4.t[0:R, 3, :], in0=x1, in1=si, op=ALU.mult)],
                 reads=[kn, cs], writes=[t4])
            P.op(dve, [lambda e: e.tensor_tensor(out=kr_out.t[0:R, 0:32], in0=t4.t[0:R, 0, :], in1=t4.t[0:R, 1, :], op=ALU.subtract),
                       lambda e: e.tensor_tensor(out=kr_out.t[0:R, 32:64], in0=t4.t[0:R, 2, :], in1=t4.t[0:R, 3, :], op=ALU.add)],
                 reads=[t4], writes=[kr_out])

        class ExpBufs:
            def __init__(self, TG):
                self.wuk = A.tile([4, NH * 128], BF16)
                self.wuv = A.tile([4, NH * 128], BF16)
                P.dma(pool, self.wuk.t, wview(w_uk, 0, NH * 128), writes=[self.wuk])
                P.dma(pool, self.wuv.t, wview(w_uv, 0, NH * 128), writes=[self.wuv])
                self.gk = A.tile([1], F32)
                P.dma(sp, self.gk.t, g_k_nope.rearrange("o d -> d o"), writes=[self.gk])
                self.cb = [A.tile([KVL], BF16, dsem=False) for _ in range(2)]
                self.krb = [A.tile([128], BF16, dsem=False) for _ in range(2)]
                self.cT = [A.tile([4, TG], BF16, dsem=False) for _ in range(2)]
                self.krT = [A.tile([TG], BF16) for _ in range(2)]
                self.sq = [A.tile([TG], BF16, dsem=False) for _ in range(2)]
                self.rs = [A.tile([TG], F32, dsem=False) for _ in range(2)]
                self.KTs = [A.tile([NH, TG], BF16) for _ in range(1)]
                self.Vs = [A.tile([NH, TG // 128 if TG >= 128 else 1, 128], BF16) for _ in range(1)]
                self.i = 0

        def expand_group(xb_, tiles, T, KT_dst, KrT_dst, V_dst, hooks=None):
            i = xb_.i
            xb_.i += 1
            cT, krT, KTs, Vs = xb_.cT[i % 2], xb_.krT[i % 2], xb_.KTs[0], xb_.Vs[0]
            col = 0
            cols = []
            for ti, (ckv, kr, R) in enumerate(tiles):
                cb, krb = xb_.cb[ti % 2], xb_.krb[ti % 2]
                P.op(act, lambda e, cb=cb, ckv=ckv, R=R: e.copy(out=cb.t[0:R, :], in_=ckv.t[0:R, :]), reads=[ckv], writes=[cb])
                P.op(dve, [lambda e, krb=krb, kr=kr, R=R: e.tensor_copy(out=krb.t[0:R, 0:64], in_=kr.t[0:R, :]),
                           lambda e, krb=krb, kr=kr, R=R: e.tensor_copy(out=krb.t[0:R, 64:128], in_=kr.t[0:R, :])], reads=[kr], writes=[krb])
                for k0, kk, ps, pv in transpose_to(None, cb, R, 4, [], None):
                    P.op(dve, lambda e, pv=pv, col=col, R=R: e.tensor_copy(out=cT.t[:, 0:4, col:col + R], in_=pv[:, 0:512].rearrange("p (k n) -> p k n", k=4, n=128)[:, :, 0:R]),
                         reads=[ps], writes=[cT])
                for k0, kk, ps, pv in transpose_to(None, krb, R, 1, [], None):
                    P.op(dve, lambda e, pv=pv, col=col, R=R: e.tensor_copy(out=krT.t[:, col:col + R], in_=pv[:, 0:R]), reads=[ps], writes=[krT])
                cols.append((col, R))
                col += R
            assert col == T
            P.dma(pool, KrT_dst, krT.t[:, 0:T], reads=[krT])
            def k_mm(h):
                psk = psrot.get()
                P.op(pe, [lambda e, kc=kc: e.matmul(psk.t[:, 0:T], xb_.wuk.t[:, kc, h * 128:(h + 1) * 128], cT.t[:, kc, 0:T], start=(kc == 0), stop=(kc == 3)) for kc in range(4)],
                     reads=[xb_.wuk, cT], writes=[psk])
                return psk

            def k_rest(h, psk):
                sq, rs = xb_.sq[h % 2], xb_.rs[h % 2]
                P.op(act, lambda e: e.activation(out=sq.t[:, 0:T], in_=psk.t[:, 0:T], func=AF.Square), reads=[psk], writes=[sq])
                pss = psrot.get()
                P.op(pe, lambda e: e.matmul(pss.t[:, 0:T], ones_b.t, sq.t[:, 0:T], start=True, stop=True), reads=[ones_b, sq], writes=[pss])
                rstd_tile(pss, rs, T, 128)
                P.op(dve, lambda e: e.scalar_tensor_tensor(out=KTs.t[:, h, 0:T], in0=psk.t[:, 0:T], scalar=xb_.gk.t[:, 0:1], in1=rs.t[:, 0:T], op0=ALU.mult, op1=ALU.mult),
                     reads=[psk, rs, xb_.gk], writes=[KTs])

            LK = 2
            psks = {}
            for i in range(NH + LK):
                if i < NH:
                    psks[i] = k_mm(i)
                if i - LK >= 0:
                    k_rest(i - LK, psks.pop(i - LK))
                    if hooks and (i - LK) in hooks:
                        hooks[i - LK]()
            P.dma(pool, KT_dst, KTs.t[:, :, 0:T], reads=[KTs])
            Rmax = max(R for _, R in cols)
            for ti, (col, R) in enumerate(cols):
                for c4 in range(4):
                    psv = psrot.get()
                    P.op(pe, [lambda e, kc=kc, psv=psv, col=col, R=R, c4=c4: e.matmul(psv.t[0:R, :], cT.t[:, kc, col:col + R], xb_.wuv.t[:, kc, c4 * 512:(c4 + 1) * 512], start=(kc == 0), stop=(kc == 3)) for kc in range(4)],
                         reads=[xb_.wuv, cT], writes=[psv])
                    P.op(act, lambda e, psv=psv, ti=ti, R=R, c4=c4: e.copy(out=Vs.t[0:R, c4 * 4:(c4 + 1) * 4, ti, :], in_=psv.t[0:R, :].rearrange("p (h c) -> p h c", h=4, c=128)),
                         reads=[psv], writes=[Vs])
            P.dma(pool, V_dst, Vs.t[0:Rmax, :, 0:len(cols), :], reads=[Vs])

        fb = FrontBufs()
        G1, S1 = make_front_rows(g_norm1, 0, D, False, fb.tmp)
        kb = KvBufs()
        Wkv = A.tile([16, KVL + ROPE], BF16)
        P.dma(pool, Wkv.t, wview(w_in, OFF_KV, OFF_GATE), writes=[Wkv])
        eb = ExpBufs(512)
        xts = [A.tile([D], F32) for _ in range(2)]
        hTs = [A.tile([16, 512], BF16) for _ in range(1)]
        ckvs = [A.tile([KVL], F32) for _ in range(4)]
        krs = [A.tile([ROPE], F32) for _ in range(4)]
        xi = [0]
        hT = hTs[0]

        def a_front(g, t):
            xt = xts[xi[0] % 2]
            xi[0] += 1
            r0 = g * 512 + t * 128
            P.dma(sp, xt.t, xb[r0:r0 + 128, :], writes=[xt])
            front_tile(fb, xt, 128, G1, S1, hT, t * 128)

        def a_zkv(g):
            tl = []
            for t in range(4):
                ck, kr_ = ckvs[t], krs[t]
                r0 = g * 512 + t * 128
                zkv_tile(kb, Wkv, hT, t * 128, 128, ropetm[r0:r0 + 128, :], ck, kr_)
                tl.append((ck, kr_, 128))
            return tl

        for t in range(4):
            a_front(0, t)
        tl = a_zkv(0)
        for g in range(NB):
            hooks = None
            if g + 1 < NB:
                hooks = {1 + 4 * t: (lambda t=t, g=g: a_front(g + 1, t)) for t in range(4)}
            expand_group(eb, tl, 512,
                         KT_d.rearrange("h d t -> d h t")[:, :, g * 512:(g + 1) * 512],
                         KrT_d[:, g * 512:(g + 1) * 512],
                         V_d.rearrange("h p t c -> p h t c")[:, :, g * 4:(g + 1) * 4, :], hooks=hooks)
            if g + 1 < NB:
                tl = a_zkv(g + 1)
        P.barrier()
        A.reset(PERSIST)

        gn1 = A.tile([D], F32)
        G1, S1 = make_front_rows(g_norm1, 0, D, False, gn1)
        fb = FrontBufs()
        xts = [A.tile([D], F32) for _ in range(2)]
        hTs = [A.tile([16, 512], BF16) for _ in range(2)]
        xi = 0
        for (bi, row0, T, nt, R, is_s) in blocks():
            if is_s:
                reload_front_rows(G1, S1, g_norm1, 0, D, gn1)
            hT = hTs[bi % 2]
            for t in range(nt):
                xt = xts[xi % 2]
                xi += 1
                P.dma(sp, xt.t[0:R, :], xo[row0 + t * 128:row0 + t * 128 + R, :], writes=[xt])
                front_tile(fb, xt, R, G1, S1, hT, t * 128)
            P.dma(pool, hT_d[:, :, row0:row0 + T], hT.t[:, :, 0:T], reads=[hT])
        P.barrier()
        A.reset(PERSIST)

        def fm_stage(c_base, func, dst_d):
            W = A.tile([16, D], BF16)
            P.dma(pool, W.t[:, :, 0:1024], wview(w_in, c_base, c_base + 1024), writes=[W])
            P.dma(pool, W.t[:, :, 1024:2048], wview(w_in, c_base + 1024, c_base + 2048), writes=[W])
            hTs = [A.tile([16, 512], BF16) for _ in range(2)]
            outs = [A.tile([16, 512], BF16) for _ in range(2)]
            for (bi, row0, T, nt, R, is_s) in blocks():
                hT, o = hTs[bi % 2], outs[bi % 2]
                P.dma(sp, hT.t[:, :, 0:T], hT_d[:, :, row0:row0 + T], writes=[hT])
                for j in range(16):
                    ps = psrot.get()
                    P.op(pe, [lambda e, k=k, j=j, ps=ps, hT=hT: e.matmul(ps.t[:, 0:T], W.t[:, k, j * 128:(j + 1) * 128], hT.t[:, k, 0:T], start=(k == 0), stop=(k == 15)) for k in range(16)],
                         reads=[W, hT], writes=[ps])
                    P.op(act, lambda e, ps=ps, o=o, j=j: e.activation(out=o.t[:, j, 0:T], in_=ps.t[:, 0:T], func=func), reads=[ps], writes=[o])
                P.dma(pool, dst_d[:, :, row0:row0 + T], o.t[:, :, 0:T], reads=[o])
            P.barrier()
            A.reset(PERSIST)

        fm_stage(0, AF.Gelu, uT_d)
        fm_stage(OFF_GATE, AF.Sigmoid, gaT_d)
        fm_stage(OFF_GATE + D, AF.Sigmoid, gbT_d)

        W = A.tile([16, D], BF16)
        P.dma(pool, W.t[:, :, 0:1024], wview(w_in, D, D + 1024), writes=[W])
        P.dma(pool, W.t[:, :, 1024:2048], wview(w_in, D + 1024, 2 * D), writes=[W])
        gsg = A.tile([D], F32)
        P.dma(sp, gsg.t, g_sg[0:1, :].partition_broadcast(128), writes=[gsg])
        hTs = [A.tile([16, 512], BF16) for _ in range(2)]
        vfs = [A.tile([D], F32) for _ in range(2)]
        vbs = [A.tile([D], BF16) for _ in range(2)]
        vjunk = A.tile([D], BF16, dsem=False)
        vss = [A.tile([1], F32, dsem=False) for _ in range(2)]
        vrs = [A.tile([1], F32, dsem=False) for _ in range(2)]
        vi = 0
        for (bi, row0, T, nt, R, is_s) in blocks():
            hT = hTs[bi % 2]
            P.dma(sp, hT.t[:, :, 0:T], hT_d[:, :, row0:row0 + T], writes=[hT])
            for t in range(nt):
                vf, vb, ss, rs = vfs[vi % 2], vbs[vi % 2], vss[vi % 2], vrs[vi % 2]
                vi += 1
                for c4 in range(4):
                    ps = psrot.get()
                    P.op(pe, [lambda e, k=k, ps=ps, hT=hT, t=t, c4=c4: e.matmul(ps.t[0:R, :], hT.t[:, k, t * 128:t * 128 + R], W.t[:, k, c4 * 512:(c4 + 1) * 512], start=(k == 0), stop=(k == 15)) for k in range(16)],
                         reads=[W, hT], writes=[ps])
                    P.op(act, lambda e, ps=ps, vf=vf, c4=c4: e.activation(out=vf.t[0:R, c4 * 512:(c4 + 1) * 512], in_=ps.t[0:R, :], func=AF.Gelu), reads=[ps], writes=[vf])
                P.op(dve, lambda e, ss=ss: e.memset(ss.t[0:R, :], 0.0), writes=[ss])
                P.op(act, lambda e, vf=vf, ss=ss: e.activation(out=vjunk.t[0:R, :], in_=vf.t[0:R, :], func=AF.Square, accum_out=ss.t[0:R, :]), reads=[vf, ss], writes=[vjunk, ss])
                rstd_col(ss, rs, R, D)
                if is_s:
                    P.op(dve, lambda e, vf=vf, rs=rs: e.scalar_tensor_tensor(out=vf.t[0:R, :], in0=vf.t[0:R, :], scalar=rs.t[0:R, :], in1=gsg.t[0:R, :], op0=ALU.mult, op1=ALU.mult),
                         reads=[vf, rs, gsg], writes=[vf])
                    P.dma(pool, v_all[:, :], vf.t[0:R, :], reads=[vf])
                    P.op(dve, lambda e, vf=vf, vb=vb: e.tensor_copy(out=vb.t[0:R, :], in_=vf.t[0:R, :]), reads=[vf], writes=[vb])
                else:
                    P.op(dve, lambda e, vf=vf, rs=rs, vb=vb: e.scalar_tensor_tensor(out=vb.t[0:R, :], in0=vf.t[0:R, :], scalar=rs.t[0:R, :], in1=gsg.t[0:R, :], op0=ALU.mult, op1=ALU.mult),
                         reads=[vf, rs, gsg], writes=[vb])
                P.dma(pool, v_d[row0 + t * 128:row0 + t * 128 + R, :], vb.t[0:R, :], reads=[vb])
        P.barrier()
        A.reset(PERSIST)

        Wq = A.tile([16, QL], BF16)
        P.dma(pool, Wq.t, wview(w_in, OFF_Q, OFF_KV), writes=[Wq])
        Wkv = A.tile([16, KVL + ROPE], BF16)
        P.dma(pool, Wkv.t, wview(w_in, OFF_KV, OFF_GATE), writes=[Wkv])
        Wuq = A.tile([4, NH * QKH], BF16)
        P.dma(pool, Wuq.t, wview(w_uq, 0, NH * QKH), writes=[Wuq])
        Wr2 = A.tile([4, NH, 128], BF16, dsem=False)
        wq3 = Wuq.t.rearrange("p k (h c) -> p k h c", h=NH, c=QKH)
        for kc in range(4):
            P.op(dve, [lambda e, kc=kc: e.tensor_copy(out=Wr2.t[:, kc, :, 0:64], in_=wq3[:, kc, :, 128:192]),
                       lambda e, kc=kc: e.tensor_copy(out=Wr2.t[:, kc, :, 64:96], in_=wq3[:, kc, :, 160:192]),
                       lambda e, kc=kc: e.tensor_copy(out=Wr2.t[:, kc, :, 96:128], in_=wq3[:, kc, :, 128:160])],
                 reads=[Wuq], writes=[Wr2])
        gqa = A.tile([4], F32)
        P.dma(sp, gqa.t, g_q_a.rearrange("o (k p) -> p (o k)", p=128), writes=[gqa])
        gqn = A.tile([1], F32)
        P.dma(sp, gqn.t, g_q_nope.rearrange("o d -> d o"), writes=[gqn])
        gq2 = A.tile([1], F32)
        gqr_col = g_q_rope.rearrange("o d -> d o")
        P.dma(sp, gq2.t[0:64, :], gqr_col[0:64, :], writes=[gq2])
        P.dma(sp, gq2.t[64:96, :], gqr_col[32:64, :], writes=[gq2])
        P.dma(sp, gq2.t[96:128, :], gqr_col[0:32, :], writes=[gq2])
        P.op(dve, lambda e: e.tensor_scalar(out=gqn.t, in0=gqn.t, scalar1=SCALE, scalar2=0.0, op0=ALU.mult, op1=ALU.add), reads=[gqn], writes=[gqn])
        P.op(dve, lambda e: e.tensor_scalar(out=gq2.t, in0=gq2.t, scalar1=SCALE, scalar2=0.0, op0=ALU.mult, op1=ALU.add), reads=[gq2], writes=[gq2])
        kb = KvBufs()
        hTs = [A.tile([16, 512], BF16) for _ in range(2)]
        sqs = [A.tile([512], BF16, dsem=False) for _ in range(2)]
        rss = [A.tile([512], F32, dsem=False) for _ in range(2)]
        tmps = [A.tile([512], F32, dsem=False) for _ in range(2)]
        sqs4 = [A.tile([512], BF16, dsem=False) for _ in range(4)]
        rss4 = [A.tile([512], F32, dsem=False) for _ in range(4)]
        qaT = A.tile([4, 512], BF16, dsem=False)
        QTs = [A.tile([NH, 512], BF16) for _ in range(1)]
        QrTs = [A.tile([NH, 512], BF16) for _ in range(1)]
        tws = [A.tile([512], F32) for _ in range(2)]
        cko = [A.tile([KVL], F32) for _ in range(2)]
        kro = [A.tile([ROPE], F32) for _ in range(2)]
        oi = 0
        qrot = Rot(PS[0:4])
        orot = Rot(PS[4:8])
        for (bi, row0, T, nt, R, is_s) in blocks():
            hT, QT, QrT, tw = hTs[bi % 2], QTs[0], QrTs[0], tws[bi % 2]
            P.dma(sp, hT.t[:, :, 0:T], hT_d[:, :, row0:row0 + T], writes=[hT])
            P.dma(sp, tw.t[:, 0:T], town[:, row0:row0 + T], writes=[tw])
            for t in range(nt):
                ck, kr_ = cko[oi % 2], kro[oi % 2]
                oi += 1
                r0 = row0 + t * 128
                zkv_tile(kb, Wkv, hT, t * 128, R, ropeo[r0:r0 + R, :], ck, kr_)
                P.dma(pool, lat_all[r0:r0 + R, :], ck.t[0:R, :], reads=[ck])
                P.dma(pool, kr_all[r0:r0 + R, :], kr_.t[0:R, :], reads=[kr_])
            pq = [qrot.get() for _ in range(4)]
            pss = orot.get()
            for kc in range(4):
                P.op(pe, [lambda e, k=k, kc=kc: e.matmul(pq[kc].t[:, 0:T], Wq.t[:, k, kc * 128:(kc + 1) * 128], hT.t[:, k, 0:T], start=(k == 0), stop=(k == 15)) for k in range(16)],
                     reads=[Wq, hT], writes=[pq[kc]])
            for kc in range(4):
                sq = sqs[kc % 2]
                P.op(act, lambda e, kc=kc, sq=sq: e.activation(out=sq.t[:, 0:T], in_=pq[kc].t[:, 0:T], func=AF.Square), reads=[pq[kc]], writes=[sq])
                if kc == 0:
                    P.deps(pe, [], [pss])
                P.op(pe, lambda e, kc=kc, sq=sq: e.matmul(pss.t[:, 0:T], ones_b.t, sq.t[:, 0:T], start=(kc == 0), stop=(kc == 3)), reads=[ones_b, sq], writes=[] if kc < 3 else [pss])
            rs = rss[0]
            rstd_tile(pss, rs, T, QL)
            for kc in range(4):
                P.op(dve, lambda e, kc=kc: e.scalar_tensor_tensor(out=qaT.t[:, kc, 0:T], in0=pq[kc].t[:, 0:T], scalar=gqa.t[:, kc:kc + 1], in1=rs.t[:, 0:T], op0=ALU.mult, op1=ALU.mult),
                     reads=[pq[kc], gqa, rs], writes=[qaT])
            def q_mm(h):
                pa = qrot.get()
                pab = qrot.get()
                P.op(pe, [lambda e, kc=kc: e.matmul(pa.t[:, 0:T], Wuq.t[:, kc, h * QKH:h * QKH + 128], qaT.t[:, kc, 0:T], start=(kc == 0), stop=(kc == 3)) for kc in range(4)],
                     reads=[Wuq, qaT], writes=[pa])
                P.op(pe, [lambda e, kc=kc: e.matmul(pab.t[:, 0:T], Wr2.t[:, kc, h, :], qaT.t[:, kc, 0:T], start=(kc == 0), stop=(kc == 3)) for kc in range(4)],
                     reads=[Wr2, qaT], writes=[pab])
                return pa, pab

            def q_rest(h, pa, pab):
                sq, sq2, rs1, rs2, tmp = sqs4[(h % 2) * 2], sqs4[(h % 2) * 2 + 1], rss4[(h % 2) * 2], rss4[(h % 2) * 2 + 1], tmps[h % 2]
                P.op(act, lambda e: e.activation(out=sq.t[:, 0:T], in_=pa.t[:, 0:T], func=AF.Square), reads=[pa], writes=[sq])
                P.op(act, lambda e: e.activation(out=sq2.t[0:64, 0:T], in_=pab.t[0:64, 0:T], func=AF.Square), reads=[pab], writes=[sq2])
                ps1 = orot.get()
                ps2 = orot.get()
                P.op(pe, lambda e: e.matmul(ps1.t[:, 0:T], ones_b.t, sq.t[:, 0:T], start=True, stop=True), reads=[ones_b, sq], writes=[ps1])
                P.op(pe, lambda e: e.matmul(ps2.t[:, 0:T], ones_b.t[0:64, :], sq2.t[0:64, 0:T], start=True, stop=True), reads=[ones_b, sq2], writes=[ps2])
                rstd_tile(ps1, rs1, T, 128)
                rstd_tile(ps2, rs2, T, ROPE)
                P.op(dve, lambda e: e.scalar_tensor_tensor(out=QT.t[:, h, 0:T], in0=pa.t[:, 0:T], scalar=gqn.t[:, 0:1], in1=rs1.t[:, 0:T], op0=ALU.mult, op1=ALU.mult),
                     reads=[pa, gqn, rs1], writes=[QT])
                P.op(dve, lambda e: e.scalar_tensor_tensor(out=tmp.t[:, 0:T], in0=pab.t[:, 0:T], scalar=gq2.t[:, 0:1], in1=rs2.t[:, 0:T], op0=ALU.mult, op1=ALU.mult),
                     reads=[pab, gq2, rs2], writes=[tmp])
                P.op(dve, lambda e: e.tensor_tensor(out=QrT.t[:, h, 0:T], in0=tmp.t[:, 0:T], in1=tw.t[:, 0:T], op=ALU.mult),
                     reads=[tmp, tw], writes=[QrT])

            pend = {}
            for i in range(NH + 1):
                if i < NH:
                    pend[i] = q_mm(i)
                if i - 1 >= 0:
                    q_rest(i - 1, *pend.pop(i - 1))
            P.dma(pool, QT_d.rearrange("h d t -> d h t")[:, :, row0:row0 + T], QT.t[:, :, 0:T], reads=[QT])
            P.dma(pool, QrT_d.rearrange("h d t -> d h t")[:, :, row0:row0 + T], QrT.t[:, :, 0:T], reads=[QrT])
        P.barrier()
        A.reset(PERSIST)

        eb = ExpBufs(512)
        cks = [A.tile([KVL], F32) for _ in range(8)]
        krs_ = [A.tile([ROPE], F32) for _ in range(8)]
        gi = 0
        for i in range(SB):
            for t0 in range(0, NKT, 4):
                ntl = min(4, NKT - t0)
                tl = []
                for t in range(ntl):
                    ck, kr_ = cks[(gi % 2) * 4 + t], krs_[(gi % 2) * 4 + t]
                    r0 = (t0 + t) * 128
                    P.dma(sp, ck.t, clat[i, r0:r0 + 128, :], writes=[ck])
                    P.dma(sp, kr_.t, ckr[i, r0:r0 + 128, :], writes=[kr_])
                    tl.append((ck, kr_, 128))
                gi += 1
                T = ntl * 128
                expand_group(eb, tl, T,
                             KTs_d[i].rearrange("h d t -> d h t")[:, :, t0 * 128:t0 * 128 + T],
                             KrTs_d[i][:, t0 * 128:t0 * 128 + T],
                             Vs_d[i].rearrange("h p t c -> p h t c")[:, :, t0:t0 + ntl, :])
            ck, kr_ = cks[(gi % 2) * 4], krs_[(gi % 2) * 4]
            gi += 1
            r0 = NOWN + i * TS
            P.dma(sp, ck.t[0:TS, :], lat_all[r0:r0 + TS, :], writes=[ck])
            P.dma(sp, kr_.t[0:TS, :], kr_all[r0:r0 + TS, :], writes=[kr_])
            expand_group(eb, [(ck, kr_, TS)], TS,
                         KTs_d[i].rearrange("h d t -> d h t")[:, :, PAST:PASTP],
                         KrTs_d[i][:, PAST:PASTP],
                         Vs_d[i].rearrange("h p t c -> p h t c")[0:TS, :, NKT:NKT + 1, :])
        P.barrier()
        A.reset(PERSIST)

        wsT = A.tile([NH, 128], BF16, dsem=False)
        wsblk = A.tile([NH, NS], BF16)
        bsb = A.tile([NH, 128], F32)
        bss = A.tile([NH, NS], F32, dsem=False)
        trl = A.tile([128], F32)
        wsf = [A.tile([128], F32) for _ in range(2)]
        wsm = [A.tile([128], BF16, dsem=False) for _ in range(2)]
        P.dma(sp, trl.t, trild[:, :], writes=[trl])
        P.dma(sp, bsb.t.rearrange("p g i -> p (g i)"), b_s[0:1, :].partition_broadcast(128), writes=[bsb])
        for g in range(NH):
            wf, wm = wsf[g % 2], wsm[g % 2]
            P.dma(sp, wf.t, w_s[g, :, :], writes=[wf])
            P.op(dve, lambda e, wf=wf, wm=wm: e.tensor_tensor(out=wm.t, in0=wf.t, in1=trl.t, op=ALU.mult), reads=[wf, trl], writes=[wm])
            for k0, kk, ps, pv in transpose_to(None, wm, 128, 1, [], None):
                P.op(dve, lambda e, pv=pv, g=g: e.tensor_copy(out=wsT.t[:, g, :], in_=pv[:, 0:128]), reads=[ps], writes=[wsT])
        P.op(dve, lambda e: e.memset(wsblk.t, 0.0), writes=[wsblk])
        for i in range(SB):
            P.dma(sp, wsblk.t[16 * i:16 * i + 16, :, 16 * i:16 * i + 16], wsT.t[0:16, :, 0:16], reads=[wsT], writes=[wsblk])
            P.op(dve, lambda e, i=i: e.tensor_copy(out=bss.t[:, :, 16 * i:16 * i + 16], in_=bsb.t[:, :, 0:16]), reads=[bsb], writes=[bss])
        uTs = [A.tile([16, 512], BF16) for _ in range(2)]
        vts = [A.tile([4, D], BF16) for _ in range(2)]
        osg = [A.tile([16, 512], BF16) for _ in range(2)]
        tmpf = [A.tile([512], F32, dsem=False) for _ in range(2)]
        for (bi, row0, T, nt, R, is_s) in blocks():
            uT, vt, o = uTs[bi % 2], vts[bi % 2], osg[bi % 2]
            P.dma(sp, uT.t[:, :, 0:T], uT_d[:, :, row0:row0 + T], writes=[uT])
            for t in range(nt):
                P.dma(sp, vt.t[0:R, t, :], v_d[row0 + t * 128:row0 + t * 128 + R, :], writes=[vt])
            for g in range(NH):
                ps = psrot.get()
                tm = tmpf[g % 2]
                if not is_s:
                    P.op(pe, [lambda e, t=t, g=g, ps=ps, vt=vt: e.matmul(ps.t[:, t * 128:(t + 1) * 128], vt.t[:, t, g * 128:(g + 1) * 128], wsT.t[:, g, :], start=True, stop=True) for t in range(4)],
                         reads=[vt, wsT], writes=[ps])
                    P.op(dve, lambda e, ps=ps, tm=tm, g=g: e.tensor_tensor(out=tm.t.rearrange("p (t i) -> p t i", t=4, i=128), in0=ps.t.rearrange("p (t i) -> p t i", t=4, i=128),
                                                                     in1=bsb.t[:, g, :].unsqueeze(1).to_broadcast([128, 4, 128]), op=ALU.add), reads=[ps, bsb], writes=[tm])
                else:
                    P.op(pe, lambda e, g=g, ps=ps, vt=vt: e.matmul(ps.t[:, 0:NS], vt.t[0:NS, 0, g * 128:(g + 1) * 128], wsblk.t[0:NS, g, :], start=True, stop=True),
                         reads=[vt, wsblk], writes=[ps])
                    P.op(dve, lambda e, ps=ps, tm=tm, g=g: e.tensor_tensor(out=tm.t[:, 0:NS], in0=ps.t[:, 0:NS], in1=bss.t[:, g, :], op=ALU.add), reads=[ps, bss], writes=[tm])
                P.op(dve, lambda e, tm=tm, g=g, uT=uT, o=o: e.tensor_tensor(out=o.t[:, g, 0:T], in0=tm.t[:, 0:T], in1=uT.t[:, g, 0:T], op=ALU.mult), reads=[tm, uT], writes=[o])
            P.dma(pool, osgT_d[:, :, row0:row0 + T], o.t[:, :, 0:T], reads=[o])
        P.barrier()
        A.reset(PERSIST)

        msk = A.tile([16, 8], F32)
        mskb = A.tile([16, 8], BF16, dsem=False)
        P.dma(sp, msk.t.rearrange("p a b -> p (a b)"), maskc[:, :], writes=[msk])
        P.op(dve, lambda e: e.tensor_copy(out=mskb.t, in_=msk.t), reads=[msk], writes=[mskb])
        CH = 8
        Kc = [A.tile([CH * 128], BF16) for _ in range(4)]
        Krc = [A.tile([CH * 128], BF16) for _ in range(4)]
        Vc = [A.tile([CH, 128], BF16) for _ in range(4)]
        Qs = [A.tile([512], BF16) for _ in range(2)]
        Qrs = [A.tile([512], BF16) for _ in range(2)]
        pTs = [A.tile([512], BF16, dsem=False) for _ in range(4)]
        ptmp = [A.tile([512], BF16, dsem=False) for _ in range(2)]
        acc = [A.tile([512], F32, dsem=False) for _ in range(4)]
        rinv = [A.tile([512], F32, dsem=False) for _ in range(2)]
        omla = [A.tile([NH, 512], BF16) for _ in range(2)]
        srot = Rot(PS[0:5])
        orot2 = Rot(PS[5:7])
        rrot = Rot(PS[7:8])
        ci = 0
        hi = 0
        pi = 0

        LOOK = 2
        jobs = []
        for (bi, row0, T, nt, R, is_s) in blocks():
            if not is_s:
                ctx = 4 * (4 * bi + 4)
                for h in range(NH):
                    jobs.append(dict(bi=bi, row0=row0, T=T, QT=QT_d[h], QrT=QrT_d[h], q0=row0, NQ=512, ctx=ctx, last=128,
                                     KT=KT_d[h], KrT=KrT_d, V=V_d[h], h=h, ocol=0, tail=ctx - 16, store=(h == NH - 1)))
            else:
                for i in range(SB):
                    for h in range(NH):
                        jobs.append(dict(bi=bi, row0=row0, T=T, QT=QT_d[h], QrT=QrT_d[h], q0=row0 + i * TS, NQ=TS, ctx=NKT + 1, last=TS,
                                         KT=KTs_d[i, h], KrT=KrTs_d[i], V=Vs_d[i, h], h=h, ocol=i * TS, tail=None,
                                         store=(i == SB - 1 and h == NH - 1)))
        chunks = []
        tiles = []
        for ji, jb in enumerate(jobs):
            for c0 in range(0, jb["ctx"], CH):
                nt_ = min(CH, jb["ctx"] - c0)
                chunks.append(dict(job=ji, c0=c0, nt=nt_, first=(c0 == 0)))
                for t in range(nt_):
                    kt = c0 + t
                    tiles.append(dict(job=ji, chunk=len(chunks) - 1, t=t, kt=kt, RK=(jb["last"] if kt == jb["ctx"] - 1 else 128),
                                      firstc=(t == 0), first=(kt == 0), last=(kt == jb["ctx"] - 1)))

        def prefetch(cj):
            if cj >= len(chunks):
                return
            ch = chunks[cj]
            jb = jobs[ch["job"]]
            if ch["first"]:
                jb["Q"], jb["Qr"] = Qs[ch["job"] % 2], Qrs[ch["job"] % 2]
                jb["acs"] = (acc[(ch["job"] % 2) * 2], acc[(ch["job"] % 2) * 2 + 1])
                jb["ri"] = rinv[ch["job"] % 2]
                jb["nacc"] = [0, 0]
                jb["om"] = omla[jb["bi"] % 2]
                P.dma(sp, jb["Q"].t[:, 0:jb["NQ"]], jb["QT"][:, jb["q0"]:jb["q0"] + jb["NQ"]], writes=[jb["Q"]])
                P.dma(sp, jb["Qr"].t[:, 0:jb["NQ"]], jb["QrT"][:, jb["q0"]:jb["q0"] + jb["NQ"]], writes=[jb["Qr"]])
            kc_, krc_, vc_ = Kc[cj % 4], Krc[cj % 4], Vc[cj % 4]
            ch["bufs"] = (kc_, krc_, vc_)
            c0, nt_ = ch["c0"], ch["nt"]
            lastc = (c0 + nt_ == jb["ctx"])
            nkeys = (nt_ - 1) * 128 + (jb["last"] if lastc else 128)
            P.dma(sp, kc_.t[:, 0:nkeys], jb["KT"][:, c0 * 128:c0 * 128 + nkeys], writes=[kc_])
            P.dma(sp, krc_.t[:, 0:nkeys], jb["KrT"][:, c0 * 128:c0 * 128 + nkeys], writes=[krc_])
            if lastc and jb["last"] < 128:
                if nt_ > 1:
                    P.dma(sp, vc_.t[:, 0:nt_ - 1, :], jb["V"][:, c0:c0 + nt_ - 1, :], writes=[vc_])
                P.dma(sp, vc_.t[0:jb["last"], nt_ - 1:nt_, :], jb["V"][0:jb["last"], c0 + nt_ - 1:c0 + nt_, :], writes=[vc_])
            else:
                P.dma(sp, vc_.t[:, 0:nt_, :], jb["V"][:, c0:c0 + nt_, :], writes=[vc_])

        def emit_S(ti_):
            tl = tiles[ti_]
            jb = jobs[tl["job"]]
            if tl["firstc"]:
                if tl["chunk"] == 0:
                    prefetch(0)
                prefetch(tl["chunk"] + 1)
            kc_, krc_, vc_ = chunks[tl["chunk"]]["bufs"]
            NQ, RK, t = jb["NQ"], tl["RK"], tl["t"]
            pss_ = srot.get()
            tl["pss"] = pss_
            Q, Qr = jb["Q"], jb["Qr"]
            P.op(pe, [lambda e: e.matmul(pss_.t[0:RK, 0:NQ], kc_.t[:, t * 128:t * 128 + RK], Q.t[:, 0:NQ], start=True, stop=False),
                      lambda e: e.matmul(pss_.t[0:RK, 0:NQ], krc_.t[:, t * 128:t * 128 + RK], Qr.t[:, 0:NQ], start=False, stop=True)],
                 reads=[kc_, krc_, Q, Qr], writes=[pss_])

        pcount = [0]

        def emit_rest(ti_):
            tl = tiles[ti_]
            jb = jobs[tl["job"]]
            kc_, krc_, vc_ = chunks[tl["chunk"]]["bufs"]
            NQ, RK, t, kt = jb["NQ"], tl["RK"], tl["t"], tl["kt"]
            pss_ = tl["pss"]
            pT = pTs[pcount[0] % 4]
            pcount[0] += 1
            P.op(act, lambda e: e.activation(out=pT.t[0:RK, 0:NQ], in_=pss_.t[0:RK, 0:NQ], func=AF.Exp), reads=[pss_], writes=[pT])
            if jb["tail"] is not None and kt >= jb["tail"]:
                kbt = kt - jb["tail"]
                P.op(dve, lambda e: e.tensor_tensor(out=pT.t.rearrange("p (a b) -> p a b", a=8, b=64), in0=pT.t.rearrange("p (a b) -> p a b", a=8, b=64),
                                                    in1=mskb.t[:, kbt, :].unsqueeze(2).to_broadcast([128, 8, 64]), op=ALU.mult), reads=[pT, mskb], writes=[pT])
            if tl["first"]:
                jb["pso"] = orot2.get()
                P.deps(pe, [], [jb["pso"]])
            pso = jb["pso"]
            first, last = tl["first"], tl["last"]
            P.op(pe, lambda e: e.matmul(pso.t[:, 0:NQ], vc_.t[0:RK, t, :], pT.t[0:RK, 0:NQ], start=first, stop=last),
                 reads=[vc_, pT], writes=[pso] if last else [])
            if tl["first"]:
                for ac0 in jb["acs"]:
                    P.op(dve, lambda e, ac0=ac0: e.memset(ac0.t[:, 0:NQ], 0.0), writes=[ac0])

            def acc_add(src, pidx):
                ac = jb["acs"][pidx % 2]
                rr = 128 if src is not pT else RK
                P.op(dve, lambda e: e.tensor_tensor(out=ac.t[0:rr, 0:NQ], in0=ac.t[0:rr, 0:NQ], in1=src.t[0:rr, 0:NQ], op=ALU.add), reads=[src, ac], writes=[ac])
                jb["nacc"][pidx % 2] += 1

            if kt % 2 == 0:
                if last or RK < 128:
                    acc_add(pT, kt // 2)
                else:
                    jb["prevpT"] = pT
            else:
                if RK < 128:
                    acc_add(jb["prevpT"], kt // 2)
                    acc_add(pT, kt // 2 + 1)
                else:
                    pp = jb["prevpT"]
                    tb = ptmp[(pcount[0] // 2) % 2]
                    P.op(dve, lambda e: e.tensor_tensor(out=tb.t[:, 0:NQ], in0=pp.t[:, 0:NQ], in1=pT.t[:, 0:NQ], op=ALU.add), reads=[pp, pT], writes=[tb])
                    acc_add(tb, kt // 2)
            if last:
                acs, ri, om, h, ocol = jb["acs"], jb["ri"], jb["om"], jb["h"], jb["ocol"]
                psr_ = rrot.get()
                P.op(pe, [lambda e: e.matmul(psr_.t[:, 0:NQ], ones_f.t, acs[0].t[:, 0:NQ], start=True, stop=False),
                          lambda e: e.matmul(psr_.t[:, 0:NQ], ones_f.t, acs[1].t[:, 0:NQ], start=False, stop=True)], reads=[ones_f, acs[0], acs[1]], writes=[psr_])
                P.op(dve, lambda e: e.reciprocal(out=ri.t[:, 0:NQ], in_=psr_.t[:, 0:NQ]), reads=[psr_], writes=[ri])
                P.op(dve, lambda e: e.tensor_tensor(out=om.t[:, h, ocol:ocol + NQ], in0=pso.t[:, 0:NQ], in1=ri.t[:, 0:NQ], op=ALU.mult), reads=[pso, ri], writes=[om])
                if jb["store"]:
                    P.dma(pool, omlaT_d[:, :, jb["row0"]:jb["row0"] + jb["T"]], om.t[:, :, 0:jb["T"]], reads=[om])

        NTL = len(tiles)
        for i in range(NTL + LOOK):
            if i < NTL:
                emit_S(i)
            if i - LOOK >= 0:
                emit_rest(i - LOOK)
        P.barrier()
        A.reset(PERSIST)

        for grp in range(2):
            Wa = A.tile([16, 1024], BF16)
            Wb = A.tile([16, 1024], BF16)
            P.dma(pool, Wa.t, wview(w_pa, grp * 1024, (grp + 1) * 1024), writes=[Wa])
            P.dma(pool, Wb.t, wview(w_pb, grp * 1024, (grp + 1) * 1024), writes=[Wb])
            ia = [A.tile([16, 512], BF16) for _ in range(2)]
            ib = [A.tile([16, 512], BF16) for _ in range(2)]
            ga = [A.tile([8, 512], BF16) for _ in range(2)]
            gb = [A.tile([8, 512], BF16) for _ in range(2)]
            mo = [A.tile([8, 512], BF16) for _ in range(2)]
            t1 = [A.tile([512], F32, dsem=False) for _ in range(2)]
            t2 = [A.tile([512], F32, dsem=False) for _ in range(2)]
            for (bi, row0, T, nt, R, is_s) in blocks():
                a_, b_, ga_, gb_, mo_ = ia[bi % 2], ib[bi % 2], ga[bi % 2], gb[bi % 2], mo[bi % 2]
                P.dma(sp, a_.t[:, :, 0:T], osgT_d[:, :, row0:row0 + T], writes=[a_])
                P.dma(sp, b_.t[:, :, 0:T], omlaT_d[:, :, row0:row0 + T], writes=[b_])
                P.dma(sp, ga_.t[:, :, 0:T], gaT_d[:, grp * 8:(grp + 1) * 8, row0:row0 + T], writes=[ga_])
                P.dma(sp, gb_.t[:, :, 0:T], gbT_d[:, grp * 8:(grp + 1) * 8, row0:row0 + T], writes=[gb_])
                for j in range(8):
                    pa = psrot.get()
                    pb = psrot.get()
                    P.op(pe, [lambda e, k=k, j=j, pa=pa, a_=a_: e.matmul(pa.t[:, 0:T], Wa.t[:, k, j * 128:(j + 1) * 128], a_.t[:, k, 0:T], start=(k == 0), stop=(k == 15)) for k in range(16)],
                         reads=[Wa, a_], writes=[pa])
                    P.op(pe, [lambda e, k=k, j=j, pb=pb, b_=b_: e.matmul(pb.t[:, 0:T], Wb.t[:, k, j * 128:(j + 1) * 128], b_.t[:, k, 0:T], start=(k == 0), stop=(k == 15)) for k in range(16)],
                         reads=[Wb, b_], writes=[pb])
                    x1_, x2_ = t1[j % 2], t2[j % 2]
                    P.op(dve, lambda e, pa=pa, x1_=x1_, ga_=ga_, j=j: e.tensor_tensor(out=x1_.t[:, 0:T], in0=pa.t[:, 0:T], in1=ga_.t[:, j, 0:T], op=ALU.mult), reads=[pa, ga_], writes=[x1_])
                    P.op(dve, lambda e, pb=pb, x2_=x2_, gb_=gb_, j=j: e.tensor_tensor(out=x2_.t[:, 0:T], in0=pb.t[:, 0:T], in1=gb_.t[:, j, 0:T], op=ALU.mult), reads=[pb, gb_], writes=[x2_])
                    P.op(dve, lambda e, x1_=x1_, x2_=x2_, mo_=mo_, j=j: e.tensor_tensor(out=mo_.t[:, j, 0:T], in0=x1_.t[:, 0:T], in1=x2_.t[:, 0:T], op=ALU.add), reads=[x1_, x2_], writes=[mo_])
                P.dma(pool, mT_d[:, grp * 8:(grp + 1) * 8, row0:row0 + T], mo_.t[:, :, 0:T], reads=[mo_])
            P.barrier()
            A.reset(PERSIST)

        Wo = A.tile([16, D], BF16)
        P.dma(pool, Wo.t[:, :, 0:1024], wview(w_o, 0, 1024), writes=[Wo])
        P.dma(pool, Wo.t[:, :, 1024:2048], wview(w_o, 1024, 2048), writes=[Wo])
        gn2 = A.tile([D], F32)
        G2, S2 = make_front_rows(g_norm2, 3 * D, 4 * D, False, gn2)
        g1r = A.tile([D], F32)
        load_rows_prompt_or_sample(g1r, mod_d[0:1, 2 * D:3 * D], None, False, D)
        fb = FrontBufs()
        mTs = [A.tile([16, 512], BF16) for _ in range(1)]
        xts = [A.tile([D], F32) for _ in range(2)]
        h2s = [A.tile([16, 512], BF16) for _ in range(1)]
        tm5 = [A.tile([512], F32, dsem=False) for _ in range(2)]
        xi = 0
        for (bi, row0, T, nt, R, is_s) in blocks():
            if is_s:
                reload_front_rows(G2, S2, g_norm2, 3 * D, 4 * D, gn2)
                load_rows_prompt_or_sample(g1r, None, [mod_d[1 + i:2 + i, 2 * D:3 * D] for i in range(SB)], True, D)
            mT, h2 = mTs[0], h2s[0]
            P.dma(sp, mT.t[:, :, 0:T], mT_d[:, :, row0:row0 + T], writes=[mT])
            for t in range(nt):
                xt = xts[xi % 2]
                xi += 1
                r0 = row0 + t * 128
                P.dma(sp, xt.t[0:R, :], xo[r0:r0 + R, :], writes=[xt])
                for c4 in range(4):
                    ps = psrot.get()
                    tm = tm5[c4 % 2]
                    P.op(pe, [lambda e, k=k, ps=ps, mT=mT, t=t, c4=c4: e.matmul(ps.t[0:R, :], mT.t[:, k, t * 128:t * 128 + R], Wo.t[:, k, c4 * 512:(c4 + 1) * 512], start=(k == 0), stop=(k == 15)) for k in range(16)],
                         reads=[Wo, mT], writes=[ps])
                    P.op(dve, lambda e, ps=ps, tm=tm, c4=c4: e.tensor_tensor(out=tm.t[0:R, :], in0=ps.t[0:R, :], in1=g1r.t[0:R, c4 * 512:(c4 + 1) * 512], op=ALU.mult), reads=[ps, g1r], writes=[tm])
                    P.op(dve, lambda e, tm=tm, xt=xt, c4=c4: e.tensor_tensor(out=xt.t[0:R, c4 * 512:(c4 + 1) * 512], in0=xt.t[0:R, c4 * 512:(c4 + 1) * 512], in1=tm.t[0:R, :], op=ALU.add), reads=[tm, xt], writes=[xt])
                P.dma(pool, x1_d[r0:r0 + R, :], xt.t[0:R, :], reads=[xt])
                front_tile(fb, xt, R, G2, S2, h2, t * 128)
            P.dma(pool, h2T_d[:, :, row0:row0 + T], h2.t[:, :, 0:T], reads=[h2])
        P.barrier()
        A.reset(PERSIST)

        for grp in range(4):
            Wu = A.tile([16, D], BF16)
            P.dma(pool, Wu.t[:, :, 0:1024], wview(w_up, grp * D, grp * D + 1024), writes=[Wu])
            P.dma(pool, Wu.t[:, :, 1024:2048], wview(w_up, grp * D + 1024, (grp + 1) * D), writes=[Wu])
            h2s = [A.tile([16, 512], BF16) for _ in range(2)]
            hid = [A.tile([16, 512], BF16) for _ in range(2)]
            sqf = [A.tile([512], F32, dsem=False) for _ in range(2)]
            for (bi, row0, T, nt, R, is_s) in blocks():
                h2, hd = h2s[bi % 2], hid[bi % 2]
                P.dma(sp, h2.t[:, :, 0:T], h2T_d[:, :, row0:row0 + T], writes=[h2])
                for j in range(16):
                    ps = psrot.get()
                    sq = sqf[j % 2]
                    P.op(pe, [lambda e, k=k, j=j, ps=ps, h2=h2: e.matmul(ps.t[:, 0:T], Wu.t[:, k, j * 128:(j + 1) * 128], h2.t[:, k, 0:T], start=(k == 0), stop=(k == 15)) for k in range(16)],
                         reads=[Wu, h2], writes=[ps])
                    P.op(act, lambda e, ps=ps, sq=sq: e.activation(out=sq.t[:, 0:T], in_=ps.t[:, 0:T], func=AF.Square), reads=[ps], writes=[sq])
                    P.op(dve, lambda e, ps=ps, sq=sq, hd=hd, j=j: e.scalar_tensor_tensor(out=hd.t[:, j, 0:T], in0=ps.t[:, 0:T], scalar=0.0, in1=sq.t[:, 0:T], op0=ALU.is_gt, op1=ALU.mult),
                         reads=[ps, sq], writes=[hd])
                P.dma(pool, hidT_d[:, grp * 16:(grp + 1) * 16, row0:row0 + T], hd.t[:, :, 0:T], reads=[hd])
            P.barrier()
            A.reset(PERSIST)

        g2r = A.tile([D], F32)
        load_rows_prompt_or_sample(g2r, mod_d[0:1, 5 * D:6 * D], None, False, D)
        g2s = A.tile([D], F32)
        load_rows_prompt_or_sample(g2s, None, [mod_d[1 + i:2 + i, 5 * D:6 * D] for i in range(SB)], True, D)
        WBASE = A.off
        for grp in range(4):
            A.reset(WBASE)
            Wd = A.tile([64, 512], BF16)
            for q in range(4):
                P.dma(pool, Wd.t[:, q * 16:(q + 1) * 16, :], wview(w_down, grp * 512, (grp + 1) * 512)[:, q * 16:(q + 1) * 16, :], writes=[Wd])
            hts = [A.tile([64, 256], BF16) for _ in range(2)]
            x1s = [A.tile([512], F32) for _ in range(2)]
            tm7 = [A.tile([512], F32, dsem=False) for _ in range(2)]
            ti = 0
            hi7 = 0
            for (bi, row0, T, nt, R, is_s) in blocks():
                gr = g2s if is_s else g2r
                TH = 256 if not is_s else NS
                for half in range(T // TH):
                    ht = hts[hi7 % 2]
                    hi7 += 1
                    c0 = row0 + half * TH
                    for q in range(2):
                        P.dma(sp, ht.t[:, q * 32:(q + 1) * 32, 0:TH], hidT_d[:, q * 32:(q + 1) * 32, c0:c0 + TH], writes=[ht])
                    for t in range(TH // R):
                        x1t, tm = x1s[ti % 2], tm7[ti % 2]
                        ti += 1
                        r0 = c0 + t * R
                        P.dma(sp, x1t.t[0:R, :], x1_d[r0:r0 + R, grp * 512:(grp + 1) * 512], writes=[x1t])
                        ps = psrot.get()
                        P.op(pe, [lambda e, k=k, ps=ps, ht=ht, t=t: e.matmul(ps.t[0:R, :], ht.t[:, k, t * R:(t + 1) * R], Wd.t[:, k, :], start=(k == 0), stop=(k == 63)) for k in range(64)],
                             reads=[Wd, ht], writes=[ps])
                        P.op(dve, lambda e, ps=ps, tm=tm, gr=gr, grp=grp: e.tensor_tensor(out=tm.t[0:R, :], in0=ps.t[0:R, :], in1=gr.t[0:R, grp * 512:(grp + 1) * 512], op=ALU.mult), reads=[ps, gr], writes=[tm])
                        P.op(dve, lambda e, tm=tm, x1t=x1t: e.tensor_tensor(out=x1t.t[0:R, :], in0=x1t.t[0:R, :], in1=tm.t[0:R, :], op=ALU.add), reads=[tm, x1t], writes=[x1t])
                        P.dma(pool, y_all[r0:r0 + R, grp * 512:(grp + 1) * 512], x1t.t[0:R, :], reads=[x1t])
            P.barrier()
        P.barrier()

        with nc.allow_non_contiguous_dma(reason="small per-partition column loads"), nc.Block() as block:
            @block.tensor
            def _(e):
                P.replay(pe, e)

            @block.scalar
            def _(e):
                P.replay(act, e)

            @block.vector
            def _(e):
                P.replay(dve, e)

            @block.gpsimd
            def _(e):
                P.replay(pool, e)

            @block.sync
            def _(e):
                P.replay(sp, e)
    return nc


_CACHE = {}


def _rope_tables(pos):
    inv = (np.float32(10000.0) ** (-(np.arange(0, ROPE, 2, dtype=np.float32) / np.float32(ROPE)))).astype(np.float32)
    ang = (pos.astype(np.float32)[:, None] * inv[None, :]).astype(np.float32)
    return np.cos(ang).astype(np.float32), np.sin(ang).astype(np.float32)


def kernel(x_prompt, x_sample, cache_kv_latent, cache_k_rope, c_prompt, c_sample,
           w_ada, b_ada, g_norm1, g_norm2, w_in, g_sg, w_s, b_s,
           g_q_a, w_uq, g_q_nope, g_q_rope, g_kv_a, g_k_rope, w_uk, g_k_nope, w_uv,
           w_pa, w_pb, w_o, w_up, w_down):
    f = lambda a: np.ascontiguousarray(np.asarray(a, dtype=np.float32))
    x_prompt, x_sample = f(x_prompt), f(x_sample)
    B, SEQ, _ = x_prompt.shape
    DB, TS_, _ = x_sample.shape
    PAST = cache_kv_latent.shape[2]
    assert B == 2 and DB == 32 and TS_ == TS
    NB = SEQ // 512
    NJ = NB // 4
    NOWN = NJ * 512
    NS = SB * TS
    NTOK = NOWN + NS
    key = (SEQ, PAST)
    if key not in _CACHE:
        _CACHE[key] = build_program(SEQ, PAST)
    nc = _CACHE[key]

    cosb, sinb = _rope_tables(np.arange(SEQ))
    ropetm = np.concatenate([cosb, sinb], axis=1)
    shared = {
        "w_ada": f(w_ada[0]), "b_ada": f(b_ada[0])[None, :], "g_norm1": f(g_norm1[0])[None, :], "g_norm2": f(g_norm2[0])[None, :],
        "w_in": f(w_in[0]), "g_sg": f(g_sg[0])[None, :], "w_s": f(w_s[0]), "b_s": f(b_s[0]).reshape(1, -1),
        "g_q_a": f(g_q_a[0])[None, :], "w_uq": f(w_uq[0]), "g_q_nope": f(g_q_nope[0])[None, :], "g_q_rope": f(g_q_rope[0])[None, :],
        "g_kv_a": f(g_kv_a[0])[None, :], "g_k_rope": f(g_k_rope[0])[None, :], "w_uk": f(w_uk[0]).reshape(KVL, NH * 128),
        "g_k_nope": f(g_k_nope[0])[None, :], "w_uv": f(w_uv[0]).reshape(KVL, NH * 128),
        "w_pa": f(w_pa[0]), "w_pb": f(w_pb[0]), "w_o": f(w_o[0]), "w_up": f(w_up[0]), "w_down": f(w_down[0]),
        "ropetm": ropetm, "ident": np.eye(128, dtype=np.float32), "tril": np.tril(np.ones((128, 128), np.float32)),
    }
    in_maps = []
    for k in range(8):
        b, c = k // 4, k % 4
        rows = np.concatenate([np.arange((4 * J + c) * 512, (4 * J + c + 1) * 512) for J in range(NJ)])
        xo = np.concatenate([x_prompt[b][rows], x_sample[4 * k:4 * k + 4].reshape(NS, D)], axis=0)
        pos_o = np.concatenate([rows, np.tile(PAST + np.arange(TS), SB)])
        co, so = _rope_tables(pos_o)
        ropeo = np.concatenate([co, so], axis=1)
        town = np.concatenate([co.T, co.T, -so.T, so.T], axis=0)
        p = np.arange(128)[:, None, None]
        kbt = np.arange(16)[None, :, None]
        qc = np.arange(8)[None, None, :]
        mask = ((kbt * 2 + p // 64) <= (c * 8 + qc)).astype(np.float32).reshape(128, 128)
        m = dict(shared)
        m.update({
            "xb": x_prompt[b], "xo": f(xo),
            "clat": f(cache_kv_latent[0, 4 * k:4 * k + 4]), "ckr": f(cache_k_rope[0, 4 * k:4 * k + 4]),
            "cvec": f(np.concatenate([np.asarray(c_prompt)[b:b + 1], np.asarray(c_sample)[4 * k:4 * k + 4]], axis=0)),
            "ropeo": f(ropeo), "town": f(town), "maskc": f(mask),
        })
        in_maps.append(m)
    res = run_bass_kernel_spmd(nc, in_maps, core_ids=list(range(8)))
    y_p = np.zeros((B, SEQ, D), np.float32)
    y_s = np.zeros((DB, TS, D), np.float32)
    lat_p = np.zeros((1, B, SEQ, KVL), np.float32)
    kr_p = np.zeros((1, B, SEQ, ROPE), np.float32)
    lat_s = np.zeros((1, DB, TS, KVL), np.float32)
    kr_s = np.zeros((1, DB, TS, ROPE), np.float32)
    v_s = np.zeros((1, DB, TS, D), np.float32)
    for k in range(8):
        b, c = k // 4, k % 4
        r = res.results[k]
        rows = np.concatenate([np.arange((4 * J + c) * 512, (4 * J + c + 1) * 512) for J in range(NJ)])
        y_p[b, rows] = r["y_all"][:NOWN]
        lat_p[0, b, rows] = r["lat_all"][:NOWN]
        kr_p[0, b, rows] = r["kr_all"][:NOWN]
        y_s[4 * k:4 * k + 4] = r["y_all"][NOWN:].reshape(SB, TS, D)
        lat_s[0, 4 * k:4 * k + 4] = r["lat_all"][NOWN:].reshape(SB, TS, KVL)
        kr_s[0, 4 * k:4 * k + 4] = r["kr_all"][NOWN:].reshape(SB, TS, ROPE)
        v_s[0, 4 * k:4 * k + 4] = r["v_all"].reshape(SB, TS, D)
    return (y_p, y_s, lat_p, kr_p, lat_s, kr_s, v_s)
```

```python
import numpy as np
import concourse.bass as bass
import concourse.mybir as mybir
from concourse.bass_utils import run_bass_kernel_spmd

F32 = mybir.dt.float32
BF16 = mybir.dt.bfloat16
AF = mybir.ActivationFunctionType
ALU = mybir.AluOpType

D = 2048
NH = 16
QL = 512
KVL = 512
ROPE = 64
QKH = 192
OFF_Q = 4096
OFF_KV = OFF_Q + QL
OFF_GATE = OFF_KV + KVL + ROPE
IN_COLS = OFF_GATE + 2 * D
HID = 4 * D
EPS = 1e-6
SCALE = QKH ** -0.5
TS = 16
SB = 4


class _Rec:
    def __init__(self):
        self.calls = []

    def __getattr__(self, name):
        def f(*a, **kw):
            self.calls.append((name, a, kw))
            return None
        return f


class Sem:
    def __init__(self, h):
        self.h = h
        self.val = 0


class Res:
    def __init__(self, t, dsem=None):
        self.t = t
        self.w = None
        self.r = {}
        self.dsem = dsem


class Eng:
    def __init__(self, name, sem, self_sync):
        self.name = name
        self.sem = sem
        self.ops = []
        self.seen = {}
        self.self_sync = self_sync

    def wait(self, tok):
        if tok is None:
            return
        sem, val = tok
        if sem is self.sem and not self.self_sync:
            return
        if self.seen.get(id(sem), 0) >= val:
            return
        self.seen[id(sem)] = val
        self.ops.append(("w", sem.h, val))


class Prog:
    def __init__(self, nc, stack):
        self.nc = nc
        mk = lambda n: Sem(stack.enter_context(nc.semaphore(n)))
        self.pe = Eng("pe", mk("s_pe"), False)
        self.act = Eng("act", mk("s_act"), True)
        self.dve = Eng("dve", mk("s_dve"), True)
        self.pool = Eng("pool", mk("s_pool"), True)
        self.sp = Eng("sp", mk("s_sp"), True)
        self.engs = [self.pe, self.act, self.dve, self.pool, self.sp]
        self.dsems = [mk("d%d" % i) for i in range(90)]
        self.dnext = 0
        self.outstanding = {}

    def new_dsem(self):
        s = self.dsems[self.dnext % len(self.dsems)]
        self.dnext += 1
        return s

    def deps(self, E, reads, writes):
        for r in reads:
            E.wait(r.w)
        for w in writes:
            E.wait(w.w)
            for tok in list(w.r.values()):
                E.wait(tok)

    def mark(self, tok, reads, writes):
        for r in reads:
            r.r[id(tok[0])] = tok
        for w in writes:
            w.w = tok
            w.r = {}

    def op(self, E, fns, reads=(), writes=()):
        if callable(fns):
            fns = [fns]
        self.deps(E, reads, writes)
        E.sem.val += 1
        tok = (E.sem, E.sem.val)
        calls = []
        for f in fns:
            rec = _Rec()
            f(rec)
            assert len(rec.calls) == 1
            calls.append(rec.calls[0])
        for c in calls[:-1]:
            E.ops.append(("i", c, None))
        E.ops.append(("i", calls[-1], E.sem.h))
        self.mark(tok, reads, writes)

    def dma(self, Q, out, in_, reads=(), writes=()):
        self.deps(Q, reads, writes)
        sem = None
        for x in list(writes) + list(reads):
            if x.dsem is not None:
                sem = x.dsem
                break
        assert sem is not None
        sem.val += 16
        tok = (sem, sem.val)
        Q.ops.append(("d", out, in_, sem.h))
        self.mark(tok, reads, writes)
        self.outstanding[id(sem)] = tok

    def barrier(self):
        toks = [(E.sem, E.sem.val) for E in self.engs if E.sem.val > 0] + list(self.outstanding.values())
        for E in self.engs:
            for t in toks:
                E.wait(t)
        self.outstanding = {}

    def replay(self, E, e):
        for o in E.ops:
            if o[0] == "w":
                e.wait_ge(o[1], o[2])
            elif o[0] == "i":
                name, a, kw = o[1]
                ins = getattr(e, name)(*a, **kw)
                if o[2] is not None:
                    ins.then_inc(o[2], 1)
            else:
                e.dma_start(out=o[1], in_=o[2]).then_inc(o[3], 16)


class Arena:
    def __init__(self, big, nwords, prog):
        self.big = big
        self.cap = nwords
        self.off = 0
        self.prog = prog

    def reset(self, to=0):
        self.off = to

    def raw(self, n, dt):
        words = (n * (4 if dt == F32 else 2) + 3) // 4
        words = (words + 7) // 8 * 8
        assert self.off + words <= self.cap, "SBUF arena overflow %d+%d>%d" % (self.off, words, self.cap)
        v = self.big[:, self.off:self.off + words]
        self.off += words
        if dt == BF16:
            v = v.bitcast(BF16)
        return v[:, 0:n]

    def tile(self, shape, dt, dsem=True):
        n = int(np.prod(shape))
        v = self.raw(n, dt)
        if len(shape) == 2:
            v = v.rearrange("p (a b) -> p a b", a=shape[0], b=shape[1])
        elif len(shape) == 3:
            v = v.rearrange("p (a b c) -> p a b c", a=shape[0], b=shape[1], c=shape[2])
        return Res(v, self.prog.new_dsem() if dsem else None)


class Rot:
    def __init__(self, items):
        self.items = items
        self.i = 0

    def get(self):
        x = self.items[self.i % len(self.items)]
        self.i += 1
        return x


def build_program(SEQ, PAST):
    import contextlib
    nc = bass.Bass("TRN2", target_bir_lowering=False)
    NB = SEQ // 512
    NJ = NB // 4
    NOWN = NJ * 512
    NS = SB * TS
    NTOK = NOWN + NS
    NKT = PAST // 128
    PASTP = PAST + TS
    NT = SEQ // 128

    def din(name, shape, dt=F32):
        return nc.dram_tensor(name, list(shape), dt, kind="ExternalInput").ap()

    def dout(name, shape):
        return nc.dram_tensor(name, list(shape), F32, kind="ExternalOutput").ap()

    def dtmp(name, shape, dt):
        return nc.dram_tensor(name, list(shape), dt, kind="Internal").ap()

    xb = din("xb", [SEQ, D])
    xo = din("xo", [NTOK, D])
    clat = din("clat", [SB, PAST, KVL])
    ckr = din("ckr", [SB, PAST, ROPE])
    cvec = din("cvec", [5, D])
    w_ada = din("w_ada", [D, 6 * D])
    b_ada = din("b_ada", [1, 6 * D])
    g_norm1 = din("g_norm1", [1, D])
    g_norm2 = din("g_norm2", [1, D])
    w_in = din("w_in", [D, IN_COLS])
    g_sg = din("g_sg", [1, D])
    w_s = din("w_s", [NH, 128, 128])
    b_s = din("b_s", [1, NH * 128])
    g_q_a = din("g_q_a", [1, QL])
    w_uq = din("w_uq", [QL, NH * QKH])
    g_q_nope = din("g_q_nope", [1, 128])
    g_q_rope = din("g_q_rope", [1, ROPE])
    g_kv_a = din("g_kv_a", [1, KVL])
    g_k_rope = din("g_k_rope", [1, ROPE])
    w_uk = din("w_uk", [KVL, NH * 128])
    g_k_nope = din("g_k_nope", [1, 128])
    w_uv = din("w_uv", [KVL, NH * 128])
    w_pa = din("w_pa", [D, D])
    w_pb = din("w_pb", [D, D])
    w_o = din("w_o", [D, D])
    w_up = din("w_up", [D, HID])
    w_down = din("w_down", [HID, D])
    ropetm = din("ropetm", [SEQ, ROPE])
    ropeo = din("ropeo", [NTOK, ROPE])
    town = din("town", [128, NTOK])
    maskc = din("maskc", [128, 16 * 8])
    identd = din("ident", [128, 128])
    trild = din("tril", [128, 128])

    y_all = dout("y_all", [NTOK, D])
    lat_all = dout("lat_all", [NTOK, KVL])
    kr_all = dout("kr_all", [NTOK, ROPE])
    v_all = dout("v_all", [NS, D])

    mod_d = dtmp("mod_d", [5, 6 * D], F32)
    KT_d = dtmp("KT_d", [NH, 128, SEQ], BF16)
    KrT_d = dtmp("KrT_d", [128, SEQ], BF16)
    V_d = dtmp("V_d", [NH, 128, NT, 128], BF16)
    KTs_d = dtmp("KTs_d", [SB, NH, 128, PASTP], BF16)
    KrTs_d = dtmp("KrTs_d", [SB, 128, PASTP], BF16)
    Vs_d = dtmp("Vs_d", [SB, NH, 128, NKT + 1, 128], BF16)
    hT_d = dtmp("hT_d", [128, 16, NTOK], BF16)
    uT_d = dtmp("uT_d", [128, 16, NTOK], BF16)
    v_d = dtmp("v_d", [NTOK, D], BF16)
    QT_d = dtmp("QT_d", [NH, 128, NTOK], BF16)
    QrT_d = dtmp("QrT_d", [NH, 128, NTOK], BF16)
    gaT_d = dtmp("gaT_d", [128, 16, NTOK], BF16)
    gbT_d = dtmp("gbT_d", [128, 16, NTOK], BF16)
    osgT_d = dtmp("osgT_d", [128, 16, NTOK], BF16)
    omlaT_d = dtmp("omlaT_d", [128, 16, NTOK], BF16)
    mT_d = dtmp("mT_d", [128, 16, NTOK], BF16)
    h2T_d = dtmp("h2T_d", [128, 16, NTOK], BF16)
    x1_d = dtmp("x1_d", [NTOK, D], F32)
    hidT_d = dtmp("hidT_d", [128, 64, NTOK], BF16)

    AW = 51200
    with contextlib.ExitStack() as stack:
        big = stack.enter_context(nc.sbuf_tensor("arena", [128, AW], F32))
        psb = [stack.enter_context(nc.psum_tensor("ps%d" % i, [128, 512], F32)) for i in range(8)]
        P = Prog(nc, stack)
        A = Arena(big, AW, P)
        PS = [Res(psb[i][:, :]) for i in range(8)]
        pe, act, dve, pool, sp = P.pe, P.act, P.dve, P.pool, P.sp

        def psbf(res):
            return res.t.bitcast(BF16)

        identf = A.tile([128], F32)
        identb = A.tile([128], BF16, dsem=False)
        ones_b = A.tile([128], BF16, dsem=False)
        ones_f = A.tile([128], F32, dsem=False)
        P.dma(sp, identf.t, identd[:, :], writes=[identf])
        P.op(dve, lambda e: e.tensor_copy(out=identb.t, in_=identf.t), reads=[identf], writes=[identb])
        P.op(dve, lambda e: e.memset(ones_b.t, 1.0), writes=[ones_b])
        P.op(dve, lambda e: e.memset(ones_f.t, 1.0), writes=[ones_f])
        PERSIST = A.off

        def blocks():
            for j in range(NJ):
                yield (j, 512 * j, 512, 4, 128, False)
            yield (NJ, NOWN, NS, 1, NS, True)

        def wview(w, c0, c1):
            return w.rearrange("(k p) n -> p k n", p=128)[:, :, c0:c1]

        def load_rows_prompt_or_sample(tile, src_prompt_row, src_sample_rows, is_s, n):
            if not is_s:
                P.dma(sp, tile.t, src_prompt_row.partition_broadcast(128), writes=[tile])
            else:
                for i in range(SB):
                    P.dma(sp, tile.t[16 * i:16 * i + 16, :], src_sample_rows[i].partition_broadcast(16), writes=[tile])

        def rstd_col(ss, rs, R, n):
            P.op(act, lambda e: e.activation(out=rs.t[0:R, :], in_=ss.t[0:R, :], func=AF.Ln, scale=1.0 / n, bias=EPS),
                 reads=[ss], writes=[rs])
            P.op(act, lambda e: e.activation(out=rs.t[0:R, :], in_=rs.t[0:R, :], func=AF.Exp, scale=-0.5), reads=[rs], writes=[rs])

        def rstd_tile(ps, rs, T, n, parts=128):
            P.op(act, lambda e: e.activation(out=rs.t[0:parts, 0:T], in_=ps.t[0:parts, 0:T], func=AF.Ln, scale=1.0 / n, bias=EPS),
                 reads=[ps], writes=[rs])
            P.op(act, lambda e: e.activation(out=rs.t[0:parts, 0:T], in_=rs.t[0:parts, 0:T], func=AF.Exp, scale=-0.5), reads=[rs], writes=[rs])

        psrot = Rot(PS)

        def transpose_to(dst_fn, src, R, nk, reads, writes_res):
            for k0 in range(0, nk, 8):
                kk = min(8, nk - k0)
                ps = psrot.get()
                pv = psbf(ps)
                fns = []
                for k in range(kk):
                    fns.append(lambda e, k=k: e.transpose(pv[:, k * 128:k * 128 + R], src.t[0:R, (k0 + k) * 128:(k0 + k + 1) * 128], identb.t[0:R, 0:R]))
                P.op(pe, fns, reads=[src, identb] + reads, writes=[ps])
                yield k0, kk, ps, pv

        c5 = A.tile([D], F32)
        c5b = A.tile([D], BF16, dsem=False)
        cT = A.tile([16, 8], BF16, dsem=False)
        bada = A.tile([6 * D], F32)
        wb = [A.tile([16, 512], BF16) for _ in range(2)]
        msb = [A.tile([512], F32) for _ in range(2)]
        P.dma(sp, c5.t[0:5, :], cvec[:, :], writes=[c5])
        P.dma(sp, bada.t[0:5, :], b_ada[0:1, :].partition_broadcast(5), writes=[bada])
        P.op(act, lambda e: e.activation(out=c5b.t[0:5, :], in_=c5.t[0:5, :], func=AF.Silu), reads=[c5], writes=[c5b])
        for k0 in (0, 8):
            ps = psrot.get()
            pv = psbf(ps)
            P.op(pe, [lambda e, k=k, pv=pv: e.transpose(pv[:, k * 8:k * 8 + 5], c5b.t[0:5, (k0 + k) * 128:(k0 + k + 1) * 128], identb.t[0:5, 0:5]) for k in range(8)],
                 reads=[c5b, identb], writes=[ps])
            P.op(dve, lambda e, pv=pv, k0=k0: e.tensor_copy(out=cT.t[:, k0:k0 + 8, 0:5], in_=pv[:, 0:64].rearrange("p (k n) -> p k n", k=8, n=8)[:, :, 0:5]),
                 reads=[ps], writes=[cT])
        for cc in range(24):
            w = wb[cc % 2]
            m = msb[cc % 2]
            P.dma(pool, w.t, wview(w_ada, cc * 512, (cc + 1) * 512), writes=[w])
            ps = psrot.get()
            P.op(pe, [lambda e, k=k, w=w, ps=ps: e.matmul(ps.t[0:5, :], cT.t[:, k, 0:5], w.t[:, k, :], start=(k == 0), stop=(k == 15)) for k in range(16)],
                 reads=[cT, w], writes=[ps])
            P.op(dve, lambda e, ps=ps, m=m, cc=cc: e.tensor_tensor(out=m.t[0:5, :], in0=ps.t[0:5, :], in1=bada.t[0:5, cc * 512:(cc + 1) * 512], op=ALU.add),
                 reads=[ps, bada], writes=[m])
            P.dma(pool, mod_d[:, cc * 512:(cc + 1) * 512], m.t[0:5, :], reads=[m])
        P.barrier()
        A.reset(PERSIST)

        def make_front_rows(gn_d, off_sh, off_sc, is_s, gn):
            G = A.tile([D], F32)
            S = A.tile([D], F32)
            load_rows_prompt_or_sample(S, mod_d[0:1, off_sh:off_sh + D], [mod_d[1 + i:2 + i, off_sh:off_sh + D] for i in range(SB)], is_s, D)
            load_rows_prompt_or_sample(G, mod_d[0:1, off_sc:off_sc + D], [mod_d[1 + i:2 + i, off_sc:off_sc + D] for i in range(SB)], is_s, D)
            P.dma(sp, gn.t, gn_d[0:1, :].partition_broadcast(128), writes=[gn])
            P.op(dve, lambda e: e.scalar_tensor_tensor(out=G.t, in0=G.t, scalar=1.0, in1=gn.t, op0=ALU.add, op1=ALU.mult),
                 reads=[G, gn], writes=[G])
            return G, S

        def reload_front_rows(G, S, gn_d, off_sh, off_sc, gn):
            load_rows_prompt_or_sample(S, None, [mod_d[1 + i:2 + i, off_sh:off_sh + D] for i in range(SB)], True, D)
            load_rows_prompt_or_sample(G, None, [mod_d[1 + i:2 + i, off_sc:off_sc + D] for i in range(SB)], True, D)
            P.op(dve, lambda e: e.scalar_tensor_tensor(out=G.t[0:NS, :], in0=G.t[0:NS, :], scalar=1.0, in1=gn.t[0:NS, :], op0=ALU.add, op1=ALU.mult),
                 reads=[G, gn], writes=[G])

        class FrontBufs:
            def __init__(self):
                self.junk = A.tile([D], BF16, dsem=False)
                self.tmp = A.tile([D], F32)
                self.hb = [A.tile([D], BF16, dsem=False) for _ in range(2)]
                self.ss = [A.tile([1], F32, dsem=False) for _ in range(2)]
                self.rs = [A.tile([1], F32, dsem=False) for _ in range(2)]
                self.i = 0

        def front_tile(fb, xt, R, G, S, hT, col0):
            i = fb.i
            fb.i += 1
            ss, rs, hb = fb.ss[i % 2], fb.rs[i % 2], fb.hb[i % 2]
            P.op(dve, lambda e: e.memset(ss.t[0:R, :], 0.0), writes=[ss])
            P.op(act, lambda e: e.activation(out=fb.junk.t[0:R, :], in_=xt.t[0:R, :], func=AF.Square, accum_out=ss.t[0:R, :]),
                 reads=[xt, ss], writes=[fb.junk, ss])
            rstd_col(ss, rs, R, D)
            P.op(dve, lambda e: e.scalar_tensor_tensor(out=fb.tmp.t[0:R, :], in0=xt.t[0:R, :], scalar=rs.t[0:R, :], in1=G.t[0:R, :], op0=ALU.mult, op1=ALU.mult),
                 reads=[xt, rs, G], writes=[fb.tmp])
            P.op(dve, lambda e: e.tensor_tensor(out=hb.t[0:R, :], in0=fb.tmp.t[0:R, :], in1=S.t[0:R, :], op=ALU.add),
                 reads=[fb.tmp, S], writes=[hb])
            for k0, kk, ps, pv in transpose_to(None, hb, R, 16, [], None):
                P.op(act, lambda e, k0=k0, kk=kk, pv=pv: e.copy(out=hT.t[:, k0:k0 + kk, col0:col0 + R],
                                                               in_=pv.rearrange("p (k n) -> p k n", k=8, n=128)[:, 0:kk, 0:R]),
                     reads=[ps], writes=[hT])

        class KvBufs:
            def __init__(self):
                self.gkv = A.tile([KVL], F32)
                self.gkr = A.tile([ROPE], F32)
                P.dma(sp, self.gkv.t, g_kv_a[0:1, :].partition_broadcast(128), writes=[self.gkv])
                P.dma(sp, self.gkr.t, g_k_rope[0:1, :].partition_broadcast(128), writes=[self.gkr])
                self.junk = A.tile([KVL], BF16, dsem=False)
                self.ss = [A.tile([2], F32, dsem=False) for _ in range(2)]
                self.rs = [A.tile([2], F32, dsem=False) for _ in range(2)]
                self.kn = [A.tile([ROPE], F32, dsem=False) for _ in range(2)]
                self.t4 = [A.tile([4, 32], F32, dsem=False) for _ in range(2)]
                self.cs = [A.tile([ROPE], F32) for _ in range(2)]
                self.i = 0

        def zkv_tile(kb, Wkv, hT, col0, R, rope_rows_ap, ckv_out, kr_out):
            i = kb.i
            kb.i += 1
            ss, rs, kn, t4, cs = kb.ss[i % 2], kb.rs[i % 2], kb.kn[i % 2], kb.t4[i % 2], kb.cs[i % 2]
            P.dma(sp, cs.t[0:R, :], rope_rows_ap, writes=[cs])
            psc = psrot.get()
            P.op(pe, [lambda e, k=k: e.matmul(psc.t[0:R, :], hT.t[:, k, col0:col0 + R], Wkv.t[:, k, 0:KVL], start=(k == 0), stop=(k == 15)) for k in range(16)],
                 reads=[hT, Wkv], writes=[psc])
            psr = psrot.get()
            P.op(pe, [lambda e, k=k: e.matmul(psr.t[0:R, 0:ROPE], hT.t[:, k, col0:col0 + R], Wkv.t[:, k, KVL:KVL + ROPE], start=(k == 0), stop=(k == 15)) for k in range(16)],
                 reads=[hT, Wkv], writes=[psr])
            P.op(dve, lambda e: e.memset(ss.t[0:R, :], 0.0), writes=[ss])
            P.op(act, lambda e: e.activation(out=kb.junk.t[0:R, :], in_=psc.t[0:R, :], func=AF.Square, accum_out=ss.t[0:R, 0:1]),
                 reads=[psc, ss], writes=[kb.junk, ss])
            P.op(act, lambda e: e.activation(out=kb.junk.t[0:R, 0:ROPE], in_=psr.t[0:R, 0:ROPE], func=AF.Square, accum_out=ss.t[0:R, 1:2]),
                 reads=[psr, ss], writes=[kb.junk, ss])
            P.op(act, lambda e: e.activation(out=rs.t[0:R, 0:1], in_=ss.t[0:R, 0:1], func=AF.Ln, scale=1.0 / KVL, bias=EPS), reads=[ss], writes=[rs])
            P.op(act, lambda e: e.activation(out=rs.t[0:R, 1:2], in_=ss.t[0:R, 1:2], func=AF.Ln, scale=1.0 / ROPE, bias=EPS), reads=[ss], writes=[rs])
            P.op(act, lambda e: e.activation(out=rs.t[0:R, :], in_=rs.t[0:R, :], func=AF.Exp, scale=-0.5), reads=[rs], writes=[rs])
            P.op(dve, lambda e: e.scalar_tensor_tensor(out=ckv_out.t[0:R, :], in0=psc.t[0:R, :], scalar=rs.t[0:R, 0:1], in1=kb.gkv.t[0:R, :], op0=ALU.mult, op1=ALU.mult),
                 reads=[psc, rs, kb.gkv], writes=[ckv_out])
            P.op(dve, lambda e: e.scalar_tensor_tensor(out=kn.t[0:R, :], in0=psr.t[0:R, 0:ROPE], scalar=rs.t[0:R, 1:2], in1=kb.gkr.t[0:R, :], op0=ALU.mult, op1=ALU.mult),
                 reads=[psr, rs, kb.gkr], writes=[kn])
            x1, x2 = kn.t[0:R, 0:32], kn.t[0:R, 32:64]
            co, si = cs.t[0:R, 0:32], cs.t[0:R, 32:64]
            P.op(dve, [lambda e: e.tensor_tensor(out=t4.t[0:R, 0, :], in0=x1, in1=co, op=ALU.mult),
                       lambda e: e.tensor_tensor(out=t4.t[0:R, 1, :], in0=x2, in1=si, op=ALU.mult),
                       lambda e: e.tensor_tensor(out=t4.t[0:R, 2, :], in0=x2, in1=co, op=ALU.mult),
                       lambda e: e.tensor_tensor(out=t4.t[0:R, 3, :], in0=x1, in1=si, op=ALU.mult)],
                 reads=[kn, cs], writes=[t4])
            P.op(dve, [lambda e: e.tensor_tensor(out=kr_out.t[0:R, 0:32], in0=t4.t[0:R, 0, :], in1=t4.t[0:R, 1, :], op=ALU.subtract),
                       lambda e: e.tensor_tensor(out=kr_out.t[0:R, 32:64], in0=t4.t[0:R, 2, :], in1=t4.t[0:R, 3, :], op=ALU.add)],
                 reads=[t4], writes=[kr_out])

        class ExpBufs:
            def __init__(self, TG):
                self.wuk = A.tile([4, NH * 128], BF16)
                self.wuv = A.tile([4, NH * 128], BF16)
                P.dma(pool, self.wuk.t, wview(w_uk, 0, NH * 128), writes=[self.wuk])
                P.dma(pool, self.wuv.t, wview(w_uv, 0, NH * 128), writes=[self.wuv])
                self.gk = A.tile([1], F32)
                P.dma(sp, self.gk.t, g_k_nope.rearrange("o d -> d o"), writes=[self.gk])
                self.cb = [A.tile([KVL], BF16, dsem=False) for _ in range(2)]
                self.krb = [A.tile([128], BF16, dsem=False) for _ in range(2)]
                self.cT = [A.tile([4, TG], BF16, dsem=False) for _ in range(2)]
                self.krT = [A.tile([TG], BF16) for _ in range(2)]
                self.sq = [A.tile([TG], BF16, dsem=False) for _ in range(2)]
                self.rs = [A.tile([TG], F32, dsem=False) for _ in range(2)]
                self.KTs = [A.tile([NH, TG], BF16) for _ in range(1)]
                self.Vs = [A.tile([NH, TG // 128 if TG >= 128 else 1, 128], BF16) for _ in range(1)]
                self.i = 0

        def expand_group(xb_, tiles, T, KT_dst, KrT_dst, V_dst, hooks=None):
            i = xb_.i
            xb_.i += 1
            cT, krT, KTs, Vs = xb_.cT[i % 2], xb_.krT[i % 2], xb_.KTs[0], xb_.Vs[0]
            col = 0
            cols = []
            for ti, (ckv, kr, R) in enumerate(tiles):
                cb, krb = xb_.cb[ti % 2], xb_.krb[ti % 2]
                P.op(act, lambda e, cb=cb, ckv=ckv, R=R: e.copy(out=cb.t[0:R, :], in_=ckv.t[0:R, :]), reads=[ckv], writes=[cb])
                P.op(dve, [lambda e, krb=krb, kr=kr, R=R: e.tensor_copy(out=krb.t[0:R, 0:64], in_=kr.t[0:R, :]),
                           lambda e, krb=krb, kr=kr, R=R: e.tensor_copy(out=krb.t[0:R, 64:128], in_=kr.t[0:R, :])], reads=[kr], writes=[krb])
                for k0, kk, ps, pv in transpose_to(None, cb, R, 4, [], None):
                    P.op(dve, lambda e, pv=pv, col=col, R=R: e.tensor_copy(out=cT.t[:, 0:4, col:col + R], in_=pv[:, 0:512].rearrange("p (k n) -> p k n", k=4, n=128)[:, :, 0:R]),
                         reads=[ps], writes=[cT])
                for k0, kk, ps, pv in transpose_to(None, krb, R, 1, [], None):
                    P.op(dve, lambda e, pv=pv, col=col, R=R: e.tensor_copy(out=krT.t[:, col:col + R], in_=pv[:, 0:R]), reads=[ps], writes=[krT])
                cols.append((col, R))
                col += R
            assert col == T
            P.dma(pool, KrT_dst, krT.t[:, 0:T], reads=[krT])
            def k_mm(h):
                psk = psrot.get()
                P.op(pe, [lambda e, kc=kc: e.matmul(psk.t[:, 0:T], xb_.wuk.t[:, kc, h * 128:(h + 1) * 128], cT.t[:, kc, 0:T], start=(kc == 0), stop=(kc == 3)) for kc in range(4)],
                     reads=[xb_.wuk, cT], writes=[psk])
                return psk

            def k_rest(h, psk):
                sq, rs = xb_.sq[h % 2], xb_.rs[h % 2]
                P.op(act, lambda e: e.activation(out=sq.t[:, 0:T], in_=psk.t[:, 0:T], func=AF.Square), reads=[psk], writes=[sq])
                pss = psrot.get()
                P.op(pe, lambda e: e.matmul(pss.t[:, 0:T], ones_b.t, sq.t[:, 0:T], start=True, stop=True), reads=[ones_b, sq], writes=[pss])
                rstd_tile(pss, rs, T, 128)
                P.op(dve, lambda e: e.scalar_tensor_tensor(out=KTs.t[:, h, 0:T], in0=psk.t[:, 0:T], scalar=xb_.gk.t[:, 0:1], in1=rs.t[:, 0:T], op0=ALU.mult, op1=ALU.mult),
                     reads=[psk, rs, xb_.gk], writes=[KTs])

            LK = 2
            psks = {}
            for i in range(NH + LK):
                if i < NH:
                    psks[i] = k_mm(i)
                if i - LK >= 0:
                    k_rest(i - LK, psks.pop(i - LK))
                    if hooks and (i - LK) in hooks:
                        hooks[i - LK]()
            P.dma(pool, KT_dst, KTs.t[:, :, 0:T], reads=[KTs])
            Rmax = max(R for _, R in cols)
            for ti, (col, R) in enumerate(cols):
                for c4 in range(4):
                    psv = psrot.get()
                    P.op(pe, [lambda e, kc=kc, psv=psv, col=col, R=R, c4=c4: e.matmul(psv.t[0:R, :], cT.t[:, kc, col:col + R], xb_.wuv.t[:, kc, c4 * 512:(c4 + 1) * 512], start=(kc == 0), stop=(kc == 3)) for kc in range(4)],
                         reads=[xb_.wuv, cT], writes=[psv])
                    P.op(act, lambda e, psv=psv, ti=ti, R=R, c4=c4: e.copy(out=Vs.t[0:R, c4 * 4:(c4 + 1) * 4, ti, :], in_=psv.t[0:R, :].rearrange("p (h c) -> p h c", h=4, c=128)),
                         reads=[psv], writes=[Vs])
            P.dma(pool, V_dst, Vs.t[0:Rmax, :, 0:len(cols), :], reads=[Vs])

        fb = FrontBufs()
        G1, S1 = make_front_rows(g_norm1, 0, D, False, fb.tmp)
        kb = KvBufs()
        Wkv = A.tile([16, KVL + ROPE], BF16)
        P.dma(pool, Wkv.t, wview(w_in, OFF_KV, OFF_GATE), writes=[Wkv])
        eb = ExpBufs(512)
        xts = [A.tile([D], F32) for _ in range(2)]
        hTs = [A.tile([16, 512], BF16) for _ in range(1)]
        ckvs = [A.tile([KVL], F32) for _ in range(4)]
        krs = [A.tile([ROPE], F32) for _ in range(4)]
        xi = [0]
        hT = hTs[0]

        def a_front(g, t):
            xt = xts[xi[0] % 2]
            xi[0] += 1
            r0 = g * 512 + t * 128
            P.dma(sp, xt.t, xb[r0:r0 + 128, :], writes=[xt])
            front_tile(fb, xt, 128, G1, S1, hT, t * 128)

        def a_zkv(g):
            tl = []
            for t in range(4):
                ck, kr_ = ckvs[t], krs[t]
                r0 = g * 512 + t * 128
                zkv_tile(kb, Wkv, hT, t * 128, 128, ropetm[r0:r0 + 128, :], ck, kr_)
                tl.append((ck, kr_, 128))
            return tl

        for t in range(4):
            a_front(0, t)
        tl = a_zkv(0)
        for g in range(NB):
            hooks = None
            if g + 1 < NB:
                hooks = {1 + 4 * t: (lambda t=t, g=g: a_front(g + 1, t)) for t in range(4)}
            expand_group(eb, tl, 512,
                         KT_d.rearrange("h d t -> d h t")[:, :, g * 512:(g + 1) * 512],
                         KrT_d[:, g * 512:(g + 1) * 512],
                         V_d.rearrange("h p t c -> p h t c")[:, :, g * 4:(g + 1) * 4, :], hooks=hooks)
            if g + 1 < NB:
                tl = a_zkv(g + 1)
        P.barrier()
        A.reset(PERSIST)

        gn1 = A.tile([D], F32)
        G1, S1 = make_front_rows(g_norm1, 0, D, False, gn1)
        fb = FrontBufs()
        xts = [A.tile([D], F32) for _ in range(2)]
        hTs = [A.tile([16, 512], BF16) for _ in range(2)]
        xi = 0
        for (bi, row0, T, nt, R, is_s) in blocks():
            if is_s:
                reload_front_rows(G1, S1, g_norm1, 0, D, gn1)
            hT = hTs[bi % 2]
            for t in range(nt):
                xt = xts[xi % 2]
                xi += 1
                P.dma(sp, xt.t[0:R, :], xo[row0 + t * 128:row0 + t * 128 + R, :], writes=[xt])
                front_tile(fb, xt, R, G1, S1, hT, t * 128)
            P.dma(pool, hT_d[:, :, row0:row0 + T], hT.t[:, :, 0:T], reads=[hT])
        P.barrier()
        A.reset(PERSIST)

        def fm_stage(c_base, func, dst_d):
            W = A.tile([16, D], BF16)
            P.dma(pool, W.t[:, :, 0:1024], wview(w_in, c_base, c_base + 1024), writes=[W])
            P.dma(pool, W.t[:, :, 1024:2048], wview(w_in, c_base + 1024, c_base + 2048), writes=[W])
            hTs = [A.tile([16, 512], BF16) for _ in range(2)]
            outs = [A.tile([16, 512], BF16) for _ in range(2)]
            for (bi, row0, T, nt, R, is_s) in blocks():
                hT, o = hTs[bi % 2], outs[bi % 2]
                P.dma(sp, hT.t[:, :, 0:T], hT_d[:, :, row0:row0 + T], writes=[hT])
                for j in range(16):
                    ps = psrot.get()
                    P.op(pe, [lambda e, k=k, j=j, ps=ps, hT=hT: e.matmul(ps.t[:, 0:T], W.t[:, k, j * 128:(j + 1) * 128], hT.t[:, k, 0:T], start=(k == 0), stop=(k == 15)) for k in range(16)],
                         reads=[W, hT], writes=[ps])
                    P.op(act, lambda e, ps=ps, o=o, j=j: e.activation(out=o.t[:, j, 0:T], in_=ps.t[:, 0:T], func=func), reads=[ps], writes=[o])
                P.dma(pool, dst_d[:, :, row0:row0 + T], o.t[:, :, 0:T], reads=[o])
            P.barrier()
            A.reset(PERSIST)

        fm_stage(0, AF.Gelu, uT_d)
        fm_stage(OFF_GATE, AF.Sigmoid, gaT_d)
        fm_stage(OFF_GATE + D, AF.Sigmoid, gbT_d)

        W = A.tile([16, D], BF16)
        P.dma(pool, W.t[:, :, 0:1024], wview(w_in, D, D + 1024), writes=[W])
        P.dma(pool, W.t[:, :, 1024:2048], wview(w_in, D + 1024, 2 * D), writes=[W])
        gsg = A.tile([D], F32)
        P.dma(sp, gsg.t, g_sg[0:1, :].partition_broadcast(128), writes=[gsg])
        hTs = [A.tile([16, 512], BF16) for _ in range(2)]
        vfs = [A.tile([D], F32) for _ in range(2)]
        vbs = [A.tile([D], BF16) for _ in range(2)]
        vjunk = A.tile([D], BF16, dsem=False)
        vss = [A.tile([1], F32, dsem=False) for _ in range(2)]
        vrs = [A.tile([1], F32, dsem=False) for _ in range(2)]
        vi = 0
        for (bi, row0, T, nt, R, is_s) in blocks():
            hT = hTs[bi % 2]
            P.dma(sp, hT.t[:, :, 0:T], hT_d[:, :, row0:row0 + T], writes=[hT])
            for t in range(nt):
                vf, vb, ss, rs = vfs[vi % 2], vbs[vi % 2], vss[vi % 2], vrs[vi % 2]
                vi += 1
                for c4 in range(4):
                    ps = psrot.get()
                    P.op(pe, [lambda e, k=k, ps=ps, hT=hT, t=t, c4=c4: e.matmul(ps.t[0:R, :], hT.t[:, k, t * 128:t * 128 + R], W.t[:, k, c4 * 512:(c4 + 1) * 512], start=(k == 0), stop=(k == 15)) for k in range(16)],
                         reads=[W, hT], writes=[ps])
                    P.op(act, lambda e, ps=ps, vf=vf, c4=c4: e.activation(out=vf.t[0:R, c4 * 512:(c4 + 1) * 512], in_=ps.t[0:R, :], func=AF.Gelu), reads=[ps], writes=[vf])
                P.op(dve, lambda e, ss=ss: e.memset(ss.t[0:R, :], 0.0), writes=[ss])
                P.op(act, lambda e, vf=vf, ss=ss: e.activation(out=vjunk.t[0:R, :], in_=vf.t[0:R, :], func=AF.Square, accum_out=ss.t[0:R, :]), reads=[vf, ss], writes=[vjunk, ss])
                rstd_col(ss, rs, R, D)
                if is_s:
                    P.op(dve, lambda e, vf=vf, rs=rs: e.scalar_tensor_tensor(out=vf.t[0:R, :], in0=vf.t[0:R, :], scalar=rs.t[0:R, :], in1=gsg.t[0:R, :], op0=ALU.mult, op1=ALU.mult),
                         reads=[vf, rs, gsg], writes=[vf])
                    P.dma(pool, v_all[:, :], vf.t[0:R, :], reads=[vf])
                    P.op(dve, lambda e, vf=vf, vb=vb: e.tensor_copy(out=vb.t[0:R, :], in_=vf.t[0:R, :]), reads=[vf], writes=[vb])
                else:
                    P.op(dve, lambda e, vf=vf, rs=rs, vb=vb: e.scalar_tensor_tensor(out=vb.t[0:R, :], in0=vf.t[0:R, :], scalar=rs.t[0:R, :], in1=gsg.t[0:R, :], op0=ALU.mult, op1=ALU.mult),
                         reads=[vf, rs, gsg], writes=[vb])
                P.dma(pool, v_d[row0 + t * 128:row0 + t * 128 + R, :], vb.t[0:R, :], reads=[vb])
        P.barrier()
        A.reset(PERSIST)

        Wq = A.tile([16, QL], BF16)
        P.dma(pool, Wq.t, wview(w_in, OFF_Q, OFF_KV), writes=[Wq])
        Wkv = A.tile([16, KVL + ROPE], BF16)
        P.dma(pool, Wkv.t, wview(w_in, OFF_KV, OFF_GATE), writes=[Wkv])
        Wuq = A.tile([4, NH * QKH], BF16)
        P.dma(pool, Wuq.t, wview(w_uq, 0, NH * QKH), writes=[Wuq])
        Wr2 = A.tile([4, NH, 128], BF16, dsem=False)
        wq3 = Wuq.t.rearrange("p k (h c) -> p k h c", h=NH, c=QKH)
        for kc in range(4):
            P.op(dve, [lambda e, kc=kc: e.tensor_copy(out=Wr2.t[:, kc, :, 0:64], in_=wq3[:, kc, :, 128:192]),
                       lambda e, kc=kc: e.tensor_copy(out=Wr2.t[:, kc, :, 64:96], in_=wq3[:, kc, :, 160:192]),
                       lambda e, kc=kc: e.tensor_copy(out=Wr2.t[:, kc, :, 96:128], in_=wq3[:, kc, :, 128:160])],
                 reads=[Wuq], writes=[Wr2])
        gqa = A.tile([4], F32)
        P.dma(sp, gqa.t, g_q_a.rearrange("o (k p) -> p (o k)", p=128), writes=[gqa])
        gqn = A.tile([1], F32)
        P.dma(sp, gqn.t, g_q_nope.rearrange("o d -> d o"), writes=[gqn])
        gq2 = A.tile([1], F32)
        gqr_col = g_q_rope.rearrange("o d -> d o")
        P.dma(sp, gq2.t[0:64, :], gqr_col[0:64, :], writes=[gq2])
        P.dma(sp, gq2.t[64:96, :], gqr_col[32:64, :], writes=[gq2])
        P.dma(sp, gq2.t[96:128, :], gqr_col[0:32, :], writes=[gq2])
        P.op(dve, lambda e: e.tensor_scalar(out=gqn.t, in0=gqn.t, scalar1=SCALE, scalar2=0.0, op0=ALU.mult, op1=ALU.add), reads=[gqn], writes=[gqn])
        P.op(dve, lambda e: e.tensor_scalar(out=gq2.t, in0=gq2.t, scalar1=SCALE, scalar2=0.0, op0=ALU.mult, op1=ALU.add), reads=[gq2], writes=[gq2])
        kb = KvBufs()
        hTs = [A.tile([16, 512], BF16) for _ in range(2)]
        sqs = [A.tile([512], BF16, dsem=False) for _ in range(2)]
        rss = [A.tile([512], F32, dsem=False) for _ in range(2)]
        tmps = [A.tile([512], F32, dsem=False) for _ in range(2)]
        sqs4 = [A.tile([512], BF16, dsem=False) for _ in range(4)]
        rss4 = [A.tile([512], F32, dsem=False) for _ in range(4)]
        qaT = A.tile([4, 512], BF16, dsem=False)
        QTs = [A.tile([NH, 512], BF16) for _ in range(1)]
        QrTs = [A.tile([NH, 512], BF16) for _ in range(1)]
        tws = [A.tile([512], F32) for _ in range(2)]
        cko = [A.tile([KVL], F32) for _ in range(2)]
        kro = [A.tile([ROPE], F32) for _ in range(2)]
        oi = 0
        qrot = Rot(PS[0:4])
        orot = Rot(PS[4:8])
        for (bi, row0, T, nt, R, is_s) in blocks():
            hT, QT, QrT, tw = hTs[bi % 2], QTs[0], QrTs[0], tws[bi % 2]
            P.dma(sp, hT.t[:, :, 0:T], hT_d[:, :, row0:row0 + T], writes=[hT])
            P.dma(sp, tw.t[:, 0:T], town[:, row0:row0 + T], writes=[tw])
            for t in range(nt):
                ck, kr_ = cko[oi % 2], kro[oi % 2]
                oi += 1
                r0 = row0 + t * 128
                zkv_tile(kb, Wkv, hT, t * 128, R, ropeo[r0:r0 + R, :], ck, kr_)
                P.dma(pool, lat_all[r0:r0 + R, :], ck.t[0:R, :], reads=[ck])
                P.dma(pool, kr_all[r0:r0 + R, :], kr_.t[0:R, :], reads=[kr_])
            pq = [qrot.get() for _ in range(4)]
            pss = orot.get()
            for kc in range(4):
                P.op(pe, [lambda e, k=k, kc=kc: e.matmul(pq[kc].t[:, 0:T], Wq.t[:, k, kc * 128:(kc + 1) * 128], hT.t[:, k, 0:T], start=(k == 0), stop=(k == 15)) for k in range(16)],
                     reads=[Wq, hT], writes=[pq[kc]])
            for kc in range(4):
                sq = sqs[kc % 2]
                P.op(act, lambda e, kc=kc, sq=sq: e.activation(out=sq.t[:, 0:T], in_=pq[kc].t[:, 0:T], func=AF.Square), reads=[pq[kc]], writes=[sq])
                if kc == 0:
                    P.deps(pe, [], [pss])
                P.op(pe, lambda e, kc=kc, sq=sq: e.matmul(pss.t[:, 0:T], ones_b.t, sq.t[:, 0:T], start=(kc == 0), stop=(kc == 3)), reads=[ones_b, sq], writes=[] if kc < 3 else [pss])
            rs = rss[0]
            rstd_tile(pss, rs, T, QL)
            for kc in range(4):
                P.op(dve, lambda e, kc=kc: e.scalar_tensor_tensor(out=qaT.t[:, kc, 0:T], in0=pq[kc].t[:, 0:T], scalar=gqa.t[:, kc:kc + 1], in1=rs.t[:, 0:T], op0=ALU.mult, op1=ALU.mult),
                     reads=[pq[kc], gqa, rs], writes=[qaT])
            def q_mm(h):
                pa = qrot.get()
                pab = qrot.get()
                P.op(pe, [lambda e, kc=kc: e.matmul(pa.t[:, 0:T], Wuq.t[:, kc, h * QKH:h * QKH + 128], qaT.t[:, kc, 0:T], start=(kc == 0), stop=(kc == 3)) for kc in range(4)],
                     reads=[Wuq, qaT], writes=[pa])
                P.op(pe, [lambda e, kc=kc: e.matmul(pab.t[:, 0:T], Wr2.t[:, kc, h, :], qaT.t[:, kc, 0:T], start=(kc == 0), stop=(kc == 3)) for kc in range(4)],
                     reads=[Wr2, qaT], writes=[pab])
                return pa, pab

            def q_rest(h, pa, pab):
                sq, sq2, rs1, rs2, tmp = sqs4[(h % 2) * 2], sqs4[(h % 2) * 2 + 1], rss4[(h % 2) * 2], rss4[(h % 2) * 2 + 1], tmps[h % 2]
                P.op(act, lambda e: e.activation(out=sq.t[:, 0:T], in_=pa.t[:, 0:T], func=AF.Square), reads=[pa], writes=[sq])
                P.op(act, lambda e: e.activation(out=sq2.t[0:64, 0:T], in_=pab.t[0:64, 0:T], func=AF.Square), reads=[pab], writes=[sq2])
                ps1 = orot.get()
                ps2 = orot.get()
                P.op(pe, lambda e: e.matmul(ps1.t[:, 0:T], ones_b.t, sq.t[:, 0:T], start=True, stop=True), reads=[ones_b, sq], writes=[ps1])
                P.op(pe, lambda e: e.matmul(ps2.t[:, 0:T], ones_b.t[0:64, :], sq2.t[0:64, 0:T], start=True, stop=True), reads=[ones_b, sq2], writes=[ps2])
                rstd_tile(ps1, rs1, T, 128)
                rstd_tile(ps2, rs2, T, ROPE)
                P.op(dve, lambda e: e.scalar_tensor_tensor(out=QT.t[:, h, 0:T], in0=pa.t[:, 0:T], scalar=gqn.t[:, 0:1], in1=rs1.t[:, 0:T], op0=ALU.mult, op1=ALU.mult),
                     reads=[pa, gqn, rs1], writes=[QT])
                P.op(dve, lambda e: e.scalar_tensor_tensor(out=tmp.t[:, 0:T], in0=pab.t[:, 0:T], scalar=gq2.t[:, 0:1], in1=rs2.t[:, 0:T], op0=ALU.mult, op1=ALU.mult),
                     reads=[pab, gq2, rs2], writes=[tmp])
                P.op(dve, lambda e: e.tensor_tensor(out=QrT.t[:, h, 0:T], in0=tmp.t[:, 0:T], in1=tw.t[:, 0:T], op=ALU.mult),
                     reads=[tmp, tw], writes=[QrT])

            pend = {}
            for i in range(NH + 1):
                if i < NH:
                    pend[i] = q_mm(i)
                if i - 1 >= 0:
                    q_rest(i - 1, *pend.pop(i - 1))
            P.dma(pool, QT_d.rearrange("h d t -> d h t")[:, :, row0:row0 + T], QT.t[:, :, 0:T], reads=[QT])
            P.dma(pool, QrT_d.rearrange("h d t -> d h t")[:, :, row0:row0 + T], QrT.t[:, :, 0:T], reads=[QrT])
        P.barrier()
        A.reset(PERSIST)

        eb = ExpBufs(512)
        cks = [A.tile([KVL], F32) for _ in range(8)]
        krs_ = [A.tile([ROPE], F32) for _ in range(8)]
        gi = 0
        for i in range(SB):
            for t0 in range(0, NKT, 4):
                ntl = min(4, NKT - t0)
                tl = []
                for t in range(ntl):
                    ck, kr_ = cks[(gi % 2) * 4 + t], krs_[(gi % 2) * 4 + t]
                    r0 = (t0 + t) * 128
                    P.dma(sp, ck.t, clat[i, r0:r0 + 128, :], writes=[ck])
                    P.dma(sp, kr_.t, ckr[i, r0:r0 + 128, :], writes=[kr_])
                    tl.append((ck, kr_, 128))
                gi += 1
                T = ntl * 128
                expand_group(eb, tl, T,
                             KTs_d[i].rearrange("h d t -> d h t")[:, :, t0 * 128:t0 * 128 + T],
                             KrTs_d[i][:, t0 * 128:t0 * 128 + T],
                             Vs_d[i].rearrange("h p t c -> p h t c")[:, :, t0:t0 + ntl, :])
            ck, kr_ = cks[(gi % 2) * 4], krs_[(gi % 2) * 4]
            gi += 1
            r0 = NOWN + i * TS
            P.dma(sp, ck.t[0:TS, :], lat_all[r0:r0 + TS, :], writes=[ck])
            P.dma(sp, kr_.t[0:TS, :], kr_all[r0:r0 + TS, :], writes=[kr_])
            expand_group(eb, [(ck, kr_, TS)], TS,
                         KTs_d[i].rearrange("h d t -> d h t")[:, :, PAST:PASTP],
                         KrTs_d[i][:, PAST:PASTP],
                         Vs_d[i].rearrange("h p t c -> p h t c")[0:TS, :, NKT:NKT + 1, :])
        P.barrier()
        A.reset(PERSIST)

        wsT = A.tile([NH, 128], BF16, dsem=False)
        wsblk = A.tile([NH, NS], BF16)
        bsb = A.tile([NH, 128], F32)
        bss = A.tile([NH, NS], F32, dsem=False)
        trl = A.tile([128], F32)
        wsf = [A.tile([128], F32) for _ in range(2)]
        wsm = [A.tile([128], BF16, dsem=False) for _ in range(2)]
        P.dma(sp, trl.t, trild[:, :], writes=[trl])
        P.dma(sp, bsb.t.rearrange("p g i -> p (g i)"), b_s[0:1, :].partition_broadcast(128), writes=[bsb])
        for g in range(NH):
            wf, wm = wsf[g % 2], wsm[g % 2]
            P.dma(sp, wf.t, w_s[g, :, :], writes=[wf])
            P.op(dve, lambda e, wf=wf, wm=wm: e.tensor_tensor(out=wm.t, in0=wf.t, in1=trl.t, op=ALU.mult), reads=[wf, trl], writes=[wm])
            for k0, kk, ps, pv in transpose_to(None, wm, 128, 1, [], None):
                P.op(dve, lambda e, pv=pv, g=g: e.tensor_copy(out=wsT.t[:, g, :], in_=pv[:, 0:128]), reads=[ps], writes=[wsT])
        P.op(dve, lambda e: e.memset(wsblk.t, 0.0), writes=[wsblk])
        for i in range(SB):
            P.dma(sp, wsblk.t[16 * i:16 * i + 16, :, 16 * i:16 * i + 16], wsT.t[0:16, :, 0:16], reads=[wsT], writes=[wsblk])
            P.op(dve, lambda e, i=i: e.tensor_copy(out=bss.t[:, :, 16 * i:16 * i + 16], in_=bsb.t[:, :, 0:16]), reads=[bsb], writes=[bss])
        uTs = [A.tile([16, 512], BF16) for _ in range(2)]
        vts = [A.tile([4, D], BF16) for _ in range(2)]
        osg = [A.tile([16, 512], BF16) for _ in range(2)]
        tmpf = [A.tile([512], F32, dsem=False) for _ in range(2)]
        for (bi, row0, T, nt, R, is_s) in blocks():
            uT, vt, o = uTs[bi % 2], vts[bi % 2], osg[bi % 2]
            P.dma(sp, uT.t[:, :, 0:T], uT_d[:, :, row0:row0 + T], writes=[uT])
            for t in range(nt):
                P.dma(sp, vt.t[0:R, t, :], v_d[row0 + t * 128:row0 + t * 128 + R, :], writes=[vt])
            for g in range(NH):
                ps = psrot.get()
                tm = tmpf[g % 2]
                if not is_s:
                    P.op(pe, [lambda e, t=t, g=g, ps=ps, vt=vt: e.matmul(ps.t[:, t * 128:(t + 1) * 128], vt.t[:, t, g * 128:(g + 1) * 128], wsT.t[:, g, :], start=True, stop=True) for t in range(4)],
                         reads=[vt, wsT], writes=[ps])
                    P.op(dve, lambda e, ps=ps, tm=tm, g=g: e.tensor_tensor(out=tm.t.rearrange("p (t i) -> p t i", t=4, i=128), in0=ps.t.rearrange("p (t i) -> p t i", t=4, i=128),
                                                                     in1=bsb.t[:, g, :].unsqueeze(1).to_broadcast([128, 4, 128]), op=ALU.add), reads=[ps, bsb], writes=[tm])
                else:
                    P.op(pe, lambda e, g=g, ps=ps, vt=vt: e.matmul(ps.t[:, 0:NS], vt.t[0:NS, 0, g * 128:(g + 1) * 128], wsblk.t[0:NS, g, :], start=True, stop=True),
                         reads=[vt, wsblk], writes=[ps])
                    P.op(dve, lambda e, ps=ps, tm=tm, g=g: e.tensor_tensor(out=tm.t[:, 0:NS], in0=ps.t[:, 0:NS], in1=bss.t[:, g, :], op=ALU.add), reads=[ps, bss], writes=[tm])
                P.op(dve, lambda e, tm=tm, g=g, uT=uT, o=o: e.tensor_tensor(out=o.t[:, g, 0:T], in0=tm.t[:, 0:T], in1=uT.t[:, g, 0:T], op=ALU.mult), reads=[tm, uT], writes=[o])
            P.dma(pool, osgT_d[:, :, row0:row0 + T], o.t[:, :, 0:T], reads=[o])
        P.barrier()
        A.reset(PERSIST)

        msk = A.tile([16, 8], F32)
        mskb = A.tile([16, 8], BF16, dsem=False)
        P.dma(sp, msk.t.rearrange("p a b -> p (a b)"), maskc[:, :], writes=[msk])
        P.op(dve, lambda e: e.tensor_copy(out=mskb.t, in_=msk.t), reads=[msk], writes=[mskb])
        CH = 8
        Kc = [A.tile([CH * 128], BF16) for _ in range(4)]
        Krc = [A.tile([CH * 128], BF16) for _ in range(4)]
        Vc = [A.tile([CH, 128], BF16) for _ in range(4)]
        Qs = [A.tile([512], BF16) for _ in range(2)]
        Qrs = [A.tile([512], BF16) for _ in range(2)]
        pTs = [A.tile([512], BF16, dsem=False) for _ in range(4)]
        ptmp = [A.tile([512], BF16, dsem=False) for _ in range(4)]
        acc = [A.tile([512], F32, dsem=False) for _ in range(4)]
        rinv = [A.tile([512], F32, dsem=False) for _ in range(2)]
        omla = [A.tile([NH, 512], BF16) for _ in range(2)]
        srot = Rot(PS[0:5])
        orot2 = Rot(PS[5:7])
        rrot = Rot(PS[7:8])
        ci = 0
        hi = 0
        pi = 0

        LOOK = 2
        jobs = []
        for (bi, row0, T, nt, R, is_s) in blocks():
            if not is_s:
                ctx = 4 * (4 * bi + 4)
                for h in range(NH):
                    jobs.append(dict(bi=bi, row0=row0, T=T, QT=QT_d[h], QrT=QrT_d[h], q0=row0, NQ=512, ctx=ctx, last=128,
                                     KT=KT_d[h], KrT=KrT_d, V=V_d[h], h=h, ocol=0, tail=ctx - 16, store=(h == NH - 1)))
            else:
                for i in range(SB):
                    for h in range(NH):
                        jobs.append(dict(bi=bi, row0=row0, T=T, QT=QT_d[h], QrT=QrT_d[h], q0=row0 + i * TS, NQ=TS, ctx=NKT + 1, last=TS,
                                         KT=KTs_d[i, h], KrT=KrTs_d[i], V=Vs_d[i, h], h=h, ocol=i * TS, tail=None,
                                         store=(i == SB - 1 and h == NH - 1)))
        chunks = []
        tiles = []
        for ji, jb in enumerate(jobs):
            for c0 in range(0, jb["ctx"], CH):
                nt_ = min(CH, jb["ctx"] - c0)
                chunks.append(dict(job=ji, c0=c0, nt=nt_, first=(c0 == 0)))
                for t in range(nt_):
                    kt = c0 + t
                    tiles.append(dict(job=ji, chunk=len(chunks) - 1, t=t, kt=kt, RK=(jb["last"] if kt == jb["ctx"] - 1 else 128),
                                      firstc=(t == 0), first=(kt == 0), last=(kt == jb["ctx"] - 1)))

        def prefetch(cj):
            if cj >= len(chunks):
                return
            ch = chunks[cj]
            jb = jobs[ch["job"]]
            if ch["first"]:
                jb["Q"], jb["Qr"] = Qs[ch["job"] % 2], Qrs[ch["job"] % 2]
                jb["acs"] = (acc[(ch["job"] % 2) * 2], acc[(ch["job"] % 2) * 2 + 1])
                jb["ri"] = rinv[ch["job"] % 2]
                jb["nacc"] = [0, 0]
                jb["om"] = omla[jb["bi"] % 2]
                P.dma(sp, jb["Q"].t[:, 0:jb["NQ"]], jb["QT"][:, jb["q0"]:jb["q0"] + jb["NQ"]], writes=[jb["Q"]])
                P.dma(sp, jb["Qr"].t[:, 0:jb["NQ"]], jb["QrT"][:, jb["q0"]:jb["q0"] + jb["NQ"]], writes=[jb["Qr"]])
            kc_, krc_, vc_ = Kc[cj % 4], Krc[cj % 4], Vc[cj % 4]
            ch["bufs"] = (kc_, krc_, vc_)
            c0, nt_ = ch["c0"], ch["nt"]
            lastc = (c0 + nt_ == jb["ctx"])
            nkeys = (nt_ - 1) * 128 + (jb["last"] if lastc else 128)
            P.dma(sp, kc_.t[:, 0:nkeys], jb["KT"][:, c0 * 128:c0 * 128 + nkeys], writes=[kc_])
            P.dma(sp, krc_.t[:, 0:nkeys], jb["KrT"][:, c0 * 128:c0 * 128 + nkeys], writes=[krc_])
            if lastc and jb["last"] < 128:
                if nt_ > 1:
                    P.dma(sp, vc_.t[:, 0:nt_ - 1, :], jb["V"][:, c0:c0 + nt_ - 1, :], writes=[vc_])
                P.dma(sp, vc_.t[0:jb["last"], nt_ - 1:nt_, :], jb["V"][0:jb["last"], c0 + nt_ - 1:c0 + nt_, :], writes=[vc_])
            else:
                P.dma(sp, vc_.t[:, 0:nt_, :], jb["V"][:, c0:c0 + nt_, :], writes=[vc_])

        def emit_S(ti_):
            tl = tiles[ti_]
            jb = jobs[tl["job"]]
            if tl["firstc"]:
                if tl["chunk"] == 0:
                    prefetch(0)
                prefetch(tl["chunk"] + 1)
            kc_, krc_, vc_ = chunks[tl["chunk"]]["bufs"]
            NQ, RK, t = jb["NQ"], tl["RK"], tl["t"]
            pss_ = srot.get()
            tl["pss"] = pss_
            Q, Qr = jb["Q"], jb["Qr"]
            P.op(pe, [lambda e: e.matmul(pss_.t[0:RK, 0:NQ], kc_.t[:, t * 128:t * 128 + RK], Q.t[:, 0:NQ], start=True, stop=False),
                      lambda e: e.matmul(pss_.t[0:RK, 0:NQ], krc_.t[:, t * 128:t * 128 + RK], Qr.t[:, 0:NQ], start=False, stop=True)],
                 reads=[kc_, krc_, Q, Qr], writes=[pss_])

        pcount = [0]

        def emit_rest(ti_):
            tl = tiles[ti_]
            jb = jobs[tl["job"]]
            kc_, krc_, vc_ = chunks[tl["chunk"]]["bufs"]
            NQ, RK, t, kt = jb["NQ"], tl["RK"], tl["t"], tl["kt"]
            pss_ = tl["pss"]
            pT = pTs[pcount[0] % 4]
            pcount[0] += 1
            P.op(act, lambda e: e.activation(out=pT.t[0:RK, 0:NQ], in_=pss_.t[0:RK, 0:NQ], func=AF.Exp), reads=[pss_], writes=[pT])
            if jb["tail"] is not None and kt >= jb["tail"]:
                kbt = kt - jb["tail"]
                P.op(dve, lambda e: e.tensor_tensor(out=pT.t.rearrange("p (a b) -> p a b", a=8, b=64), in0=pT.t.rearrange("p (a b) -> p a b", a=8, b=64),
                                                    in1=mskb.t[:, kbt, :].unsqueeze(2).to_broadcast([128, 8, 64]), op=ALU.mult), reads=[pT, mskb], writes=[pT])
            if tl["first"]:
                jb["pso"] = orot2.get()
                P.deps(pe, [], [jb["pso"]])
            pso = jb["pso"]
            first, last = tl["first"], tl["last"]
            P.op(pe, lambda e: e.matmul(pso.t[:, 0:NQ], vc_.t[0:RK, t, :], pT.t[0:RK, 0:NQ], start=first, stop=last),
                 reads=[vc_, pT], writes=[pso] if last else [])
            if tl["first"]:
                for ac0 in jb["acs"]:
                    P.op(dve, lambda e, ac0=ac0: e.memset(ac0.t[:, 0:NQ], 0.0), writes=[ac0])

            def acc_add(src, pidx):
                ac = jb["acs"][pidx % 2]
                rr = 128 if src is not pT else RK
                P.op(dve, lambda e: e.tensor_tensor(out=ac.t[0:rr, 0:NQ], in0=ac.t[0:rr, 0:NQ], in1=src.t[0:rr, 0:NQ], op=ALU.add), reads=[src, ac], writes=[ac])
                jb["nacc"][pidx % 2] += 1

            if kt % 2 == 0:
                if last or RK < 128:
                    acc_add(pT, kt // 2)
                else:
                    jb["prevpT"] = pT
            else:
                if RK < 128:
                    acc_add(jb["prevpT"], kt // 2)
                    acc_add(pT, kt // 2 + 1)
                else:
                    pp = jb["prevpT"]
                    tb = ptmp[(pcount[0] // 2) % 4]
                    P.op(dve, lambda e: e.tensor_tensor(out=tb.t[:, 0:NQ], in0=pp.t[:, 0:NQ], in1=pT.t[:, 0:NQ], op=ALU.add), reads=[pp, pT], writes=[tb])
                    acc_add(tb, kt // 2)
            if last:
                acs, ri, om, h, ocol = jb["acs"], jb["ri"], jb["om"], jb["h"], jb["ocol"]
                psr_ = rrot.get()
                P.op(pe, [lambda e: e.matmul(psr_.t[:, 0:NQ], ones_f.t, acs[0].t[:, 0:NQ], start=True, stop=False),
                          lambda e: e.matmul(psr_.t[:, 0:NQ], ones_f.t, acs[1].t[:, 0:NQ], start=False, stop=True)], reads=[ones_f, acs[0], acs[1]], writes=[psr_])
                P.op(act, lambda e: e.activation(out=ri.t[:, 0:NQ], in_=psr_.t[:, 0:NQ], func=AF.Ln), reads=[psr_], writes=[ri])
                P.op(act, lambda e: e.activation(out=ri.t[:, 0:NQ], in_=ri.t[:, 0:NQ], func=AF.Exp, scale=-1.0), reads=[ri], writes=[ri])
                P.op(dve, lambda e: e.tensor_tensor(out=om.t[:, h, ocol:ocol + NQ], in0=pso.t[:, 0:NQ], in1=ri.t[:, 0:NQ], op=ALU.mult), reads=[pso, ri], writes=[om])
                if jb["store"]:
                    P.dma(pool, omlaT_d[:, :, jb["row0"]:jb["row0"] + jb["T"]], om.t[:, :, 0:jb["T"]], reads=[om])

        NTL = len(tiles)
        for i in range(NTL + LOOK):
            if i < NTL:
                emit_S(i)
            if i - LOOK >= 0:
                emit_rest(i - LOOK)
        P.barrier()
        A.reset(PERSIST)

        for grp in range(2):
            Wa = A.tile([16, 1024], BF16)
            Wb = A.tile([16, 1024], BF16)
            P.dma(pool, Wa.t, wview(w_pa, grp * 1024, (grp + 1) * 1024), writes=[Wa])
            P.dma(pool, Wb.t, wview(w_pb, grp * 1024, (grp + 1) * 1024), writes=[Wb])
            ia = [A.tile([16, 512], BF16) for _ in range(2)]
            ib = [A.tile([16, 512], BF16) for _ in range(2)]
            ga = [A.tile([8, 512], BF16) for _ in range(2)]
            gb = [A.tile([8, 512], BF16) for _ in range(2)]
            mo = [A.tile([8, 512], BF16) for _ in range(2)]
            t1 = [A.tile([512], F32, dsem=False) for _ in range(2)]
            t2 = [A.tile([512], F32, dsem=False) for _ in range(2)]
            for (bi, row0, T, nt, R, is_s) in blocks():
                a_, b_, ga_, gb_, mo_ = ia[bi % 2], ib[bi % 2], ga[bi % 2], gb[bi % 2], mo[bi % 2]
                P.dma(sp, a_.t[:, :, 0:T], osgT_d[:, :, row0:row0 + T], writes=[a_])
                P.dma(sp, b_.t[:, :, 0:T], omlaT_d[:, :, row0:row0 + T], writes=[b_])
                P.dma(sp, ga_.t[:, :, 0:T], gaT_d[:, grp * 8:(grp + 1) * 8, row0:row0 + T], writes=[ga_])
                P.dma(sp, gb_.t[:, :, 0:T], gbT_d[:, grp * 8:(grp + 1) * 8, row0:row0 + T], writes=[gb_])
                for j in range(8):
                    pa = psrot.get()
                    pb = psrot.get()
                    P.op(pe, [lambda e, k=k, j=j, pa=pa, a_=a_: e.matmul(pa.t[:, 0:T], Wa.t[:, k, j * 128:(j + 1) * 128], a_.t[:, k, 0:T], start=(k == 0), stop=(k == 15)) for k in range(16)],
                         reads=[Wa, a_], writes=[pa])
                    P.op(pe, [lambda e, k=k, j=j, pb=pb, b_=b_: e.matmul(pb.t[:, 0:T], Wb.t[:, k, j * 128:(j + 1) * 128], b_.t[:, k, 0:T], start=(k == 0), stop=(k == 15)) for k in range(16)],
                         reads=[Wb, b_], writes=[pb])
                    x1_, x2_ = t1[j % 2], t2[j % 2]
                    P.op(dve, lambda e, pa=pa, x1_=x1_, ga_=ga_, j=j: e.tensor_tensor(out=x1_.t[:, 0:T], in0=pa.t[:, 0:T], in1=ga_.t[:, j, 0:T], op=ALU.mult), reads=[pa, ga_], writes=[x1_])
                    P.op(dve, lambda e, pb=pb, x2_=x2_, gb_=gb_, j=j: e.tensor_tensor(out=x2_.t[:, 0:T], in0=pb.t[:, 0:T], in1=gb_.t[:, j, 0:T], op=ALU.mult), reads=[pb, gb_], writes=[x2_])
                    P.op(dve, lambda e, x1_=x1_, x2_=x2_, mo_=mo_, j=j: e.tensor_tensor(out=mo_.t[:, j, 0:T], in0=x1_.t[:, 0:T], in1=x2_.t[:, 0:T], op=ALU.add), reads=[x1_, x2_], writes=[mo_])
                P.dma(pool, mT_d[:, grp * 8:(grp + 1) * 8, row0:row0 + T], mo_.t[:, :, 0:T], reads=[mo_])
            P.barrier()
            A.reset(PERSIST)

        Wo = A.tile([16, D], BF16)
        P.dma(pool, Wo.t[:, :, 0:1024], wview(w_o, 0, 1024), writes=[Wo])
        P.dma(pool, Wo.t[:, :, 1024:2048], wview(w_o, 1024, 2048), writes=[Wo])
        gn2 = A.tile([D], F32)
        G2, S2 = make_front_rows(g_norm2, 3 * D, 4 * D, False, gn2)
        g1r = A.tile([D], F32)
        load_rows_prompt_or_sample(g1r, mod_d[0:1, 2 * D:3 * D], None, False, D)
        fb = FrontBufs()
        mTs = [A.tile([16, 512], BF16) for _ in range(1)]
        xts = [A.tile([D], F32) for _ in range(2)]
        h2s = [A.tile([16, 512], BF16) for _ in range(1)]
        tm5 = [A.tile([512], F32, dsem=False) for _ in range(2)]
        xi = 0
        for (bi, row0, T, nt, R, is_s) in blocks():
            if is_s:
                reload_front_rows(G2, S2, g_norm2, 3 * D, 4 * D, gn2)
                load_rows_prompt_or_sample(g1r, None, [mod_d[1 + i:2 + i, 2 * D:3 * D] for i in range(SB)], True, D)
            mT, h2 = mTs[0], h2s[0]
            P.dma(sp, mT.t[:, :, 0:T], mT_d[:, :, row0:row0 + T], writes=[mT])
            for t in range(nt):
                xt = xts[xi % 2]
                xi += 1
                r0 = row0 + t * 128
                P.dma(sp, xt.t[0:R, :], xo[r0:r0 + R, :], writes=[xt])
                for c4 in range(4):
                    ps = psrot.get()
                    tm = tm5[c4 % 2]
                    P.op(pe, [lambda e, k=k, ps=ps, mT=mT, t=t, c4=c4: e.matmul(ps.t[0:R, :], mT.t[:, k, t * 128:t * 128 + R], Wo.t[:, k, c4 * 512:(c4 + 1) * 512], start=(k == 0), stop=(k == 15)) for k in range(16)],
                         reads=[Wo, mT], writes=[ps])
                    P.op(dve, lambda e, ps=ps, tm=tm, c4=c4: e.tensor_tensor(out=tm.t[0:R, :], in0=ps.t[0:R, :], in1=g1r.t[0:R, c4 * 512:(c4 + 1) * 512], op=ALU.mult), reads=[ps, g1r], writes=[tm])
                    P.op(dve, lambda e, tm=tm, xt=xt, c4=c4: e.tensor_tensor(out=xt.t[0:R, c4 * 512:(c4 + 1) * 512], in0=xt.t[0:R, c4 * 512:(c4 + 1) * 512], in1=tm.t[0:R, :], op=ALU.add), reads=[tm, xt], writes=[xt])
                P.dma(pool, x1_d[r0:r0 + R, :], xt.t[0:R, :], reads=[xt])
                front_tile(fb, xt, R, G2, S2, h2, t * 128)
            P.dma(pool, h2T_d[:, :, row0:row0 + T], h2.t[:, :, 0:T], reads=[h2])
        P.barrier()
        A.reset(PERSIST)

        for grp in range(4):
            Wu = A.tile([16, D], BF16)
            P.dma(pool, Wu.t[:, :, 0:1024], wview(w_up, grp * D, grp * D + 1024), writes=[Wu])
            P.dma(pool, Wu.t[:, :, 1024:2048], wview(w_up, grp * D + 1024, (grp + 1) * D), writes=[Wu])
            h2s = [A.tile([16, 512], BF16) for _ in range(2)]
            hid = [A.tile([16, 512], BF16) for _ in range(2)]
            sqf = [A.tile([512], F32, dsem=False) for _ in range(2)]
            for (bi, row0, T, nt, R, is_s) in blocks():
                h2, hd = h2s[bi % 2], hid[bi % 2]
                P.dma(sp, h2.t[:, :, 0:T], h2T_d[:, :, row0:row0 + T], writes=[h2])
                for j in range(16):
                    ps = psrot.get()
                    sq = sqf[j % 2]
                    P.op(pe, [lambda e, k=k, j=j, ps=ps, h2=h2: e.matmul(ps.t[:, 0:T], Wu.t[:, k, j * 128:(j + 1) * 128], h2.t[:, k, 0:T], start=(k == 0), stop=(k == 15)) for k in range(16)],
                         reads=[Wu, h2], writes=[ps])
                    P.op(act, lambda e, ps=ps, sq=sq: e.activation(out=sq.t[:, 0:T], in_=ps.t[:, 0:T], func=AF.Square), reads=[ps], writes=[sq])
                    P.op(dve, lambda e, ps=ps, sq=sq, hd=hd, j=j: e.scalar_tensor_tensor(out=hd.t[:, j, 0:T], in0=ps.t[:, 0:T], scalar=0.0, in1=sq.t[:, 0:T], op0=ALU.is_gt, op1=ALU.mult),
                         reads=[ps, sq], writes=[hd])
                P.dma(pool, hidT_d[:, grp * 16:(grp + 1) * 16, row0:row0 + T], hd.t[:, :, 0:T], reads=[hd])
            P.barrier()
            A.reset(PERSIST)

        g2r = A.tile([D], F32)
        load_rows_prompt_or_sample(g2r, mod_d[0:1, 5 * D:6 * D], None, False, D)
        g2s = A.tile([D], F32)
        load_rows_prompt_or_sample(g2s, None, [mod_d[1 + i:2 + i, 5 * D:6 * D] for i in range(SB)], True, D)
        WBASE = A.off
        for grp in range(4):
            A.reset(WBASE)
            Wd = A.tile([64, 512], BF16)
            for q in range(4):
                P.dma(pool, Wd.t[:, q * 16:(q + 1) * 16, :], wview(w_down, grp * 512, (grp + 1) * 512)[:, q * 16:(q + 1) * 16, :], writes=[Wd])
            hts = [A.tile([64, 256], BF16) for _ in range(2)]
            x1s = [A.tile([512], F32) for _ in range(2)]
            tm7 = [A.tile([512], F32, dsem=False) for _ in range(2)]
            ti = 0
            hi7 = 0
            for (bi, row0, T, nt, R, is_s) in blocks():
                gr = g2s if is_s else g2r
                TH = 256 if not is_s else NS
                for half in range(T // TH):
                    ht = hts[hi7 % 2]
                    hi7 += 1
                    c0 = row0 + half * TH
                    for q in range(2):
                        P.dma(sp, ht.t[:, q * 32:(q + 1) * 32, 0:TH], hidT_d[:, q * 32:(q + 1) * 32, c0:c0 + TH], writes=[ht])
                    for t in range(TH // R):
                        x1t, tm = x1s[ti % 2], tm7[ti % 2]
                        ti += 1
                        r0 = c0 + t * R
                        P.dma(sp, x1t.t[0:R, :], x1_d[r0:r0 + R, grp * 512:(grp + 1) * 512], writes=[x1t])
                        ps = psrot.get()
                        P.op(pe, [lambda e, k=k, ps=ps, ht=ht, t=t: e.matmul(ps.t[0:R, :], ht.t[:, k, t * R:(t + 1) * R], Wd.t[:, k, :], start=(k == 0), stop=(k == 63)) for k in range(64)],
                             reads=[Wd, ht], writes=[ps])
                        P.op(dve, lambda e, ps=ps, tm=tm, gr=gr, grp=grp: e.tensor_tensor(out=tm.t[0:R, :], in0=ps.t[0:R, :], in1=gr.t[0:R, grp * 512:(grp + 1) * 512], op=ALU.mult), reads=[ps, gr], writes=[tm])
                        P.op(dve, lambda e, tm=tm, x1t=x1t: e.tensor_tensor(out=x1t.t[0:R, :], in0=x1t.t[0:R, :], in1=tm.t[0:R, :], op=ALU.add), reads=[tm, x1t], writes=[x1t])
                        P.dma(pool, y_all[r0:r0 + R, grp * 512:(grp + 1) * 512], x1t.t[0:R, :], reads=[x1t])
            P.barrier()
        P.barrier()

        with nc.allow_non_contiguous_dma(reason="small per-partition column loads"), nc.Block() as block:
            @block.tensor
            def _(e):
                P.replay(pe, e)

            @block.scalar
            def _(e):
                P.replay(act, e)

            @block.vector
            def _(e):
                P.replay(dve, e)

            @block.gpsimd
            def _(e):
                P.replay(pool, e)

            @block.sync
            def _(e):
                P.replay(sp, e)
    return nc


_CACHE = {}


def _rope_tables(pos):
    inv = (np.float32(10000.0) ** (-(np.arange(0, ROPE, 2, dtype=np.float32) / np.float32(ROPE)))).astype(np.float32)
    ang = (pos.astype(np.float32)[:, None] * inv[None, :]).astype(np.float32)
    return np.cos(ang).astype(np.float32), np.sin(ang).astype(np.float32)


def kernel(x_prompt, x_sample, cache_kv_latent, cache_k_rope, c_prompt, c_sample,
           w_ada, b_ada, g_norm1, g_norm2, w_in, g_sg, w_s, b_s,
           g_q_a, w_uq, g_q_nope, g_q_rope, g_kv_a, g_k_rope, w_uk, g_k_nope, w_uv,
           w_pa, w_pb, w_o, w_up, w_down):
    f = lambda a: np.ascontiguousarray(np.asarray(a, dtype=np.float32))
    x_prompt, x_sample = f(x_prompt), f(x_sample)
    B, SEQ, _ = x_prompt.shape
    DB, TS_, _ = x_sample.shape
    PAST = cache_kv_latent.shape[2]
    assert B == 2 and DB == 32 and TS_ == TS
    NB = SEQ // 512
    NJ = NB // 4
    NOWN = NJ * 512
    NS = SB * TS
    NTOK = NOWN + NS
    key = (SEQ, PAST)
    if key not in _CACHE:
        _CACHE[key] = build_program(SEQ, PAST)
    nc = _CACHE[key]

    cosb, sinb = _rope_tables(np.arange(SEQ))
    ropetm = np.concatenate([cosb, sinb], axis=1)
    shared = {
        "w_ada": f(w_ada[0]), "b_ada": f(b_ada[0])[None, :], "g_norm1": f(g_norm1[0])[None, :], "g_norm2": f(g_norm2[0])[None, :],
        "w_in": f(w_in[0]), "g_sg": f(g_sg[0])[None, :], "w_s": f(w_s[0]), "b_s": f(b_s[0]).reshape(1, -1),
        "g_q_a": f(g_q_a[0])[None, :], "w_uq": f(w_uq[0]), "g_q_nope": f(g_q_nope[0])[None, :], "g_q_rope": f(g_q_rope[0])[None, :],
        "g_kv_a": f(g_kv_a[0])[None, :], "g_k_rope": f(g_k_rope[0])[None, :], "w_uk": f(w_uk[0]).reshape(KVL, NH * 128),
        "g_k_nope": f(g_k_nope[0])[None, :], "w_uv": f(w_uv[0]).reshape(KVL, NH * 128),
        "w_pa": f(w_pa[0]), "w_pb": f(w_pb[0]), "w_o": f(w_o[0]), "w_up": f(w_up[0]), "w_down": f(w_down[0]),
        "ropetm": ropetm, "ident": np.eye(128, dtype=np.float32), "tril": np.tril(np.ones((128, 128), np.float32)),
    }
    in_maps = []
    for k in range(8):
        b, c = k // 4, k % 4
        rows = np.concatenate([np.arange((4 * J + c) * 512, (4 * J + c + 1) * 512) for J in range(NJ)])
        xo = np.concatenate([x_prompt[b][rows], x_sample[4 * k:4 * k + 4].reshape(NS, D)], axis=0)
        pos_o = np.concatenate([rows, np.tile(PAST + np.arange(TS), SB)])
        co, so = _rope_tables(pos_o)
        ropeo = np.concatenate([co, so], axis=1)
        town = np.concatenate([co.T, co.T, -so.T, so.T], axis=0)
        p = np.arange(128)[:, None, None]
        kbt = np.arange(16)[None, :, None]
        qc = np.arange(8)[None, None, :]
        mask = ((kbt * 2 + p // 64) <= (c * 8 + qc)).astype(np.float32).reshape(128, 128)
        m = dict(shared)
        m.update({
            "xb": x_prompt[b], "xo": f(xo),
            "clat": f(cache_kv_latent[0, 4 * k:4 * k + 4]), "ckr": f(cache_k_rope[0, 4 * k:4 * k + 4]),
            "cvec": f(np.concatenate([np.asarray(c_prompt)[b:b + 1], np.asarray(c_sample)[4 * k:4 * k + 4]], axis=0)),
            "ropeo": f(ropeo), "town": f(town), "maskc": f(mask),
        })
        in_maps.append(m)
    res = run_bass_kernel_spmd(nc, in_maps, core_ids=list(range(8)))
    y_p = np.zeros((B, SEQ, D), np.float32)
    y_s = np.zeros((DB, TS, D), np.float32)
    lat_p = np.zeros((1, B, SEQ, KVL), np.float32)
    kr_p = np.zeros((1, B, SEQ, ROPE), np.float32)
    lat_s = np.zeros((1, DB, TS, KVL), np.float32)
    kr_s = np.zeros((1, DB, TS, ROPE), np.float32)
    v_s = np.zeros((1, DB, TS, D), np.float32)
    for k in range(8):
        b, c = k // 4, k % 4
        r = res.results[k]
        rows = np.concatenate([np.arange((4 * J + c) * 512, (4 * J + c + 1) * 512) for J in range(NJ)])
        y_p[b, rows] = r["y_all"][:NOWN]
        lat_p[0, b, rows] = r["lat_all"][:NOWN]
        kr_p[0, b, rows] = r["kr_all"][:NOWN]
        y_s[4 * k:4 * k + 4] = r["y_all"][NOWN:].reshape(SB, TS, D)
        lat_s[0, 4 * k:4 * k + 4] = r["lat_all"][NOWN:].reshape(SB, TS, KVL)
        kr_s[0, 4 * k:4 * k + 4] = r["kr_all"][NOWN:].reshape(SB, TS, ROPE)
        v_s[0, 4 * k:4 * k + 4] = r["v_all"].reshape(SB, TS, D)
    return (y_p, y_s, lat_p, kr_p, lat_s, kr_s, v_s)
```

```python
import numpy as np
import concourse.bass as bass
import concourse.mybir as mybir
from concourse.bass_utils import run_bass_kernel_spmd

F32 = mybir.dt.float32
BF16 = mybir.dt.bfloat16
AF = mybir.ActivationFunctionType
ALU = mybir.AluOpType

D = 2048
NH = 16
QL = 512
KVL = 512
ROPE = 64
QKH = 192
OFF_Q = 4096
OFF_KV = OFF_Q + QL
OFF_GATE = OFF_KV + KVL + ROPE
IN_COLS = OFF_GATE + 2 * D
HID = 4 * D
EPS = 1e-6
SCALE = QKH ** -0.5
TS = 16
SB = 4


class _Rec:
    def __init__(self):
        self.calls = []

    def __getattr__(self, name):
        def f(*a, **kw):
            self.calls.append((name, a, kw))
            return None
        return f


class Sem:
    def __init__(self, h):
        self.h = h
        self.val = 0


class Res:
    def __init__(self, t, dsem=None):
        self.t = t
        self.w = None
        self.r = {}
        self.dsem = dsem


class Eng:
    def __init__(self, name, sem, self_sync):
        self.name = name
        self.sem = sem
        self.ops = []
        self.seen = {}
        self.self_sync = self_sync

    def wait(self, tok):
        if tok is None:
            return
        sem, val = tok
        if sem is self.sem and not self.self_sync:
            return
        if self.seen.get(id(sem), 0) >= val:
            return
        self.seen[id(sem)] = val
        self.ops.append(("w", sem.h, val))


class Prog:
    def __init__(self, nc, stack):
        self.nc = nc
        mk = lambda n: Sem(stack.enter_context(nc.semaphore(n)))
        self.pe = Eng("pe", mk("s_pe"), False)
        self.act = Eng("act", mk("s_act"), True)
        self.dve = Eng("dve", mk("s_dve"), True)
        self.pool = Eng("pool", mk("s_pool"), True)
        self.sp = Eng("sp", mk("s_sp"), True)
        self.engs = [self.pe, self.act, self.dve, self.pool, self.sp]
        self.dsems = [mk("d%d" % i) for i in range(90)]
        self.dnext = 0
        self.outstanding = {}

    def new_dsem(self):
        s = self.dsems[self.dnext % len(self.dsems)]
        self.dnext += 1
        return s

    def deps(self, E, reads, writes):
        for r in reads:
            E.wait(r.w)
        for w in writes:
            E.wait(w.w)
            for tok in list(w.r.values()):
                E.wait(tok)

    def mark(self, tok, reads, writes):
        for r in reads:
            r.r[id(tok[0])] = tok
        for w in writes:
            w.w = tok
            w.r = {}

    def op(self, E, fns, reads=(), writes=()):
        if callable(fns):
            fns = [fns]
        self.deps(E, reads, writes)
        E.sem.val += 1
        tok = (E.sem, E.sem.val)
        calls = []
        for f in fns:
            rec = _Rec()
            f(rec)
            assert len(rec.calls) == 1
            calls.append(rec.calls[0])
        for c in calls[:-1]:
            E.ops.append(("i", c, None))
        E.ops.append(("i", calls[-1], E.sem.h))
        self.mark(tok, reads, writes)

    def dma(self, Q, out, in_, reads=(), writes=()):
        self.deps(Q, reads, writes)
        sem = None
        for x in list(writes) + list(reads):
            if x.dsem is not None:
                sem = x.dsem
                break
        assert sem is not None
        sem.val += 16
        tok = (sem, sem.val)
        Q.ops.append(("d", out, in_, sem.h))
        self.mark(tok, reads, writes)
        self.outstanding[id(sem)] = tok

    def barrier(self):
        toks = [(E.sem, E.sem.val) for E in self.engs if E.sem.val > 0] + list(self.outstanding.values())
        for E in self.engs:
            for t in toks:
                E.wait(t)
        self.outstanding = {}

    def replay(self, E, e):
        for o in E.ops:
            if o[0] == "w":
                e.wait_ge(o[1], o[2])
            elif o[0] == "i":
                name, a, kw = o[1]
                ins = getattr(e, name)(*a, **kw)
                if o[2] is not None:
                    ins.then_inc(o[2], 1)
            else:
                e.dma_start(out=o[1], in_=o[2]).then_inc(o[3], 16)


class Arena:
    def __init__(self, big, nwords, prog):
        self.big = big
        self.cap = nwords
        self.off = 0
        self.prog = prog

    def reset(self, to=0):
        self.off = to

    def raw(self, n, dt):
        words = (n * (4 if dt == F32 else 2) + 3) // 4
        words = (words + 7) // 8 * 8
        assert self.off + words <= self.cap, "SBUF arena overflow %d+%d>%d" % (self.off, words, self.cap)
        v = self.big[:, self.off:self.off + words]
        self.off += words
        if dt == BF16:
            v = v.bitcast(BF16)
        return v[:, 0:n]

    def tile(self, shape, dt, dsem=True):
        n = int(np.prod(shape))
        v = self.raw(n, dt)
        if len(shape) == 2:
            v = v.rearrange("p (a b) -> p a b", a=shape[0], b=shape[1])
        elif len(shape) == 3:
            v = v.rearrange("p (a b c) -> p a b c", a=shape[0], b=shape[1], c=shape[2])
        return Res(v, self.prog.new_dsem() if dsem else None)


class Rot:
    def __init__(self, items):
        self.items = items
        self.i = 0

    def get(self):
        x = self.items[self.i % len(self.items)]
        self.i += 1
        return x


def build_program(SEQ, PAST):
    import contextlib
    nc = bass.Bass("TRN2", target_bir_lowering=False)
    NB = SEQ // 512
    NJ = NB // 4
    NOWN = NJ * 512
    NS = SB * TS
    NTOK = NOWN + NS
    NKT = PAST // 128
    PASTP = PAST + TS
    NT = SEQ // 128

    def din(name, shape, dt=F32):
        return nc.dram_tensor(name, list(shape), dt, kind="ExternalInput").ap()

    def dout(name, shape):
        return nc.dram_tensor(name, list(shape), F32, kind="ExternalOutput").ap()

    def dtmp(name, shape, dt):
        return nc.dram_tensor(name, list(shape), dt, kind="Internal").ap()

    xb = din("xb", [SEQ, D])
    xo = din("xo", [NTOK, D])
    clat = din("clat", [SB, PAST, KVL])
    ckr = din("ckr", [SB, PAST, ROPE])
    cvec = din("cvec", [5, D])
    w_ada = din("w_ada", [D, 6 * D])
    b_ada = din("b_ada", [1, 6 * D])
    g_norm1 = din("g_norm1", [1, D])
    g_norm2 = din("g_norm2", [1, D])
    w_in = din("w_in", [D, IN_COLS])
    g_sg = din("g_sg", [1, D])
    w_s = din("w_s", [NH, 128, 128])
    b_s = din("b_s", [1, NH * 128])
    g_q_a = din("g_q_a", [1, QL])
    w_uq = din("w_uq", [QL, NH * QKH])
    g_q_nope = din("g_q_nope", [1, 128])
    g_q_rope = din("g_q_rope", [1, ROPE])
    g_kv_a = din("g_kv_a", [1, KVL])
    g_k_rope = din("g_k_rope", [1, ROPE])
    w_uk = din("w_uk", [KVL, NH * 128])
    g_k_nope = din("g_k_nope", [1, 128])
    w_uv = din("w_uv", [KVL, NH * 128])
    w_pa = din("w_pa", [D, D])
    w_pb = din("w_pb", [D, D])
    w_o = din("w_o", [D, D])
    w_up = din("w_up", [D, HID])
    w_down = din("w_down", [HID, D])
    ropetm = din("ropetm", [SEQ, ROPE])
    ropeo = din("ropeo", [NTOK, ROPE])
    town = din("town", [128, NTOK])
    maskc = din("maskc", [128, 16 * 8])
    identd = din("ident", [128, 128])
    trild = din("tril", [128, 128])

    y_all = dout("y_all", [NTOK, D])
    lat_all = dout("lat_all", [NTOK, KVL])
    kr_all = dout("kr_all", [NTOK, ROPE])
    v_all = dout("v_all", [NS, D])

    mod_d = dtmp("mod_d", [5, 6 * D], F32)
    KT_d = dtmp("KT_d", [NH, 128, SEQ], BF16)
    KrT_d = dtmp("KrT_d", [128, SEQ], BF16)
    V_d = dtmp("V_d", [NH, 128, NT, 128], BF16)
    KTs_d = dtmp("KTs_d", [SB, NH, 128, PASTP], BF16)
    KrTs_d = dtmp("KrTs_d", [SB, 128, PASTP], BF16)
    Vs_d = dtmp("Vs_d", [SB, NH, 128, NKT + 1, 128], BF16)
    hT_d = dtmp("hT_d", [128, 16, NTOK], BF16)
    uT_d = dtmp("uT_d", [128, 16, NTOK], BF16)
    v_d = dtmp("v_d", [NTOK, D], BF16)
    QT_d = dtmp("QT_d", [NH, 128, NTOK], BF16)
    QrT_d = dtmp("QrT_d", [NH, 128, NTOK], BF16)
    gaT_d = dtmp("gaT_d", [128, 16, NTOK], BF16)
    gbT_d = dtmp("gbT_d", [128, 16, NTOK], BF16)
    osgT_d = dtmp("osgT_d", [128, 16, NTOK], BF16)
    omlaT_d = dtmp("omlaT_d", [128, 16, NTOK], BF16)
    mT_d = dtmp("mT_d", [128, 16, NTOK], BF16)
    h2T_d = dtmp("h2T_d", [128, 16, NTOK], BF16)
    x1_d = dtmp("x1_d", [NTOK, D], F32)
    hidT_d = dtmp("hidT_d", [128, 64, NTOK], BF16)

    AW = 51200
    with contextlib.ExitStack() as stack:
        big = stack.enter_context(nc.sbuf_tensor("arena", [128, AW], F32))
        psb = [stack.enter_context(nc.psum_tensor("ps%d" % i, [128, 512], F32)) for i in range(8)]
        P = Prog(nc, stack)
        A = Arena(big, AW, P)
        PS = [Res(psb[i][:, :]) for i in range(8)]
        pe, act, dve, pool, sp = P.pe, P.act, P.dve, P.pool, P.sp

        def psbf(res):
            return res.t.bitcast(BF16)

        identf = A.tile([128], F32)
        identb = A.tile([128], BF16, dsem=False)
        ones_b = A.tile([128], BF16, dsem=False)
        ones_f = A.tile([128], F32, dsem=False)
        P.dma(sp, identf.t, identd[:, :], writes=[identf])
        P.op(dve, lambda e: e.tensor_copy(out=identb.t, in_=identf.t), reads=[identf], writes=[identb])
        P.op(dve, lambda e: e.memset(ones_b.t, 1.0), writes=[ones_b])
        P.op(dve, lambda e: e.memset(ones_f.t, 1.0), writes=[ones_f])
        PERSIST = A.off

        def blocks():
            for j in range(NJ):
                yield (j, 512 * j, 512, 4, 128, False)
            yield (NJ, NOWN, NS, 1, NS, True)

        def wview(w, c0, c1):
            return w.rearrange("(k p) n -> p k n", p=128)[:, :, c0:c1]

        def load_rows_prompt_or_sample(tile, src_prompt_row, src_sample_rows, is_s, n):
            if not is_s:
                P.dma(sp, tile.t, src_prompt_row.partition_broadcast(128), writes=[tile])
            else:
                for i in range(SB):
                    P.dma(sp, tile.t[16 * i:16 * i + 16, :], src_sample_rows[i].partition_broadcast(16), writes=[tile])

        def rstd_col(ss, rs, R, n):
            P.op(act, lambda e: e.activation(out=rs.t[0:R, :], in_=ss.t[0:R, :], func=AF.Ln, scale=1.0 / n, bias=EPS),
                 reads=[ss], writes=[rs])
            P.op(act, lambda e: e.activation(out=rs.t[0:R, :], in_=rs.t[0:R, :], func=AF.Exp, scale=-0.5), reads=[rs], writes=[rs])

        def rstd_tile(ps, rs, T, n, parts=128):
            P.op(act, lambda e: e.activation(out=rs.t[0:parts, 0:T], in_=ps.t[0:parts, 0:T], func=AF.Ln, scale=1.0 / n, bias=EPS),
                 reads=[ps], writes=[rs])
            P.op(act, lambda e: e.activation(out=rs.t[0:parts, 0:T], in_=rs.t[0:parts, 0:T], func=AF.Exp, scale=-0.5), reads=[rs], writes=[rs])

        psrot = Rot(PS)

        def transpose_to(dst_fn, src, R, nk, reads, writes_res):
            for k0 in range(0, nk, 8):
                kk = min(8, nk - k0)
                ps = psrot.get()
                pv = psbf(ps)
                fns = []
                for k in range(kk):
                    fns.append(lambda e, k=k: e.transpose(pv[:, k * 128:k * 128 + R], src.t[0:R, (k0 + k) * 128:(k0 + k + 1) * 128], identb.t[0:R, 0:R]))
                P.op(pe, fns, reads=[src, identb] + reads, writes=[ps])
                yield k0, kk, ps, pv

        c5 = A.tile([D], F32)
        c5b = A.tile([D], BF16, dsem=False)
        cT = A.tile([16, 8], BF16, dsem=False)
        bada = A.tile([6 * D], F32)
        wb = [A.tile([16, 512], BF16) for _ in range(2)]
        msb = [A.tile([512], F32) for _ in range(2)]
        P.dma(sp, c5.t[0:5, :], cvec[:, :], writes=[c5])
        P.dma(sp, bada.t[0:5, :], b_ada[0:1, :].partition_broadcast(5), writes=[bada])
        P.op(act, lambda e: e.activation(out=c5b.t[0:5, :], in_=c5.t[0:5, :], func=AF.Silu), reads=[c5], writes=[c5b])
        for k0 in (0, 8):
            ps = psrot.get()
            pv = psbf(ps)
            P.op(pe, [lambda e, k=k, pv=pv: e.transpose(pv[:, k * 8:k * 8 + 5], c5b.t[0:5, (k0 + k) * 128:(k0 + k + 1) * 128], identb.t[0:5, 0:5]) for k in range(8)],
                 reads=[c5b, identb], writes=[ps])
            P.op(dve, lambda e, pv=pv, k0=k0: e.tensor_copy(out=cT.t[:, k0:k0 + 8, 0:5], in_=pv[:, 0:64].rearrange("p (k n) -> p k n", k=8, n=8)[:, :, 0:5]),
                 reads=[ps], writes=[cT])
        for cc in range(24):
            w = wb[cc % 2]
            m = msb[cc % 2]
            P.dma(pool, w.t, wview(w_ada, cc * 512, (cc + 1) * 512), writes=[w])
            ps = psrot.get()
            P.op(pe, [lambda e, k=k, w=w, ps=ps: e.matmul(ps.t[0:5, :], cT.t[:, k, 0:5], w.t[:, k, :], start=(k == 0), stop=(k == 15)) for k in range(16)],
                 reads=[cT, w], writes=[ps])
            P.op(dve, lambda e, ps=ps, m=m, cc=cc: e.tensor_tensor(out=m.t[0:5, :], in0=ps.t[0:5, :], in1=bada.t[0:5, cc * 512:(cc + 1) * 512], op=ALU.add),
                 reads=[ps, bada], writes=[m])
            P.dma(pool, mod_d[:, cc * 512:(cc + 1) * 512], m.t[0:5, :], reads=[m])
        P.barrier()
        A.reset(PERSIST)

        def make_front_rows(gn_d, off_sh, off_sc, is_s, gn):
            G = A.tile([D], F32)
            S = A.tile([D], F32)
            load_rows_prompt_or_sample(S, mod_d[0:1, off_sh:off_sh + D], [mod_d[1 + i:2 + i, off_sh:off_sh + D] for i in range(SB)], is_s, D)
            load_rows_prompt_or_sample(G, mod_d[0:1, off_sc:off_sc + D], [mod_d[1 + i:2 + i, off_sc:off_sc + D] for i in range(SB)], is_s, D)
            P.dma(sp, gn.t, gn_d[0:1, :].partition_broadcast(128), writes=[gn])
            P.op(dve, lambda e: e.scalar_tensor_tensor(out=G.t, in0=G.t, scalar=1.0, in1=gn.t, op0=ALU.add, op1=ALU.mult),
                 reads=[G, gn], writes=[G])
            return G, S

        def reload_front_rows(G, S, gn_d, off_sh, off_sc, gn):
            load_rows_prompt_or_sample(S, None, [mod_d[1 + i:2 + i, off_sh:off_sh + D] for i in range(SB)], True, D)
            load_rows_prompt_or_sample(G, None, [mod_d[1 + i:2 + i, off_sc:off_sc + D] for i in range(SB)], True, D)
            P.op(dve, lambda e: e.scalar_tensor_tensor(out=G.t[0:NS, :], in0=G.t[0:NS, :], scalar=1.0, in1=gn.t[0:NS, :], op0=ALU.add, op1=ALU.mult),
                 reads=[G, gn], writes=[G])

        class FrontBufs:
            def __init__(self):
                self.junk = A.tile([D], BF16, dsem=False)
                self.tmp = A.tile([D], F32)
                self.hb = [A.tile([D], BF16, dsem=False) for _ in range(2)]
                self.ss = [A.tile([1], F32, dsem=False) for _ in range(2)]
                self.rs = [A.tile([1], F32, dsem=False) for _ in range(2)]
                self.i = 0

        def front_a(fb, xt, R, G, S):
            i = fb.i
            fb.i += 1
            ss, rs, hb = fb.ss[i % 2], fb.rs[i % 2], fb.hb[i % 2]
            P.op(dve, lambda e: e.memset(ss.t[0:R, :], 0.0), writes=[ss])
            P.op(act, lambda e: e.activation(out=fb.junk.t[0:R, :], in_=xt.t[0:R, :], func=AF.Square, accum_out=ss.t[0:R, :]),
                 reads=[xt, ss], writes=[fb.junk, ss])
            rstd_col(ss, rs, R, D)
            P.op(dve, lambda e: e.scalar_tensor_tensor(out=fb.tmp.t[0:R, :], in0=xt.t[0:R, :], scalar=rs.t[0:R, :], in1=G.t[0:R, :], op0=ALU.mult, op1=ALU.mult),
                 reads=[xt, rs, G], writes=[fb.tmp])
            P.op(dve, lambda e: e.tensor_tensor(out=hb.t[0:R, :], in0=fb.tmp.t[0:R, :], in1=S.t[0:R, :], op=ALU.add),
                 reads=[fb.tmp, S], writes=[hb])
            return hb

        def front_b(hb, R, hT, col0):
            for k0, kk, ps, pv in transpose_to(None, hb, R, 16, [], None):
                P.op(act, lambda e, k0=k0, kk=kk, pv=pv: e.copy(out=hT.t[:, k0:k0 + kk, col0:col0 + R],
                                                               in_=pv.rearrange("p (k n) -> p k n", k=8, n=128)[:, 0:kk, 0:R]),
                     reads=[ps], writes=[hT])

        def front_tile(fb, xt, R, G, S, hT, col0):
            hb = front_a(fb, xt, R, G, S)
            front_b(hb, R, hT, col0)

        class KvBufs:
            def __init__(self):
                self.gkv = A.tile([KVL], F32)
                self.gkr = A.tile([ROPE], F32)
                P.dma(sp, self.gkv.t, g_kv_a[0:1, :].partition_broadcast(128), writes=[self.gkv])
                P.dma(sp, self.gkr.t, g_k_rope[0:1, :].partition_broadcast(128), writes=[self.gkr])
                self.junk = A.tile([KVL], BF16, dsem=False)
                self.ss = [A.tile([2], F32, dsem=False) for _ in range(2)]
                self.rs = [A.tile([2], F32, dsem=False) for _ in range(2)]
                self.kn = [A.tile([ROPE], F32, dsem=False) for _ in range(2)]
                self.t4 = [A.tile([4, 32], F32, dsem=False) for _ in range(2)]
                self.cs = [A.tile([ROPE], F32) for _ in range(2)]
                self.i = 0

        def zkv_tile(kb, Wkv, hT, col0, R, rope_rows_ap, ckv_out, kr_out):
            i = kb.i
            kb.i += 1
            ss, rs, kn, t4, cs = kb.ss[i % 2], kb.rs[i % 2], kb.kn[i % 2], kb.t4[i % 2], kb.cs[i % 2]
            P.dma(sp, cs.t[0:R, :], rope_rows_ap, writes=[cs])
            psc = psrot.get()
            P.op(pe, [lambda e, k=k: e.matmul(psc.t[0:R, :], hT.t[:, k, col0:col0 + R], Wkv.t[:, k, 0:KVL], start=(k == 0), stop=(k == 15)) for k in range(16)],
                 reads=[hT, Wkv], writes=[psc])
            psr = psrot.get()
            P.op(pe, [lambda e, k=k: e.matmul(psr.t[0:R, 0:ROPE], hT.t[:, k, col0:col0 + R], Wkv.t[:, k, KVL:KVL + ROPE], start=(k == 0), stop=(k == 15)) for k in range(16)],
                 reads=[hT, Wkv], writes=[psr])
            P.op(dve, lambda e: e.memset(ss.t[0:R, :], 0.0), writes=[ss])
            P.op(act, lambda e: e.activation(out=kb.junk.t[0:R, :], in_=psc.t[0:R, :], func=AF.Square, accum_out=ss.t[0:R, 0:1]),
                 reads=[psc, ss], writes=[kb.junk, ss])
            P.op(act, lambda e: e.activation(out=kb.junk.t[0:R, 0:ROPE], in_=psr.t[0:R, 0:ROPE], func=AF.Square, accum_out=ss.t[0:R, 1:2]),
                 reads=[psr, ss], writes=[kb.junk, ss])
            P.op(act, lambda e: e.activation(out=rs.t[0:R, 0:1], in_=ss.t[0:R, 0:1], func=AF.Ln, scale=1.0 / KVL, bias=EPS), reads=[ss], writes=[rs])
            P.op(act, lambda e: e.activation(out=rs.t[0:R, 1:2], in_=ss.t[0:R, 1:2], func=AF.Ln, scale=1.0 / ROPE, bias=EPS), reads=[ss], writes=[rs])
            P.op(act, lambda e: e.activation(out=rs.t[0:R, :], in_=rs.t[0:R, :], func=AF.Exp, scale=-0.5), reads=[rs], writes=[rs])
            P.op(dve, lambda e: e.scalar_tensor_tensor(out=ckv_out.t[0:R, :], in0=psc.t[0:R, :], scalar=rs.t[0:R, 0:1], in1=kb.gkv.t[0:R, :], op0=ALU.mult, op1=ALU.mult),
                 reads=[psc, rs, kb.gkv], writes=[ckv_out])
            P.op(dve, lambda e: e.scalar_tensor_tensor(out=kn.t[0:R, :], in0=psr.t[0:R, 0:ROPE], scalar=rs.t[0:R, 1:2], in1=kb.gkr.t[0:R, :], op0=ALU.mult, op1=ALU.mult),
                 reads=[psr, rs, kb.gkr], writes=[kn])
            x1, x2 = kn.t[0:R, 0:32], kn.t[0:R, 32:64]
            co, si = cs.t[0:R, 0:32], cs.t[0:R, 32:64]
            P.op(dve, [lambda e: e.tensor_tensor(out=t4.t[0:R, 0, :], in0=x1, in1=co, op=ALU.mult),
                       lambda e: e.tensor_tensor(out=t4.t[0:R, 1, :], in0=x2, in1=si, op=ALU.mult),
                       lambda e: e.tensor_tensor(out=t4.t[0:R, 2, :], in0=x2, in1=co, op=ALU.mult),
                       lambda e: e.tensor_tensor(out=t4.t[0:R, 3, :], in0=x1, in1=si, op=ALU.mult)],
                 reads=[kn, cs], writes=[t4])
            P.op(dve, [lambda e: e.tensor_tensor(out=kr_out.t[0:R, 0:32], in0=t4.t[0:R, 0, :], in1=t4.t[0:R, 1, :], op=ALU.subtract),
                       lambda e: e.tensor_tensor(out=kr_out.t[0:R, 32:64], in0=t4.t[0:R, 2, :], in1=t4.t[0:R, 3, :], op=ALU.add)],
                 reads=[t4], writes=[kr_out])

        class ExpBufs:
            def __init__(self, TG):
                self.wuk = A.tile([4, NH * 128], BF16)
                self.wuv = A.tile([4, NH * 128], BF16)
                P.dma(pool, self.wuk.t, wview(w_uk, 0, NH * 128), writes=[self.wuk])
                P.dma(pool, self.wuv.t, wview(w_uv, 0, NH * 128), writes=[self.wuv])
                self.gk = A.tile([1], F32)
                P.dma(sp, self.gk.t, g_k_nope.rearrange("o d -> d o"), writes=[self.gk])
                self.cb = [A.tile([KVL], BF16, dsem=False) for _ in range(2)]
                self.krb = [A.tile([128], BF16, dsem=False) for _ in range(2)]
                self.cT = [A.tile([4, TG], BF16, dsem=False) for _ in range(2)]
                self.krT = [A.tile([TG], BF16) for _ in range(2)]
                self.sq = [A.tile([TG], BF16, dsem=False) for _ in range(2)]
                self.rs = [A.tile([TG], F32, dsem=False) for _ in range(2)]
                self.KTs = [A.tile([NH, TG], BF16) for _ in range(1)]
                self.Vs = [A.tile([NH, TG // 128 if TG >= 128 else 1, 128], BF16) for _ in range(1)]
                self.i = 0

        def expand_group(xb_, tiles, T, KT_dst, KrT_dst, V_dst, hooks=None):
            i = xb_.i
            xb_.i += 1
            cT, krT, KTs, Vs = xb_.cT[i % 2], xb_.krT[i % 2], xb_.KTs[0], xb_.Vs[0]
            col = 0
            cols = []
            for ti, (ckv, kr, R) in enumerate(tiles):
                cb, krb = xb_.cb[ti % 2], xb_.krb[ti % 2]
                P.op(act, lambda e, cb=cb, ckv=ckv, R=R: e.copy(out=cb.t[0:R, :], in_=ckv.t[0:R, :]), reads=[ckv], writes=[cb])
                P.op(dve, [lambda e, krb=krb, kr=kr, R=R: e.tensor_copy(out=krb.t[0:R, 0:64], in_=kr.t[0:R, :]),
                           lambda e, krb=krb, kr=kr, R=R: e.tensor_copy(out=krb.t[0:R, 64:128], in_=kr.t[0:R, :])], reads=[kr], writes=[krb])
                for k0, kk, ps, pv in transpose_to(None, cb, R, 4, [], None):
                    P.op(dve, lambda e, pv=pv, col=col, R=R: e.tensor_copy(out=cT.t[:, 0:4, col:col + R], in_=pv[:, 0:512].rearrange("p (k n) -> p k n", k=4, n=128)[:, :, 0:R]),
                         reads=[ps], writes=[cT])
                for k0, kk, ps, pv in transpose_to(None, krb, R, 1, [], None):
                    P.op(dve, lambda e, pv=pv, col=col, R=R: e.tensor_copy(out=krT.t[:, col:col + R], in_=pv[:, 0:R]), reads=[ps], writes=[krT])
                cols.append((col, R))
                col += R
            assert col == T
            P.dma(pool, KrT_dst, krT.t[:, 0:T], reads=[krT])
            def k_mm(h):
                psk = psrot.get()
                P.op(pe, [lambda e, kc=kc: e.matmul(psk.t[:, 0:T], xb_.wuk.t[:, kc, h * 128:(h + 1) * 128], cT.t[:, kc, 0:T], start=(kc == 0), stop=(kc == 3)) for kc in range(4)],
                     reads=[xb_.wuk, cT], writes=[psk])
                return psk

            def k_rest(h, psk):
                sq, rs = xb_.sq[h % 2], xb_.rs[h % 2]
                P.op(act, lambda e: e.activation(out=sq.t[:, 0:T], in_=psk.t[:, 0:T], func=AF.Square), reads=[psk], writes=[sq])
                pss = psrot.get()
                P.op(pe, lambda e: e.matmul(pss.t[:, 0:T], ones_b.t, sq.t[:, 0:T], start=True, stop=True), reads=[ones_b, sq], writes=[pss])
                rstd_tile(pss, rs, T, 128)
                P.op(dve, lambda e: e.scalar_tensor_tensor(out=KTs.t[:, h, 0:T], in0=psk.t[:, 0:T], scalar=xb_.gk.t[:, 0:1], in1=rs.t[:, 0:T], op0=ALU.mult, op1=ALU.mult),
                     reads=[psk, rs, xb_.gk], writes=[KTs])

            LK = 2
            psks = {}
            for i in range(NH + LK):
                if i < NH:
                    psks[i] = k_mm(i)
                if i - LK >= 0:
                    k_rest(i - LK, psks.pop(i - LK))
                    if hooks and (i - LK) in hooks:
                        hooks[i - LK]()
            P.dma(pool, KT_dst, KTs.t[:, :, 0:T], reads=[KTs])
            Rmax = max(R for _, R in cols)
            for ti, (col, R) in enumerate(cols):
                for c4 in range(4):
                    psv = psrot.get()
                    P.op(pe, [lambda e, kc=kc, psv=psv, col=col, R=R, c4=c4: e.matmul(psv.t[0:R, :], cT.t[:, kc, col:col + R], xb_.wuv.t[:, kc, c4 * 512:(c4 + 1) * 512], start=(kc == 0), stop=(kc == 3)) for kc in range(4)],
                         reads=[xb_.wuv, cT], writes=[psv])
                    P.op(act, lambda e, psv=psv, ti=ti, R=R, c4=c4: e.copy(out=Vs.t[0:R, c4 * 4:(c4 + 1) * 4, ti, :], in_=psv.t[0:R, :].rearrange("p (h c) -> p h c", h=4, c=128)),
                         reads=[psv], writes=[Vs])
            P.dma(pool, V_dst, Vs.t[0:Rmax, :, 0:len(cols), :], reads=[Vs])

        fb = FrontBufs()
        G1, S1 = make_front_rows(g_norm1, 0, D, False, fb.tmp)
        kb = KvBufs()
        Wkv = A.tile([16, KVL + ROPE], BF16)
        P.dma(pool, Wkv.t, wview(w_in, OFF_KV, OFF_GATE), writes=[Wkv])
        eb = ExpBufs(512)
        xts = [A.tile([D], F32) for _ in range(2)]
        hTs = [A.tile([16, 512], BF16) for _ in range(1)]
        ckvs = [A.tile([KVL], F32) for _ in range(4)]
        krs = [A.tile([ROPE], F32) for _ in range(4)]
        xi = [0]
        hT = hTs[0]

        pend_hb = {}

        def a_front_a(g, t):
            xt = xts[xi[0] % 2]
            xi[0] += 1
            r0 = g * 512 + t * 128
            P.dma(sp, xt.t, xb[r0:r0 + 128, :], writes=[xt])
            pend_hb[t] = front_a(fb, xt, 128, G1, S1)

        def a_front_b(g, t):
            front_b(pend_hb.pop(t), 128, hT, t * 128)

        def a_front(g, t):
            a_front_a(g, t)
            a_front_b(g, t)

        def a_zkv(g):
            tl = []
            for t in range(4):
                ck, kr_ = ckvs[t], krs[t]
                r0 = g * 512 + t * 128
                zkv_tile(kb, Wkv, hT, t * 128, 128, ropetm[r0:r0 + 128, :], ck, kr_)
                tl.append((ck, kr_, 128))
            return tl

        for t in range(4):
            a_front(0, t)
        tl = a_zkv(0)
        for g in range(NB):
            hooks = None
            if g + 1 < NB:
                hooks = {4 * t: (lambda t=t, g=g: a_front_a(g + 1, t)) for t in range(4)}
                hooks.update({4 * t + 3: (lambda t=t, g=g: a_front_b(g + 1, t)) for t in range(4)})
            expand_group(eb, tl, 512,
                         KT_d.rearrange("h d t -> d h t")[:, :, g * 512:(g + 1) * 512],
                         KrT_d[:, g * 512:(g + 1) * 512],
                         V_d.rearrange("h p t c -> p h t c")[:, :, g * 4:(g + 1) * 4, :], hooks=hooks)
            if g + 1 < NB:
                tl = a_zkv(g + 1)
        P.barrier()
        A.reset(PERSIST)

        gn1 = A.tile([D], F32)
        G1, S1 = make_front_rows(g_norm1, 0, D, False, gn1)
        fb = FrontBufs()
        xts = [A.tile([D], F32) for _ in range(2)]
        hTs = [A.tile([16, 512], BF16) for _ in range(2)]
        xi = 0
        for (bi, row0, T, nt, R, is_s) in blocks():
            if is_s:
                reload_front_rows(G1, S1, g_norm1, 0, D, gn1)
            hT = hTs[bi % 2]
            for t in range(nt):
                xt = xts[xi % 2]
                xi += 1
                P.dma(sp, xt.t[0:R, :], xo[row0 + t * 128:row0 + t * 128 + R, :], writes=[xt])
                front_tile(fb, xt, R, G1, S1, hT, t * 128)
            P.dma(pool, hT_d[:, :, row0:row0 + T], hT.t[:, :, 0:T], reads=[hT])
        P.barrier()
        A.reset(PERSIST)

        def fm_stage(c_base, func, dst_d):
            W = A.tile([16, D], BF16)
            P.dma(pool, W.t[:, :, 0:1024], wview(w_in, c_base, c_base + 1024), writes=[W])
            P.dma(pool, W.t[:, :, 1024:2048], wview(w_in, c_base + 1024, c_base + 2048), writes=[W])
            hTs = [A.tile([16, 512], BF16) for _ in range(2)]
            outs = [A.tile([16, 512], BF16) for _ in range(2)]
            for (bi, row0, T, nt, R, is_s) in blocks():
                hT, o = hTs[bi % 2], outs[bi % 2]
                P.dma(sp, hT.t[:, :, 0:T], hT_d[:, :, row0:row0 + T], writes=[hT])
                for j in range(16):
                    ps = psrot.get()
                    P.op(pe, [lambda e, k=k, j=j, ps=ps, hT=hT: e.matmul(ps.t[:, 0:T], W.t[:, k, j * 128:(j + 1) * 128], hT.t[:, k, 0:T], start=(k == 0), stop=(k == 15)) for k in range(16)],
                         reads=[W, hT], writes=[ps])
                    P.op(act, lambda e, ps=ps, o=o, j=j: e.activation(out=o.t[:, j, 0:T], in_=ps.t[:, 0:T], func=func), reads=[ps], writes=[o])
                P.dma(pool, dst_d[:, :, row0:row0 + T], o.t[:, :, 0:T], reads=[o])
            P.barrier()
            A.reset(PERSIST)

        fm_stage(0, AF.Gelu, uT_d)
        fm_stage(OFF_GATE, AF.Sigmoid, gaT_d)
        fm_stage(OFF_GATE + D, AF.Sigmoid, gbT_d)

        W = A.tile([16, D], BF16)
        P.dma(pool, W.t[:, :, 0:1024], wview(w_in, D, D + 1024), writes=[W])
        P.dma(pool, W.t[:, :, 1024:2048], wview(w_in, D + 1024, 2 * D), writes=[W])
        gsg = A.tile([D], F32)
        P.dma(sp, gsg.t, g_sg[0:1, :].partition_broadcast(128), writes=[gsg])
        hTs = [A.tile([16, 512], BF16) for _ in range(2)]
        vfs = [A.tile([D], F32) for _ in range(2)]
        vbs = [A.tile([D], BF16) for _ in range(2)]
        vjunk = A.tile([D], BF16, dsem=False)
        vss = [A.tile([1], F32, dsem=False) for _ in range(2)]
        vrs = [A.tile([1], F32, dsem=False) for _ in range(2)]
        vi = 0
        for (bi, row0, T, nt, R, is_s) in blocks():
            hT = hTs[bi % 2]
            P.dma(sp, hT.t[:, :, 0:T], hT_d[:, :, row0:row0 + T], writes=[hT])
            for t in range(nt):
                vf, vb, ss, rs = vfs[vi % 2], vbs[vi % 2], vss[vi % 2], vrs[vi % 2]
                vi += 1
                for c4 in range(4):
                    ps = psrot.get()
                    P.op(pe, [lambda e, k=k, ps=ps, hT=hT, t=t, c4=c4: e.matmul(ps.t[0:R, :], hT.t[:, k, t * 128:t * 128 + R], W.t[:, k, c4 * 512:(c4 + 1) * 512], start=(k == 0), stop=(k == 15)) for k in range(16)],
                         reads=[W, hT], writes=[ps])
                    P.op(act, lambda e, ps=ps, vf=vf, c4=c4: e.activation(out=vf.t[0:R, c4 * 512:(c4 + 1) * 512], in_=ps.t[0:R, :], func=AF.Gelu), reads=[ps], writes=[vf])
                P.op(dve, lambda e, ss=ss: e.memset(ss.t[0:R, :], 0.0), writes=[ss])
                P.op(act, lambda e, vf=vf, ss=ss: e.activation(out=vjunk.t[0:R, :], in_=vf.t[0:R, :], func=AF.Square, accum_out=ss.t[0:R, :]), reads=[vf, ss], writes=[vjunk, ss])
                rstd_col(ss, rs, R, D)
                if is_s:
                    P.op(dve, lambda e, vf=vf, rs=rs: e.scalar_tensor_tensor(out=vf.t[0:R, :], in0=vf.t[0:R, :], scalar=rs.t[0:R, :], in1=gsg.t[0:R, :], op0=ALU.mult, op1=ALU.mult),
                         reads=[vf, rs, gsg], writes=[vf])
                    P.dma(pool, v_all[:, :], vf.t[0:R, :], reads=[vf])
                    P.op(dve, lambda e, vf=vf, vb=vb: e.tensor_copy(out=vb.t[0:R, :], in_=vf.t[0:R, :]), reads=[vf], writes=[vb])
                else:
                    P.op(dve, lambda e, vf=vf, rs=rs, vb=vb: e.scalar_tensor_tensor(out=vb.t[0:R, :], in0=vf.t[0:R, :], scalar=rs.t[0:R, :], in1=gsg.t[0:R, :], op0=ALU.mult, op1=ALU.mult),
                         reads=[vf, rs, gsg], writes=[vb])
                P.dma(pool, v_d[row0 + t * 128:row0 + t * 128 + R, :], vb.t[0:R, :], reads=[vb])
        P.barrier()
        A.reset(PERSIST)

        Wq = A.tile([16, QL], BF16)
        P.dma(pool, Wq.t, wview(w_in, OFF_Q, OFF_KV), writes=[Wq])
        Wkv = A.tile([16, KVL + ROPE], BF16)
        P.dma(pool, Wkv.t, wview(w_in, OFF_KV, OFF_GATE), writes=[Wkv])
        Wuq = A.tile([4, NH * QKH], BF16)
        P.dma(pool, Wuq.t, wview(w_uq, 0, NH * QKH), writes=[Wuq])
        Wr2 = A.tile([4, NH, 128], BF16, dsem=False)
        wq3 = Wuq.t.rearrange("p k (h c) -> p k h c", h=NH, c=QKH)
        for kc in range(4):
            P.op(dve, [lambda e, kc=kc: e.tensor_copy(out=Wr2.t[:, kc, :, 0:64], in_=wq3[:, kc, :, 128:192]),
                       lambda e, kc=kc: e.tensor_copy(out=Wr2.t[:, kc, :, 64:96], in_=wq3[:, kc, :, 160:192]),
                       lambda e, kc=kc: e.tensor_copy(out=Wr2.t[:, kc, :, 96:128], in_=wq3[:, kc, :, 128:160])],
                 reads=[Wuq], writes=[Wr2])
        gqa = A.tile([4], F32)
        P.dma(sp, gqa.t, g_q_a.rearrange("o (k p) -> p (o k)", p=128), writes=[gqa])
        gqn = A.tile([1], F32)
        P.dma(sp, gqn.t, g_q_nope.rearrange("o d -> d o"), writes=[gqn])
        gq2 = A.tile([1], F32)
        gqr_col = g_q_rope.rearrange("o d -> d o")
        P.dma(sp, gq2.t[0:64, :], gqr_col[0:64, :], writes=[gq2])
        P.dma(sp, gq2.t[64:96, :], gqr_col[32:64, :], writes=[gq2])
        P.dma(sp, gq2.t[96:128, :], gqr_col[0:32, :], writes=[gq2])
        P.op(dve, lambda e: e.tensor_scalar(out=gqn.t, in0=gqn.t, scalar1=SCALE, scalar2=0.0, op0=ALU.mult, op1=ALU.add), reads=[gqn], writes=[gqn])
        P.op(dve, lambda e: e.tensor_scalar(out=gq2.t, in0=gq2.t, scalar1=SCALE, scalar2=0.0, op0=ALU.mult, op1=ALU.add), reads=[gq2], writes=[gq2])
        kb = KvBufs()
        hTs = [A.tile([16, 512], BF16) for _ in range(2)]
        sqs = [A.tile([512], BF16, dsem=False) for _ in range(2)]
        rss = [A.tile([512], F32, dsem=False) for _ in range(2)]
        tmps = [A.tile([512], F32, dsem=False) for _ in range(2)]
        sqs4 = [A.tile([512], BF16, dsem=False) for _ in range(4)]
        rss4 = [A.tile([512], F32, dsem=False) for _ in range(4)]
        qaT = A.tile([4, 512], BF16, dsem=False)
        QTs = [A.tile([NH, 512], BF16) for _ in range(1)]
        QrTs = [A.tile([NH, 512], BF16) for _ in range(1)]
        tws = [A.tile([512], F32) for _ in range(2)]
        cko = [A.tile([KVL], F32) for _ in range(2)]
        kro = [A.tile([ROPE], F32) for _ in range(2)]
        oi = 0
        qrot = Rot(PS[0:4])
        orot = Rot(PS[4:8])
        for (bi, row0, T, nt, R, is_s) in blocks():
            hT, QT, QrT, tw = hTs[bi % 2], QTs[0], QrTs[0], tws[bi % 2]
            P.dma(sp, hT.t[:, :, 0:T], hT_d[:, :, row0:row0 + T], writes=[hT])
            P.dma(sp, tw.t[:, 0:T], town[:, row0:row0 + T], writes=[tw])
            for t in range(nt):
                ck, kr_ = cko[oi % 2], kro[oi % 2]
                oi += 1
                r0 = row0 + t * 128
                zkv_tile(kb, Wkv, hT, t * 128, R, ropeo[r0:r0 + R, :], ck, kr_)
                P.dma(pool, lat_all[r0:r0 + R, :], ck.t[0:R, :], reads=[ck])
                P.dma(pool, kr_all[r0:r0 + R, :], kr_.t[0:R, :], reads=[kr_])
            pq = [qrot.get() for _ in range(4)]
            pss = orot.get()
            for kc in range(4):
                P.op(pe, [lambda e, k=k, kc=kc: e.matmul(pq[kc].t[:, 0:T], Wq.t[:, k, kc * 128:(kc + 1) * 128], hT.t[:, k, 0:T], start=(k == 0), stop=(k == 15)) for k in range(16)],
                     reads=[Wq, hT], writes=[pq[kc]])
            for kc in range(4):
                sq = sqs[kc % 2]
                P.op(act, lambda e, kc=kc, sq=sq: e.activation(out=sq.t[:, 0:T], in_=pq[kc].t[:, 0:T], func=AF.Square), reads=[pq[kc]], writes=[sq])
                if kc == 0:
                    P.deps(pe, [], [pss])
                P.op(pe, lambda e, kc=kc, sq=sq: e.matmul(pss.t[:, 0:T], ones_b.t, sq.t[:, 0:T], start=(kc == 0), stop=(kc == 3)), reads=[ones_b, sq], writes=[] if kc < 3 else [pss])
            rs = rss[0]
            rstd_tile(pss, rs, T, QL)
            for kc in range(4):
                P.op(dve, lambda e, kc=kc: e.scalar_tensor_tensor(out=qaT.t[:, kc, 0:T], in0=pq[kc].t[:, 0:T], scalar=gqa.t[:, kc:kc + 1], in1=rs.t[:, 0:T], op0=ALU.mult, op1=ALU.mult),
                     reads=[pq[kc], gqa, rs], writes=[qaT])
            def q_mm(h):
                pa = qrot.get()
                pab = qrot.get()
                P.op(pe, [lambda e, kc=kc: e.matmul(pa.t[:, 0:T], Wuq.t[:, kc, h * QKH:h * QKH + 128], qaT.t[:, kc, 0:T], start=(kc == 0), stop=(kc == 3)) for kc in range(4)],
                     reads=[Wuq, qaT], writes=[pa])
                P.op(pe, [lambda e, kc=kc: e.matmul(pab.t[:, 0:T], Wr2.t[:, kc, h, :], qaT.t[:, kc, 0:T], start=(kc == 0), stop=(kc == 3)) for kc in range(4)],
                     reads=[Wr2, qaT], writes=[pab])
                return pa, pab

            def q_rest(h, pa, pab):
                sq, sq2, rs1, rs2, tmp = sqs4[(h % 2) * 2], sqs4[(h % 2) * 2 + 1], rss4[(h % 2) * 2], rss4[(h % 2) * 2 + 1], tmps[h % 2]
                P.op(act, lambda e: e.activation(out=sq.t[:, 0:T], in_=pa.t[:, 0:T], func=AF.Square), reads=[pa], writes=[sq])
                P.op(act, lambda e: e.activation(out=sq2.t[0:64, 0:T], in_=pab.t[0:64, 0:T], func=AF.Square), reads=[pab], writes=[sq2])
                ps1 = orot.get()
                ps2 = orot.get()
                P.op(pe, lambda e: e.matmul(ps1.t[:, 0:T], ones_b.t, sq.t[:, 0:T], start=True, stop=True), reads=[ones_b, sq], writes=[ps1])
                P.op(pe, lambda e: e.matmul(ps2.t[:, 0:T], ones_b.t[0:64, :], sq2.t[0:64, 0:T], start=True, stop=True), reads=[ones_b, sq2], writes=[ps2])
                rstd_tile(ps1, rs1, T, 128)
                rstd_tile(ps2, rs2, T, ROPE)
                P.op(dve, lambda e: e.scalar_tensor_tensor(out=QT.t[:, h, 0:T], in0=pa.t[:, 0:T], scalar=gqn.t[:, 0:1], in1=rs1.t[:, 0:T], op0=ALU.mult, op1=ALU.mult),
                     reads=[pa, gqn, rs1], writes=[QT])
                P.op(dve, lambda e: e.scalar_tensor_tensor(out=tmp.t[:, 0:T], in0=pab.t[:, 0:T], scalar=gq2.t[:, 0:1], in1=rs2.t[:, 0:T], op0=ALU.mult, op1=ALU.mult),
                     reads=[pab, gq2, rs2], writes=[tmp])
                P.op(dve, lambda e: e.tensor_tensor(out=QrT.t[:, h, 0:T], in0=tmp.t[:, 0:T], in1=tw.t[:, 0:T], op=ALU.mult),
                     reads=[tmp, tw], writes=[QrT])

            pend = {}
            for i in range(NH + 1):
                if i < NH:
                    pend[i] = q_mm(i)
                if i - 1 >= 0:
                    q_rest(i - 1, *pend.pop(i - 1))
            P.dma(pool, QT_d.rearrange("h d t -> d h t")[:, :, row0:row0 + T], QT.t[:, :, 0:T], reads=[QT])
            P.dma(pool, QrT_d.rearrange("h d t -> d h t")[:, :, row0:row0 + T], QrT.t[:, :, 0:T], reads=[QrT])
        P.barrier()
        A.reset(PERSIST)

        eb = ExpBufs(512)
        cks = [A.tile([KVL], F32) for _ in range(8)]
        krs_ = [A.tile([ROPE], F32) for _ in range(8)]
        gi = 0
        for i in range(SB):
            for t0 in range(0, NKT, 4):
                ntl = min(4, NKT - t0)
                tl = []
                for t in range(ntl):
                    ck, kr_ = cks[(gi % 2) * 4 + t], krs_[(gi % 2) * 4 + t]
                    r0 = (t0 + t) * 128
                    P.dma(sp, ck.t, clat[i, r0:r0 + 128, :], writes=[ck])
                    P.dma(sp, kr_.t, ckr[i, r0:r0 + 128, :], writes=[kr_])
                    tl.append((ck, kr_, 128))
                gi += 1
                T = ntl * 128
                expand_group(eb, tl, T,
                             KTs_d[i].rearrange("h d t -> d h t")[:, :, t0 * 128:t0 * 128 + T],
                             KrTs_d[i][:, t0 * 128:t0 * 128 + T],
                             Vs_d[i].rearrange("h p t c -> p h t c")[:, :, t0:t0 + ntl, :])
            ck, kr_ = cks[(gi % 2) * 4], krs_[(gi % 2) * 4]
            gi += 1
            r0 = NOWN + i * TS
            P.dma(sp, ck.t[0:TS, :], lat_all[r0:r0 + TS, :], writes=[ck])
            P.dma(sp, kr_.t[0:TS, :], kr_all[r0:r0 + TS, :], writes=[kr_])
            expand_group(eb, [(ck, kr_, TS)], TS,
                         KTs_d[i].rearrange("h d t -> d h t")[:, :, PAST:PASTP],
                         KrTs_d[i][:, PAST:PASTP],
                         Vs_d[i].rearrange("h p t c -> p h t c")[0:TS, :, NKT:NKT + 1, :])
        P.barrier()
        A.reset(PERSIST)

        wsT = A.tile([NH, 128], BF16, dsem=False)
        wsblk = A.tile([NH, NS], BF16)
        bsb = A.tile([NH, 128], F32)
        bss = A.tile([NH, NS], F32, dsem=False)
        trl = A.tile([128], F32)
        wsf = [A.tile([128], F32) for _ in range(2)]
        wsm = [A.tile([128], BF16, dsem=False) for _ in range(2)]
        P.dma(sp, trl.t, trild[:, :], writes=[trl])
        P.dma(sp, bsb.t.rearrange("p g i -> p (g i)"), b_s[0:1, :].partition_broadcast(128), writes=[bsb])
        for g in range(NH):
            wf, wm = wsf[g % 2], wsm[g % 2]
            P.dma(sp, wf.t, w_s[g, :, :], writes=[wf])
            P.op(dve, lambda e, wf=wf, wm=wm: e.tensor_tensor(out=wm.t, in0=wf.t, in1=trl.t, op=ALU.mult), reads=[wf, trl], writes=[wm])
            for k0, kk, ps, pv in transpose_to(None, wm, 128, 1, [], None):
                P.op(dve, lambda e, pv=pv, g=g: e.tensor_copy(out=wsT.t[:, g, :], in_=pv[:, 0:128]), reads=[ps], writes=[wsT])
        P.op(dve, lambda e: e.memset(wsblk.t, 0.0), writes=[wsblk])
        for i in range(SB):
            P.dma(sp, wsblk.t[16 * i:16 * i + 16, :, 16 * i:16 * i + 16], wsT.t[0:16, :, 0:16], reads=[wsT], writes=[wsblk])
            P.op(dve, lambda e, i=i: e.tensor_copy(out=bss.t[:, :, 16 * i:16 * i + 16], in_=bsb.t[:, :, 0:16]), reads=[bsb], writes=[bss])
        uTs = [A.tile([16, 512], BF16) for _ in range(2)]
        vts = [A.tile([4, D], BF16) for _ in range(2)]
        osg = [A.tile([16, 512], BF16) for _ in range(2)]
        tmpf = [A.tile([512], F32, dsem=False) for _ in range(2)]
        for (bi, row0, T, nt, R, is_s) in blocks():
            uT, vt, o = uTs[bi % 2], vts[bi % 2], osg[bi % 2]
            P.dma(sp, uT.t[:, :, 0:T], uT_d[:, :, row0:row0 + T], writes=[uT])
            for t in range(nt):
                P.dma(sp, vt.t[0:R, t, :], v_d[row0 + t * 128:row0 + t * 128 + R, :], writes=[vt])
            for g in range(NH):
                ps = psrot.get()
                tm = tmpf[g % 2]
                if not is_s:
                    P.op(pe, [lambda e, t=t, g=g, ps=ps, vt=vt: e.matmul(ps.t[:, t * 128:(t + 1) * 128], vt.t[:, t, g * 128:(g + 1) * 128], wsT.t[:, g, :], start=True, stop=True) for t in range(4)],
                         reads=[vt, wsT], writes=[ps])
                    P.op(dve, lambda e, ps=ps, tm=tm, g=g: e.tensor_tensor(out=tm.t.rearrange("p (t i) -> p t i", t=4, i=128), in0=ps.t.rearrange("p (t i) -> p t i", t=4, i=128),
                                                                     in1=bsb.t[:, g, :].unsqueeze(1).to_broadcast([128, 4, 128]), op=ALU.add), reads=[ps, bsb], writes=[tm])
                else:
                    P.op(pe, lambda e, g=g, ps=ps, vt=vt: e.matmul(ps.t[:, 0:NS], vt.t[0:NS, 0, g * 128:(g + 1) * 128], wsblk.t[0:NS, g, :], start=True, stop=True),
                         reads=[vt, wsblk], writes=[ps])
                    P.op(dve, lambda e, ps=ps, tm=tm, g=g: e.tensor_tensor(out=tm.t[:, 0:NS], in0=ps.t[:, 0:NS], in1=bss.t[:, g, :], op=ALU.add), reads=[ps, bss], writes=[tm])
                P.op(dve, lambda e, tm=tm, g=g, uT=uT, o=o: e.tensor_tensor(out=o.t[:, g, 0:T], in0=tm.t[:, 0:T], in1=uT.t[:, g, 0:T], op=ALU.mult), reads=[tm, uT], writes=[o])
            P.dma(pool, osgT_d[:, :, row0:row0 + T], o.t[:, :, 0:T], reads=[o])
        P.barrier()
        A.reset(PERSIST)

        msk = A.tile([16, 8], F32)
        mskb = A.tile([16, 8], BF16, dsem=False)
        P.dma(sp, msk.t.rearrange("p a b -> p (a b)"), maskc[:, :], writes=[msk])
        P.op(dve, lambda e: e.tensor_copy(out=mskb.t, in_=msk.t), reads=[msk], writes=[mskb])
        CH = 8
        Kc = [A.tile([CH * 128], BF16) for _ in range(4)]
        Krc = [A.tile([CH * 128], BF16) for _ in range(4)]
        Vc = [A.tile([CH, 128], BF16) for _ in range(4)]
        Qs = [A.tile([512], BF16) for _ in range(2)]
        Qrs = [A.tile([512], BF16) for _ in range(2)]
        pTs = [A.tile([512], BF16, dsem=False) for _ in range(4)]
        ptmp = [A.tile([512], BF16, dsem=False) for _ in range(4)]
        acc = [A.tile([512], F32, dsem=False) for _ in range(4)]
        rinv = [A.tile([512], F32, dsem=False) for _ in range(2)]
        omla = [A.tile([NH, 512], BF16) for _ in range(2)]
        srot = Rot(PS[0:5])
        orot2 = Rot(PS[5:7])
        rrot = Rot(PS[7:8])
        ci = 0
        hi = 0
        pi = 0

        LOOK = 2
        jobs = []
        for (bi, row0, T, nt, R, is_s) in blocks():
            if not is_s:
                ctx = 4 * (4 * bi + 4)
                for h in range(NH):
                    jobs.append(dict(bi=bi, row0=row0, T=T, QT=QT_d[h], QrT=QrT_d[h], q0=row0, NQ=512, ctx=ctx, last=128,
                                     KT=KT_d[h], KrT=KrT_d, V=V_d[h], h=h, ocol=0, tail=ctx - 16, store=(h == NH - 1)))
            else:
                for i in range(SB):
                    for h in range(NH):
                        jobs.append(dict(bi=bi, row0=row0, T=T, QT=QT_d[h], QrT=QrT_d[h], q0=row0 + i * TS, NQ=TS, ctx=NKT + 1, last=TS,
                                         KT=KTs_d[i, h], KrT=KrTs_d[i], V=Vs_d[i, h], h=h, ocol=i * TS, tail=None,
                                         store=(i == SB - 1 and h == NH - 1)))
        chunks = []
        tiles = []
        for ji, jb in enumerate(jobs):
            for c0 in range(0, jb["ctx"], CH):
                nt_ = min(CH, jb["ctx"] - c0)
                chunks.append(dict(job=ji, c0=c0, nt=nt_, first=(c0 == 0)))
                for t in range(nt_):
                    kt = c0 + t
                    tiles.append(dict(job=ji, chunk=len(chunks) - 1, t=t, kt=kt, RK=(jb["last"] if kt == jb["ctx"] - 1 else 128),
                                      firstc=(t == 0), first=(kt == 0), last=(kt == jb["ctx"] - 1)))

        def prefetch(cj):
            if cj >= len(chunks):
                return
            ch = chunks[cj]
            jb = jobs[ch["job"]]
            if ch["first"]:
                jb["Q"], jb["Qr"] = Qs[ch["job"] % 2], Qrs[ch["job"] % 2]
                jb["acs"] = (acc[(ch["job"] % 2) * 2], acc[(ch["job"] % 2) * 2 + 1])
                jb["ri"] = rinv[ch["job"] % 2]
                jb["nacc"] = [0, 0]
                jb["om"] = omla[jb["bi"] % 2]
                P.dma(sp, jb["Q"].t[:, 0:jb["NQ"]], jb["QT"][:, jb["q0"]:jb["q0"] + jb["NQ"]], writes=[jb["Q"]])
                P.dma(sp, jb["Qr"].t[:, 0:jb["NQ"]], jb["QrT"][:, jb["q0"]:jb["q0"] + jb["NQ"]], writes=[jb["Qr"]])
            kc_, krc_, vc_ = Kc[cj % 4], Krc[cj % 4], Vc[cj % 4]
            ch["bufs"] = (kc_, krc_, vc_)
            c0, nt_ = ch["c0"], ch["nt"]
            lastc = (c0 + nt_ == jb["ctx"])
            nkeys = (nt_ - 1) * 128 + (jb["last"] if lastc else 128)
            P.dma(sp, kc_.t[:, 0:nkeys], jb["KT"][:, c0 * 128:c0 * 128 + nkeys], writes=[kc_])
            P.dma(sp, krc_.t[:, 0:nkeys], jb["KrT"][:, c0 * 128:c0 * 128 + nkeys], writes=[krc_])
            if lastc and jb["last"] < 128:
                if nt_ > 1:
                    P.dma(sp, vc_.t[:, 0:nt_ - 1, :], jb["V"][:, c0:c0 + nt_ - 1, :], writes=[vc_])
                P.dma(sp, vc_.t[0:jb["last"], nt_ - 1:nt_, :], jb["V"][0:jb["last"], c0 + nt_ - 1:c0 + nt_, :], writes=[vc_])
            else:
                P.dma(sp, vc_.t[:, 0:nt_, :], jb["V"][:, c0:c0 + nt_, :], writes=[vc_])

        def emit_S(ti_):
            tl = tiles[ti_]
            jb = jobs[tl["job"]]
            if tl["firstc"]:
                if tl["chunk"] == 0:
                    prefetch(0)
                prefetch(tl["chunk"] + 1)
            kc_, krc_, vc_ = chunks[tl["chunk"]]["bufs"]
            NQ, RK, t = jb["NQ"], tl["RK"], tl["t"]
            pss_ = srot.get()
            tl["pss"] = pss_
            Q, Qr = jb["Q"], jb["Qr"]
            P.op(pe, [lambda e: e.matmul(pss_.t[0:RK, 0:NQ], kc_.t[:, t * 128:t * 128 + RK], Q.t[:, 0:NQ], start=True, stop=False),
                      lambda e: e.matmul(pss_.t[0:RK, 0:NQ], krc_.t[:, t * 128:t * 128 + RK], Qr.t[:, 0:NQ], start=False, stop=True)],
                 reads=[kc_, krc_, Q, Qr], writes=[pss_])

        pcount = [0]

        def emit_rest(ti_):
            tl = tiles[ti_]
            jb = jobs[tl["job"]]
            kc_, krc_, vc_ = chunks[tl["chunk"]]["bufs"]
            NQ, RK, t, kt = jb["NQ"], tl["RK"], tl["t"], tl["kt"]
            pss_ = tl["pss"]
            pT = pTs[pcount[0] % 4]
            pcount[0] += 1
            P.op(act, lambda e: e.activation(out=pT.t[0:RK, 0:NQ], in_=pss_.t[0:RK, 0:NQ], func=AF.Exp), reads=[pss_], writes=[pT])
            if jb["tail"] is not None and kt >= jb["tail"]:
                kbt = kt - jb["tail"]
                P.op(dve, lambda e: e.tensor_tensor(out=pT.t.rearrange("p (a b) -> p a b", a=8, b=64), in0=pT.t.rearrange("p (a b) -> p a b", a=8, b=64),
                                                    in1=mskb.t[:, kbt, :].unsqueeze(2).to_broadcast([128, 8, 64]), op=ALU.mult), reads=[pT, mskb], writes=[pT])
            if tl["first"]:
                jb["pso"] = orot2.get()
                P.deps(pe, [], [jb["pso"]])
            pso = jb["pso"]
            first, last = tl["first"], tl["last"]
            P.op(pe, lambda e: e.matmul(pso.t[:, 0:NQ], vc_.t[0:RK, t, :], pT.t[0:RK, 0:NQ], start=first, stop=last),
                 reads=[vc_, pT], writes=[pso] if last else [])
            if tl["first"]:
                for ac0 in jb["acs"]:
                    P.op(dve, lambda e, ac0=ac0: e.memset(ac0.t[:, 0:NQ], 0.0), writes=[ac0])

            def acc_add(src, pidx):
                ac = jb["acs"][pidx % 2]
                rr = 128 if src is not pT else RK
                P.op(dve, lambda e: e.tensor_tensor(out=ac.t[0:rr, 0:NQ], in0=ac.t[0:rr, 0:NQ], in1=src.t[0:rr, 0:NQ], op=ALU.add), reads=[src, ac], writes=[ac])
                jb["nacc"][pidx % 2] += 1

            if kt % 2 == 0:
                if last or RK < 128:
                    acc_add(pT, kt // 2)
                else:
                    jb["prevpT"] = pT
            else:
                if RK < 128:
                    acc_add(jb["prevpT"], kt // 2)
                    acc_add(pT, kt // 2 + 1)
                else:
                    pp = jb["prevpT"]
                    tb = ptmp[(pcount[0] // 2) % 4]
                    P.op(dve, lambda e: e.tensor_tensor(out=tb.t[:, 0:NQ], in0=pp.t[:, 0:NQ], in1=pT.t[:, 0:NQ], op=ALU.add), reads=[pp, pT], writes=[tb])
                    acc_add(tb, kt // 2)
            if last:
                acs, ri, om, h, ocol = jb["acs"], jb["ri"], jb["om"], jb["h"], jb["ocol"]
                psr_ = rrot.get()
                P.op(pe, [lambda e: e.matmul(psr_.t[:, 0:NQ], ones_f.t, acs[0].t[:, 0:NQ], start=True, stop=False),
                          lambda e: e.matmul(psr_.t[:, 0:NQ], ones_f.t, acs[1].t[:, 0:NQ], start=False, stop=True)], reads=[ones_f, acs[0], acs[1]], writes=[psr_])
                P.op(act, lambda e: e.activation(out=ri.t[:, 0:NQ], in_=psr_.t[:, 0:NQ], func=AF.Ln), reads=[psr_], writes=[ri])
                P.op(act, lambda e: e.activation(out=ri.t[:, 0:NQ], in_=ri.t[:, 0:NQ], func=AF.Exp, scale=-1.0), reads=[ri], writes=[ri])
                P.op(dve, lambda e: e.tensor_tensor(out=om.t[:, h, ocol:ocol + NQ], in0=pso.t[:, 0:NQ], in1=ri.t[:, 0:NQ], op=ALU.mult), reads=[pso, ri], writes=[om])
                if jb["store"]:
                    P.dma(pool, omlaT_d[:, :, jb["row0"]:jb["row0"] + jb["T"]], om.t[:, :, 0:jb["T"]], reads=[om])

        NTL = len(tiles)
        for i in range(NTL + LOOK):
            if i < NTL:
                emit_S(i)
            if i - LOOK >= 0:
                emit_rest(i - LOOK)
        P.barrier()
        A.reset(PERSIST)

        for grp in range(2):
            Wa = A.tile([16, 1024], BF16)
            Wb = A.tile([16, 1024], BF16)
            P.dma(pool, Wa.t, wview(w_pa, grp * 1024, (grp + 1) * 1024), writes=[Wa])
            P.dma(pool, Wb.t, wview(w_pb, grp * 1024, (grp + 1) * 1024), writes=[Wb])
            ia = [A.tile([16, 512], BF16) for _ in range(2)]
            ib = [A.tile([16, 512], BF16) for _ in range(2)]
            ga = [A.tile([8, 512], BF16) for _ in range(2)]
            gb = [A.tile([8, 512], BF16) for _ in range(2)]
            mo = [A.tile([8, 512], BF16) for _ in range(2)]
            t1 = [A.tile([512], F32, dsem=False) for _ in range(2)]
            t2 = [A.tile([512], F32, dsem=False) for _ in range(2)]
            for (bi, row0, T, nt, R, is_s) in blocks():
                a_, b_, ga_, gb_, mo_ = ia[bi % 2], ib[bi % 2], ga[bi % 2], gb[bi % 2], mo[bi % 2]
                P.dma(sp, a_.t[:, :, 0:T], osgT_d[:, :, row0:row0 + T], writes=[a_])
                P.dma(sp, b_.t[:, :, 0:T], omlaT_d[:, :, row0:row0 + T], writes=[b_])
                P.dma(sp, ga_.t[:, :, 0:T], gaT_d[:, grp * 8:(grp + 1) * 8, row0:row0 + T], writes=[ga_])
                P.dma(sp, gb_.t[:, :, 0:T], gbT_d[:, grp * 8:(grp + 1) * 8, row0:row0 + T], writes=[gb_])
                for j in range(8):
                    pa = psrot.get()
                    pb = psrot.get()
                    P.op(pe, [lambda e, k=k, j=j, pa=pa, a_=a_: e.matmul(pa.t[:, 0:T], Wa.t[:, k, j * 128:(j + 1) * 128], a_.t[:, k, 0:T], start=(k == 0), stop=(k == 15)) for k in range(16)],
                         reads=[Wa, a_], writes=[pa])
                    P.op(pe, [lambda e, k=k, j=j, pb=pb, b_=b_: e.matmul(pb.t[:, 0:T], Wb.t[:, k, j * 128:(j + 1) * 128], b_.t[:, k, 0:T], start=(k == 0), stop=(k == 15)) for k in range(16)],
                         reads=[Wb, b_], writes=[pb])
                    x1_, x2_ = t1[j % 2], t2[j % 2]
                    P.op(dve, lambda e, pa=pa, x1_=x1_, ga_=ga_, j=j: e.tensor_tensor(out=x1_.t[:, 0:T], in0=pa.t[:, 0:T], in1=ga_.t[:, j, 0:T], op=ALU.mult), reads=[pa, ga_], writes=[x1_])
                    P.op(dve, lambda e, pb=pb, x2_=x2_, gb_=gb_, j=j: e.tensor_tensor(out=x2_.t[:, 0:T], in0=pb.t[:, 0:T], in1=gb_.t[:, j, 0:T], op=ALU.mult), reads=[pb, gb_], writes=[x2_])
                    P.op(dve, lambda e, x1_=x1_, x2_=x2_, mo_=mo_, j=j: e.tensor_tensor(out=mo_.t[:, j, 0:T], in0=x1_.t[:, 0:T], in1=x2_.t[:, 0:T], op=ALU.add), reads=[x1_, x2_], writes=[mo_])
                P.dma(pool, mT_d[:, grp * 8:(grp + 1) * 8, row0:row0 + T], mo_.t[:, :, 0:T], reads=[mo_])
            P.barrier()
            A.reset(PERSIST)

        Wo = A.tile([16, D], BF16)
        P.dma(pool, Wo.t[:, :, 0:1024], wview(w_o, 0, 1024), writes=[Wo])
        P.dma(pool, Wo.t[:, :, 1024:2048], wview(w_o, 1024, 2048), writes=[Wo])
        gn2 = A.tile([D], F32)
        G2, S2 = make_front_rows(g_norm2, 3 * D, 4 * D, False, gn2)
        g1r = A.tile([D], F32)
        load_rows_prompt_or_sample(g1r, mod_d[0:1, 2 * D:3 * D], None, False, D)
        fb = FrontBufs()
        mTs = [A.tile([16, 512], BF16) for _ in range(1)]
        xts = [A.tile([D], F32) for _ in range(2)]
        h2s = [A.tile([16, 512], BF16) for _ in range(1)]
        tm5 = [A.tile([512], F32, dsem=False) for _ in range(2)]
        xi = 0
        for (bi, row0, T, nt, R, is_s) in blocks():
            if is_s:
                reload_front_rows(G2, S2, g_norm2, 3 * D, 4 * D, gn2)
                load_rows_prompt_or_sample(g1r, None, [mod_d[1 + i:2 + i, 2 * D:3 * D] for i in range(SB)], True, D)
            mT, h2 = mTs[0], h2s[0]
            P.dma(sp, mT.t[:, :, 0:T], mT_d[:, :, row0:row0 + T], writes=[mT])
            for t in range(nt):
                xt = xts[xi % 2]
                xi += 1
                r0 = row0 + t * 128
                P.dma(sp, xt.t[0:R, :], xo[r0:r0 + R, :], writes=[xt])
                for c4 in range(4):
                    ps = psrot.get()
                    tm = tm5[c4 % 2]
                    P.op(pe, [lambda e, k=k, ps=ps, mT=mT, t=t, c4=c4: e.matmul(ps.t[0:R, :], mT.t[:, k, t * 128:t * 128 + R], Wo.t[:, k, c4 * 512:(c4 + 1) * 512], start=(k == 0), stop=(k == 15)) for k in range(16)],
                         reads=[Wo, mT], writes=[ps])
                    P.op(dve, lambda e, ps=ps, tm=tm, c4=c4: e.tensor_tensor(out=tm.t[0:R, :], in0=ps.t[0:R, :], in1=g1r.t[0:R, c4 * 512:(c4 + 1) * 512], op=ALU.mult), reads=[ps, g1r], writes=[tm])
                    P.op(dve, lambda e, tm=tm, xt=xt, c4=c4: e.tensor_tensor(out=xt.t[0:R, c4 * 512:(c4 + 1) * 512], in0=xt.t[0:R, c4 * 512:(c4 + 1) * 512], in1=tm.t[0:R, :], op=ALU.add), reads=[tm, xt], writes=[xt])
                P.dma(pool, x1_d[r0:r0 + R, :], xt.t[0:R, :], reads=[xt])
                front_tile(fb, xt, R, G2, S2, h2, t * 128)
            P.dma(pool, h2T_d[:, :, row0:row0 + T], h2.t[:, :, 0:T], reads=[h2])
        P.barrier()
        A.reset(PERSIST)

        for grp in range(4):
            Wu = A.tile([16, D], BF16)
            P.dma(pool, Wu.t[:, :, 0:1024], wview(w_up, grp * D, grp * D + 1024), writes=[Wu])
            P.dma(pool, Wu.t[:, :, 1024:2048], wview(w_up, grp * D + 1024, (grp + 1) * D), writes=[Wu])
            h2s = [A.tile([16, 512], BF16) for _ in range(2)]
            hid = [A.tile([16, 512], BF16) for _ in range(2)]
            sqf = [A.tile([512], F32, dsem=False) for _ in range(2)]
            for (bi, row0, T, nt, R, is_s) in blocks():
                h2, hd = h2s[bi % 2], hid[bi % 2]
                P.dma(sp, h2.t[:, :, 0:T], h2T_d[:, :, row0:row0 + T], writes=[h2])
                for j in range(16):
                    ps = psrot.get()
                    sq = sqf[j % 2]
                    P.op(pe, [lambda e, k=k, j=j, ps=ps, h2=h2: e.matmul(ps.t[:, 0:T], Wu.t[:, k, j * 128:(j + 1) * 128], h2.t[:, k, 0:T], start=(k == 0), stop=(k == 15)) for k in range(16)],
                         reads=[Wu, h2], writes=[ps])
                    P.op(act, lambda e, ps=ps, sq=sq: e.activation(out=sq.t[:, 0:T], in_=ps.t[:, 0:T], func=AF.Square), reads=[ps], writes=[sq])
                    P.op(dve, lambda e, ps=ps, sq=sq, hd=hd, j=j: e.scalar_tensor_tensor(out=hd.t[:, j, 0:T], in0=ps.t[:, 0:T], scalar=0.0, in1=sq.t[:, 0:T], op0=ALU.is_gt, op1=ALU.mult),
                         reads=[ps, sq], writes=[hd])
                P.dma(pool, hidT_d[:, grp * 16:(grp + 1) * 16, row0:row0 + T], hd.t[:, :, 0:T], reads=[hd])
            P.barrier()
            A.reset(PERSIST)

        g2r = A.tile([D], F32)
        load_rows_prompt_or_sample(g2r, mod_d[0:1, 5 * D:6 * D], None, False, D)
        g2s = A.tile([D], F32)
        load_rows_prompt_or_sample(g2s, None, [mod_d[1 + i:2 + i, 5 * D:6 * D] for i in range(SB)], True, D)
        WBASE = A.off
        for grp in range(4):
            A.reset(WBASE)
            Wd = A.tile([64, 512], BF16)
            for q in range(4):
                P.dma(pool, Wd.t[:, q * 16:(q + 1) * 16, :], wview(w_down, grp * 512, (grp + 1) * 512)[:, q * 16:(q + 1) * 16, :], writes=[Wd])
            hts = [A.tile([64, 256], BF16) for _ in range(2)]
            x1s = [A.tile([512], F32) for _ in range(2)]
            tm7 = [A.tile([512], F32, dsem=False) for _ in range(2)]
            ti = 0
            hi7 = 0
            for (bi, row0, T, nt, R, is_s) in blocks():
                gr = g2s if is_s else g2r
                TH = 256 if not is_s else NS
                for half in range(T // TH):
                    ht = hts[hi7 % 2]
                    hi7 += 1
                    c0 = row0 + half * TH
                    for q in range(2):
                        P.dma(sp, ht.t[:, q * 32:(q + 1) * 32, 0:TH], hidT_d[:, q * 32:(q + 1) * 32, c0:c0 + TH], writes=[ht])
                    for t in range(TH // R):
                        x1t, tm = x1s[ti % 2], tm7[ti % 2]
                        ti += 1
                        r0 = c0 + t * R
                        P.dma(sp, x1t.t[0:R, :], x1_d[r0:r0 + R, grp * 512:(grp + 1) * 512], writes=[x1t])
                        ps = psrot.get()
                        P.op(pe, [lambda e, k=k, ps=ps, ht=ht, t=t: e.matmul(ps.t[0:R, :], ht.t[:, k, t * R:(t + 1) * R], Wd.t[:, k, :], start=(k == 0), stop=(k == 63)) for k in range(64)],
                             reads=[Wd, ht], writes=[ps])
                        P.op(dve, lambda e, ps=ps, tm=tm, gr=gr, grp=grp: e.tensor_tensor(out=tm.t[0:R, :], in0=ps.t[0:R, :], in1=gr.t[0:R, grp * 512:(grp + 1) * 512], op=ALU.mult), reads=[ps, gr], writes=[tm])
                        P.op(dve, lambda e, tm=tm, x1t=x1t: e.tensor_tensor(out=x1t.t[0:R, :], in0=x1t.t[0:R, :], in1=tm.t[0:R, :], op=ALU.add), reads=[tm, x1t], writes=[x1t])
                        P.dma(pool, y_all[r0:r0 + R, grp * 512:(grp + 1) * 512], x1t.t[0:R, :], reads=[x1t])
            P.barrier()
        P.barrier()

        with nc.allow_non_contiguous_dma(reason="small per-partition column loads"), nc.Block() as block:
            @block.tensor
            def _(e):
                P.replay(pe, e)

            @block.scalar
            def _(e):
                P.replay(act, e)

            @block.vector
            def _(e):
                P.replay(dve, e)

            @block.gpsimd
            def _(e):
                P.replay(pool, e)

            @block.sync
            def _(e):
                P.replay(sp, e)
    return nc


_CACHE = {}


def _rope_tables(pos):
    inv = (np.float32(10000.0) ** (-(np.arange(0, ROPE, 2, dtype=np.float32) / np.float32(ROPE)))).astype(np.float32)
    ang = (pos.astype(np.float32)[:, None] * inv[None, :]).astype(np.float32)
    return np.cos(ang).astype(np.float32), np.sin(ang).astype(np.float32)


def kernel(x_prompt, x_sample, cache_kv_latent, cache_k_rope, c_prompt, c_sample,
           w_ada, b_ada, g_norm1, g_norm2, w_in, g_sg, w_s, b_s,
           g_q_a, w_uq, g_q_nope, g_q_rope, g_kv_a, g_k_rope, w_uk, g_k_nope, w_uv,
           w_pa, w_pb, w_o, w_up, w_down):
    f = lambda a: np.ascontiguousarray(np.asarray(a, dtype=np.float32))
    x_prompt, x_sample = f(x_prompt), f(x_sample)
    B, SEQ, _ = x_prompt.shape
    DB, TS_, _ = x_sample.shape
    PAST = cache_kv_latent.shape[2]
    assert B == 2 and DB == 32 and TS_ == TS
    NB = SEQ // 512
    NJ = NB // 4
    NOWN = NJ * 512
    NS = SB * TS
    NTOK = NOWN + NS
    key = (SEQ, PAST)
    if key not in _CACHE:
        _CACHE[key] = build_program(SEQ, PAST)
    nc = _CACHE[key]

    cosb, sinb = _rope_tables(np.arange(SEQ))
    ropetm = np.concatenate([cosb, sinb], axis=1)
    shared = {
        "w_ada": f(w_ada[0]), "b_ada": f(b_ada[0])[None, :], "g_norm1": f(g_norm1[0])[None, :], "g_norm2": f(g_norm2[0])[None, :],
        "w_in": f(w_in[0]), "g_sg": f(g_sg[0])[None, :], "w_s": f(w_s[0]), "b_s": f(b_s[0]).reshape(1, -1),
        "g_q_a": f(g_q_a[0])[None, :], "w_uq": f(w_uq[0]), "g_q_nope": f(g_q_nope[0])[None, :], "g_q_rope": f(g_q_rope[0])[None, :],
        "g_kv_a": f(g_kv_a[0])[None, :], "g_k_rope": f(g_k_rope[0])[None, :], "w_uk": f(w_uk[0]).reshape(KVL, NH * 128),
        "g_k_nope": f(g_k_nope[0])[None, :], "w_uv": f(w_uv[0]).reshape(KVL, NH * 128),
        "w_pa": f(w_pa[0]), "w_pb": f(w_pb[0]), "w_o": f(w_o[0]), "w_up": f(w_up[0]), "w_down": f(w_down[0]),
        "ropetm": ropetm, "ident": np.eye(128, dtype=np.float32), "tril": np.tril(np.ones((128, 128), np.float32)),
    }
    in_maps = []
    for k in range(8):
        b, c = k // 4, k % 4
        rows = np.concatenate([np.arange((4 * J + c) * 512, (4 * J + c + 1) * 512) for J in range(NJ)])
        xo = np.concatenate([x_prompt[b][rows], x_sample[4 * k:4 * k + 4].reshape(NS, D)], axis=0)
        pos_o = np.concatenate([rows, np.tile(PAST + np.arange(TS), SB)])
        co, so = _rope_tables(pos_o)
        ropeo = np.concatenate([co, so], axis=1)
        town = np.concatenate([co.T, co.T, -so.T, so.T], axis=0)
        p = np.arange(128)[:, None, None]
        kbt = np.arange(16)[None, :, None]
        qc = np.arange(8)[None, None, :]
        mask = ((kbt * 2 + p // 64) <= (c * 8 + qc)).astype(np.float32).reshape(128, 128)
        m = dict(shared)
        m.update({
            "xb": x_prompt[b], "xo": f(xo),
            "clat": f(cache_kv_latent[0, 4 * k:4 * k + 4]), "ckr": f(cache_k_rope[0, 4 * k:4 * k + 4]),
            "cvec": f(np.concatenate([np.asarray(c_prompt)[b:b + 1], np.asarray(c_sample)[4 * k:4 * k + 4]], axis=0)),
            "ropeo": f(ropeo), "town": f(town), "maskc": f(mask),
        })
        in_maps.append(m)
    res = run_bass_kernel_spmd(nc, in_maps, core_ids=list(range(8)))
    y_p = np.zeros((B, SEQ, D), np.float32)
    y_s = np.zeros((DB, TS, D), np.float32)
    lat_p = np.zeros((1, B, SEQ, KVL), np.float32)
    kr_p = np.zeros((1, B, SEQ, ROPE), np.float32)
    lat_s = np.zeros((1, DB, TS, KVL), np.float32)
    kr_s = np.zeros((1, DB, TS, ROPE), np.float32)
    v_s = np.zeros((1, DB, TS, D), np.float32)
    for k in range(8):
        b, c = k // 4, k % 4
        r = res.results[k]
        rows = np.concatenate([np.arange((4 * J + c) * 512, (4 * J + c + 1) * 512) for J in range(NJ)])
        y_p[b, rows] = r["y_all"][:NOWN]
        lat_p[0, b, rows] = r["lat_all"][:NOWN]
        kr_p[0, b, rows] = r["kr_all"][:NOWN]
        y_s[4 * k:4 * k + 4] = r["y_all"][NOWN:].reshape(SB, TS, D)
        lat_s[0, 4 * k:4 * k + 4] = r["lat_all"][NOWN:].reshape(SB, TS, KVL)
        kr_s[0, 4 * k:4 * k + 4] = r["kr_all"][NOWN:].reshape(SB, TS, ROPE)
        v_s[0, 4 * k:4 * k + 4] = r["v_all"].reshape(SB, TS, D)
    return (y_p, y_s, lat_p, kr_p, lat_s, kr_s, v_s)
```

```python
import numpy as np
import concourse.bass as bass
import concourse.mybir as mybir
from concourse.bass_utils import run_bass_kernel_spmd

F32 = mybir.dt.float32
BF16 = mybir.dt.bfloat16
AF = mybir.ActivationFunctionType
ALU = mybir.AluOpType

D = 2048
NH = 16
QL = 512
KVL = 512
ROPE = 64
QKH = 192
OFF_Q = 4096
OFF_KV = OFF_Q + QL
OFF_GATE = OFF_KV + KVL + ROPE
IN_COLS = OFF_GATE + 2 * D
HID = 4 * D
EPS = 1e-6
SCALE = QKH ** -0.5
TS = 16
SB = 4


class _Rec:
    def __init__(self):
        self.calls = []

    def __getattr__(self, name):
        def f(*a, **kw):
            self.calls.append((name, a, kw))
            return None
        return f


class Sem:
    def __init__(self, h):
        self.h = h
        self.val = 0


class Res:
    def __init__(self, t, dsem=None):
        self.t = t
        self.w = None
        self.r = {}
        self.dsem = dsem


class Eng:
    def __init__(self, name, sem, self_sync):
        self.name = name
        self.sem = sem
        self.ops = []
        self.seen = {}
        self.self_sync = self_sync

    def wait(self, tok):
        if tok is None:
            return
        sem, val = tok
        if sem is self.sem and not self.self_sync:
            return
        if self.seen.get(id(sem), 0) >= val:
            return
        self.seen[id(sem)] = val
        self.ops.append(("w", sem.h, val))


class Prog:
    def __init__(self, nc, stack):
        self.nc = nc
        mk = lambda n: Sem(stack.enter_context(nc.semaphore(n)))
        self.pe = Eng("pe", mk("s_pe"), False)
        self.act = Eng("act", mk("s_act"), True)
        self.dve = Eng("dve", mk("s_dve"), True)
        self.pool = Eng("pool", mk("s_pool"), True)
        self.sp = Eng("sp", mk("s_sp"), True)
        self.engs = [self.pe, self.act, self.dve, self.pool, self.sp]
        self.dsems = [mk("d%d" % i) for i in range(90)]
        self.dnext = 0
        self.outstanding = {}

    def new_dsem(self):
        s = self.dsems[self.dnext % len(self.dsems)]
        self.dnext += 1
        return s

    def deps(self, E, reads, writes):
        for r in reads:
            E.wait(r.w)
        for w in writes:
            E.wait(w.w)
            for tok in list(w.r.values()):
                E.wait(tok)

    def mark(self, tok, reads, writes):
        for r in reads:
            r.r[id(tok[0])] = tok
        for w in writes:
            w.w = tok
            w.r = {}

    def op(self, E, fns, reads=(), writes=()):
        if callable(fns):
            fns = [fns]
        self.deps(E, reads, writes)
        E.sem.val += 1
        tok = (E.sem, E.sem.val)
        calls = []
        for f in fns:
            rec = _Rec()
            f(rec)
            assert len(rec.calls) == 1
            calls.append(rec.calls[0])
        for c in calls[:-1]:
            E.ops.append(("i", c, None))
        E.ops.append(("i", calls[-1], E.sem.h))
        self.mark(tok, reads, writes)

    def dma(self, Q, out, in_, reads=(), writes=()):
        self.deps(Q, reads, writes)
        sem = None
        for x in list(writes) + list(reads):
            if x.dsem is not None:
                sem = x.dsem
                break
        assert sem is not None
        sem.val += 16
        tok = (sem, sem.val)
        Q.ops.append(("d", out, in_, sem.h))
        self.mark(tok, reads, writes)
        self.outstanding[id(sem)] = tok

    def barrier(self):
        toks = [(E.sem, E.sem.val) for E in self.engs if E.sem.val > 0] + list(self.outstanding.values())
        for E in self.engs:
            for t in toks:
                E.wait(t)
        self.outstanding = {}

    def replay(self, E, e):
        for o in E.ops:
            if o[0] == "w":
                e.wait_ge(o[1], o[2])
            elif o[0] == "i":
                name, a, kw = o[1]
                ins = getattr(e, name)(*a, **kw)
                if o[2] is not None:
                    ins.then_inc(o[2], 1)
            else:
                e.dma_start(out=o[1], in_=o[2]).then_inc(o[3], 16)


class Arena:
    def __init__(self, big, nwords, prog):
        self.big = big
        self.cap = nwords
        self.off = 0
        self.prog = prog

    def reset(self, to=0):
        self.off = to

    def raw(self, n, dt):
        words = (n * (4 if dt == F32 else 2) + 3) // 4
        words = (words + 7) // 8 * 8
        assert self.off + words <= self.cap, "SBUF arena overflow %d+%d>%d" % (self.off, words, self.cap)
        v = self.big[:, self.off:self.off + words]
        self.off += words
        if dt == BF16:
            v = v.bitcast(BF16)
        return v[:, 0:n]

    def tile(self, shape, dt, dsem=True):
        n = int(np.prod(shape))
        v = self.raw(n, dt)
        if len(shape) == 2:
            v = v.rearrange("p (a b) -> p a b", a=shape[0], b=shape[1])
        elif len(shape) == 3:
            v = v.rearrange("p (a b c) -> p a b c", a=shape[0], b=shape[1], c=shape[2])
        return Res(v, self.prog.new_dsem() if dsem else None)


class Rot:
    def __init__(self, items):
        self.items = items
        self.i = 0

    def get(self):
        x = self.items[self.i % len(self.items)]
        self.i += 1
        return x


def build_program(SEQ, PAST):
    import contextlib
    nc = bass.Bass("TRN2", target_bir_lowering=False)
    NB = SEQ // 512
    NJ = NB // 4
    NOWN = NJ * 512
    NS = SB * TS
    NTOK = NOWN + NS
    NKT = PAST // 128
    PASTP = PAST + TS
    NT = SEQ // 128

    def din(name, shape, dt=F32):
        return nc.dram_tensor(name, list(shape), dt, kind="ExternalInput").ap()

    def dout(name, shape):
        return nc.dram_tensor(name, list(shape), F32, kind="ExternalOutput").ap()

    def dtmp(name, shape, dt):
        return nc.dram_tensor(name, list(shape), dt, kind="Internal").ap()

    xb = din("xb", [SEQ, D])
    xo = din("xo", [NTOK, D])
    clat = din("clat", [SB, PAST, KVL])
    ckr = din("ckr", [SB, PAST, ROPE])
    cvec = din("cvec", [5, D])
    w_ada = din("w_ada", [D, 6 * D])
    b_ada = din("b_ada", [1, 6 * D])
    g_norm1 = din("g_norm1", [1, D])
    g_norm2 = din("g_norm2", [1, D])
    w_in = din("w_in", [D, IN_COLS])
    g_sg = din("g_sg", [1, D])
    w_s = din("w_s", [NH, 128, 128])
    b_s = din("b_s", [1, NH * 128])
    g_q_a = din("g_q_a", [1, QL])
    w_uq = din("w_uq", [QL, NH * QKH])
    g_q_nope = din("g_q_nope", [1, 128])
    g_q_rope = din("g_q_rope", [1, ROPE])
    g_kv_a = din("g_kv_a", [1, KVL])
    g_k_rope = din("g_k_rope", [1, ROPE])
    w_uk = din("w_uk", [KVL, NH * 128])
    g_k_nope = din("g_k_nope", [1, 128])
    w_uv = din("w_uv", [KVL, NH * 128])
    w_pa = din("w_pa", [D, D])
    w_pb = din("w_pb", [D, D])
    w_o = din("w_o", [D, D])
    w_up = din("w_up", [D, HID])
    w_down = din("w_down", [HID, D])
    ropetm = din("ropetm", [SEQ, ROPE])
    ropeo = din("ropeo", [NTOK, ROPE])
    town = din("town", [128, NTOK])
    maskc = din("maskc", [128, 16 * 8])
    identd = din("ident", [128, 128])
    trild = din("tril", [128, 128])

    y_all = dout("y_all", [NTOK, D])
    lat_all = dout("lat_all", [NTOK, KVL])
    kr_all = dout("kr_all", [NTOK, ROPE])
    v_all = dout("v_all", [NS, D])

    mod_d = dtmp("mod_d", [5, 6 * D], F32)
    KT_d = dtmp("KT_d", [NH, 128, SEQ], BF16)
    KrT_d = dtmp("KrT_d", [128, SEQ], BF16)
    V_d = dtmp("V_d", [NH, 128, NT, 128], BF16)
    KTs_d = dtmp("KTs_d", [SB, NH, 128, PASTP], BF16)
    KrTs_d = dtmp("KrTs_d", [SB, 128, PASTP], BF16)
    Vs_d = dtmp("Vs_d", [SB, NH, 128, NKT + 1, 128], BF16)
    hT_d = dtmp("hT_d", [128, 16, NTOK], BF16)
    uT_d = dtmp("uT_d", [128, 16, NTOK], BF16)
    v_d = dtmp("v_d", [NTOK, D], BF16)
    QT_d = dtmp("QT_d", [NH, 128, NTOK], BF16)
    QrT_d = dtmp("QrT_d", [NH, 128, NTOK], BF16)
    gaT_d = dtmp("gaT_d", [128, 16, NTOK], BF16)
    gbT_d = dtmp("gbT_d", [128, 16, NTOK], BF16)
    osgT_d = dtmp("osgT_d", [128, 16, NTOK], BF16)
    omlaT_d = dtmp("omlaT_d", [128, 16, NTOK], BF16)
    mT_d = dtmp("mT_d", [128, 16, NTOK], BF16)
    h2T_d = dtmp("h2T_d", [128, 16, NTOK], BF16)
    x1_d = dtmp("x1_d", [NTOK, D], F32)
    hidT_d = dtmp("hidT_d", [128, 64, NTOK], BF16)

    AW = 51200
    with contextlib.ExitStack() as stack:
        big = stack.enter_context(nc.sbuf_tensor("arena", [128, AW], F32))
        psb = [stack.enter_context(nc.psum_tensor("ps%d" % i, [128, 512], F32)) for i in range(8)]
        P = Prog(nc, stack)
        A = Arena(big, AW, P)
        PS = [Res(psb[i][:, :]) for i in range(8)]
        pe, act, dve, pool, sp = P.pe, P.act, P.dve, P.pool, P.sp

        def psbf(res):
            return res.t.bitcast(BF16)

        identf = A.tile([128], F32)
        identb = A.tile([128], BF16, dsem=False)
        ones_b = A.tile([128], BF16, dsem=False)
        ones_f = A.tile([128], F32, dsem=False)
        P.dma(sp, identf.t, identd[:, :], writes=[identf])
        P.op(dve, lambda e: e.tensor_copy(out=identb.t, in_=identf.t), reads=[identf], writes=[identb])
        P.op(dve, lambda e: e.memset(ones_b.t, 1.0), writes=[ones_b])
        P.op(dve, lambda e: e.memset(ones_f.t, 1.0), writes=[ones_f])
        PERSIST = A.off

        def blocks():
            for j in range(NJ):
                yield (j, 512 * j, 512, 4, 128, False)
            yield (NJ, NOWN, NS, 1, NS, True)

        def wview(w, c0, c1):
            return w.rearrange("(k p) n -> p k n", p=128)[:, :, c0:c1]

        def load_rows_prompt_or_sample(tile, src_prompt_row, src_sample_rows, is_s, n):
            if not is_s:
                P.dma(sp, tile.t, src_prompt_row.partition_broadcast(128), writes=[tile])
            else:
                for i in range(SB):
                    P.dma(sp, tile.t[16 * i:16 * i + 16, :], src_sample_rows[i].partition_broadcast(16), writes=[tile])

        def rstd_col(ss, rs, R, n):
            P.op(act, lambda e: e.activation(out=rs.t[0:R, :], in_=ss.t[0:R, :], func=AF.Ln, scale=1.0 / n, bias=EPS),
                 reads=[ss], writes=[rs])
            P.op(act, lambda e: e.activation(out=rs.t[0:R, :], in_=rs.t[0:R, :], func=AF.Exp, scale=-0.5), reads=[rs], writes=[rs])

        def rstd_tile(ps, rs, T, n, parts=128):
            P.op(act, lambda e: e.activation(out=rs.t[0:parts, 0:T], in_=ps.t[0:parts, 0:T], func=AF.Ln, scale=1.0 / n, bias=EPS),
                 reads=[ps], writes=[rs])
            P.op(act, lambda e: e.activation(out=rs.t[0:parts, 0:T], in_=rs.t[0:parts, 0:T], func=AF.Exp, scale=-0.5), reads=[rs], writes=[rs])

        psrot = Rot(PS)

        def transpose_to(dst_fn, src, R, nk, reads, writes_res):
            for k0 in range(0, nk, 8):
                kk = min(8, nk - k0)
                ps = psrot.get()
                pv = psbf(ps)
                fns = []
                for k in range(kk):
                    fns.append(lambda e, k=k: e.transpose(pv[:, k * 128:k * 128 + R], src.t[0:R, (k0 + k) * 128:(k0 + k + 1) * 128], identb.t[0:R, 0:R]))
                P.op(pe, fns, reads=[src, identb] + reads, writes=[ps])
                yield k0, kk, ps, pv

        c5 = A.tile([D], F32)
        c5b = A.tile([D], BF16, dsem=False)
        cT = A.tile([16, 8], BF16, dsem=False)
        bada = A.tile([6 * D], F32)
        wb = [A.tile([16, 512], BF16) for _ in range(2)]
        msb = [A.tile([512], F32) for _ in range(2)]
        P.dma(sp, c5.t[0:5, :], cvec[:, :], writes=[c5])
        P.dma(sp, bada.t[0:5, :], b_ada[0:1, :].partition_broadcast(5), writes=[bada])
        P.op(act, lambda e: e.activation(out=c5b.t[0:5, :], in_=c5.t[0:5, :], func=AF.Silu), reads=[c5], writes=[c5b])
        for k0 in (0, 8):
            ps = psrot.get()
            pv = psbf(ps)
            P.op(pe, [lambda e, k=k, pv=pv: e.transpose(pv[:, k * 8:k * 8 + 5], c5b.t[0:5, (k0 + k) * 128:(k0 + k + 1) * 128], identb.t[0:5, 0:5]) for k in range(8)],
                 reads=[c5b, identb], writes=[ps])
            P.op(dve, lambda e, pv=pv, k0=k0: e.tensor_copy(out=cT.t[:, k0:k0 + 8, 0:5], in_=pv[:, 0:64].rearrange("p (k n) -> p k n", k=8, n=8)[:, :, 0:5]),
                 reads=[ps], writes=[cT])
        for cc in range(24):
            w = wb[cc % 2]
            m = msb[cc % 2]
            P.dma(pool, w.t, wview(w_ada, cc * 512, (cc + 1) * 512), writes=[w])
            ps = psrot.get()
            P.op(pe, [lambda e, k=k, w=w, ps=ps: e.matmul(ps.t[0:5, :], cT.t[:, k, 0:5], w.t[:, k, :], start=(k == 0), stop=(k == 15)) for k in range(16)],
                 reads=[cT, w], writes=[ps])
            P.op(dve, lambda e, ps=ps, m=m, cc=cc: e.tensor_tensor(out=m.t[0:5, :], in0=ps.t[0:5, :], in1=bada.t[0:5, cc * 512:(cc + 1) * 512], op=ALU.add),
                 reads=[ps, bada], writes=[m])
            P.dma(pool, mod_d[:, cc * 512:(cc + 1) * 512], m.t[0:5, :], reads=[m])
        P.barrier()
        A.reset(PERSIST)

        def make_front_rows(gn_d, off_sh, off_sc, is_s, gn):
            G = A.tile([D], F32)
            S = A.tile([D], F32)
            load_rows_prompt_or_sample(S, mod_d[0:1, off_sh:off_sh + D], [mod_d[1 + i:2 + i, off_sh:off_sh + D] for i in range(SB)], is_s, D)
            load_rows_prompt_or_sample(G, mod_d[0:1, off_sc:off_sc + D], [mod_d[1 + i:2 + i, off_sc:off_sc + D] for i in range(SB)], is_s, D)
            P.dma(sp, gn.t, gn_d[0:1, :].partition_broadcast(128), writes=[gn])
            P.op(dve, lambda e: e.scalar_tensor_tensor(out=G.t, in0=G.t, scalar=1.0, in1=gn.t, op0=ALU.add, op1=ALU.mult),
                 reads=[G, gn], writes=[G])
            return G, S

        def reload_front_rows(G, S, gn_d, off_sh, off_sc, gn):
            load_rows_prompt_or_sample(S, None, [mod_d[1 + i:2 + i, off_sh:off_sh + D] for i in range(SB)], True, D)
            load_rows_prompt_or_sample(G, None, [mod_d[1 + i:2 + i, off_sc:off_sc + D] for i in range(SB)], True, D)
            P.op(dve, lambda e: e.scalar_tensor_tensor(out=G.t[0:NS, :], in0=G.t[0:NS, :], scalar=1.0, in1=gn.t[0:NS, :], op0=ALU.add, op1=ALU.mult),
                 reads=[G, gn], writes=[G])

        class FrontBufs:
            def __init__(self):
                self.junk = A.tile([D], BF16, dsem=False)
                self.tmp = A.tile([D], F32)
                self.hb = [A.tile([D], BF16, dsem=False) for _ in range(2)]
                self.ss = [A.tile([1], F32, dsem=False) for _ in range(2)]
                self.rs = [A.tile([1], F32, dsem=False) for _ in range(2)]
                self.i = 0

        def front_a(fb, xt, R, G, S):
            i = fb.i
            fb.i += 1
            ss, rs, hb = fb.ss[i % 2], fb.rs[i % 2], fb.hb[i % 2]
            P.op(dve, lambda e: e.memset(ss.t[0:R, :], 0.0), writes=[ss])
            P.op(act, lambda e: e.activation(out=fb.junk.t[0:R, :], in_=xt.t[0:R, :], func=AF.Square, accum_out=ss.t[0:R, :]),
                 reads=[xt, ss], writes=[fb.junk, ss])
            rstd_col(ss, rs, R, D)
            P.op(dve, lambda e: e.scalar_tensor_tensor(out=fb.tmp.t[0:R, :], in0=xt.t[0:R, :], scalar=rs.t[0:R, :], in1=G.t[0:R, :], op0=ALU.mult, op1=ALU.mult),
                 reads=[xt, rs, G], writes=[fb.tmp])
            P.op(dve, lambda e: e.tensor_tensor(out=hb.t[0:R, :], in0=fb.tmp.t[0:R, :], in1=S.t[0:R, :], op=ALU.add),
                 reads=[fb.tmp, S], writes=[hb])
            return hb

        def front_b(hb, R, hT, col0):
            for k0, kk, ps, pv in transpose_to(None, hb, R, 16, [], None):
                P.op(act, lambda e, k0=k0, kk=kk, pv=pv: e.copy(out=hT.t[:, k0:k0 + kk, col0:col0 + R],
                                                               in_=pv.rearrange("p (k n) -> p k n", k=8, n=128)[:, 0:kk, 0:R]),
                     reads=[ps], writes=[hT])

        def front_tile(fb, xt, R, G, S, hT, col0):
            hb = front_a(fb, xt, R, G, S)
            front_b(hb, R, hT, col0)

        class KvBufs:
            def __init__(self):
                self.gkv = A.tile([KVL], F32)
                self.gkr = A.tile([ROPE], F32)
                P.dma(sp, self.gkv.t, g_kv_a[0:1, :].partition_broadcast(128), writes=[self.gkv])
                P.dma(sp, self.gkr.t, g_k_rope[0:1, :].partition_broadcast(128), writes=[self.gkr])
                self.junk = A.tile([KVL], BF16, dsem=False)
                self.ss = [A.tile([2], F32, dsem=False) for _ in range(2)]
                self.rs = [A.tile([2], F32, dsem=False) for _ in range(2)]
                self.kn = [A.tile([ROPE], F32, dsem=False) for _ in range(2)]
                self.t4 = [A.tile([4, 32], F32, dsem=False) for _ in range(2)]
                self.cs = [A.tile([ROPE], F32) for _ in range(2)]
                self.i = 0

        def zkv_tile(kb, Wkv, hT, col0, R, rope_rows_ap, ckv_out, kr_out):
            i = kb.i
            kb.i += 1
            ss, rs, kn, t4, cs = kb.ss[i % 2], kb.rs[i % 2], kb.kn[i % 2], kb.t4[i % 2], kb.cs[i % 2]
            P.dma(sp, cs.t[0:R, :], rope_rows_ap, writes=[cs])
            psc = psrot.get()
            P.op(pe, [lambda e, k=k: e.matmul(psc.t[0:R, :], hT.t[:, k, col0:col0 + R], Wkv.t[:, k, 0:KVL], start=(k == 0), stop=(k == 15)) for k in range(16)],
                 reads=[hT, Wkv], writes=[psc])
            psr = psrot.get()
            P.op(pe, [lambda e, k=k: e.matmul(psr.t[0:R, 0:ROPE], hT.t[:, k, col0:col0 + R], Wkv.t[:, k, KVL:KVL + ROPE], start=(k == 0), stop=(k == 15)) for k in range(16)],
                 reads=[hT, Wkv], writes=[psr])
            P.op(dve, lambda e: e.memset(ss.t[0:R, :], 0.0), writes=[ss])
            P.op(act, lambda e: e.activation(out=kb.junk.t[0:R, :], in_=psc.t[0:R, :], func=AF.Square, accum_out=ss.t[0:R, 0:1]),
                 reads=[psc, ss], writes=[kb.junk, ss])
            P.op(act, lambda e: e.activation(out=kb.junk.t[0:R, 0:ROPE], in_=psr.t[0:R, 0:ROPE], func=AF.Square, accum_out=ss.t[0:R, 1:2]),
                 reads=[psr, ss], writes=[kb.junk, ss])
            P.op(act, lambda e: e.activation(out=rs.t[0:R, 0:1], in_=ss.t[0:R, 0:1], func=AF.Ln, scale=1.0 / KVL, bias=EPS), reads=[ss], writes=[rs])
            P.op(act, lambda e: e.activation(out=rs.t[0:R, 1:2], in_=ss.t[0:R, 1:2], func=AF.Ln, scale=1.0 / ROPE, bias=EPS), reads=[ss], writes=[rs])
            P.op(act, lambda e: e.activation(out=rs.t[0:R, :], in_=rs.t[0:R, :], func=AF.Exp, scale=-0.5), reads=[rs], writes=[rs])
            P.op(dve, lambda e: e.scalar_tensor_tensor(out=ckv_out.t[0:R, :], in0=psc.t[0:R, :], scalar=rs.t[0:R, 0:1], in1=kb.gkv.t[0:R, :], op0=ALU.mult, op1=ALU.mult),
                 reads=[psc, rs, kb.gkv], writes=[ckv_out])
            P.op(dve, lambda e: e.scalar_tensor_tensor(out=kn.t[0:R, :], in0=psr.t[0:R, 0:ROPE], scalar=rs.t[0:R, 1:2], in1=kb.gkr.t[0:R, :], op0=ALU.mult, op1=ALU.mult),
                 reads=[psr, rs, kb.gkr], writes=[kn])
            x1, x2 = kn.t[0:R, 0:32], kn.t[0:R, 32:64]
            co, si = cs.t[0:R, 0:32], cs.t[0:R, 32:64]
            P.op(dve, [lambda e: e.tensor_tensor(out=t4.t[0:R, 0, :], in0=x1, in1=co, op=ALU.mult),
                       lambda e: e.tensor_tensor(out=t4.t[0:R, 1, :], in0=x2, in1=si, op=ALU.mult),
                       lambda e: e.tensor_tensor(out=t4.t[0:R, 2, :], in0=x2, in1=co, op=ALU.mult),
                       lambda e: e.tensor_tensor(out=t4.t[0:R, 3, :], in0=x1, in1=si, op=ALU.mult)],
                 reads=[kn, cs], writes=[t4])
            P.op(dve, [lambda e: e.tensor_tensor(out=kr_out.t[0:R, 0:32], in0=t4.t[0:R, 0, :], in1=t4.t[0:R, 1, :], op=ALU.subtract),
                       lambda e: e.tensor_tensor(out=kr_out.t[0:R, 32:64], in0=t4.t[0:R, 2, :], in1=t4.t[0:R, 3, :], op=ALU.add)],
                 reads=[t4], writes=[kr_out])

        class ExpBufs:
            def __init__(self, TG):
                self.wuk = A.tile([4, NH * 128], BF16)
                self.wuv = A.tile([4, NH * 128], BF16)
                P.dma(pool, self.wuk.t, wview(w_uk, 0, NH * 128), writes=[self.wuk])
                P.dma(pool, self.wuv.t, wview(w_uv, 0, NH * 128), writes=[self.wuv])
                self.gk = A.tile([1], F32)
                P.dma(sp, self.gk.t, g_k_nope.rearrange("o d -> d o"), writes=[self.gk])
                self.cb = [A.tile([KVL], BF16, dsem=False) for _ in range(2)]
                self.krb = [A.tile([128], BF16, dsem=False) for _ in range(2)]
                self.cT = [A.tile([4, TG], BF16, dsem=False) for _ in range(2)]
                self.krT = [A.tile([TG], BF16) for _ in range(2)]
                self.sq = [A.tile([TG], BF16, dsem=False) for _ in range(2)]
                self.rs = [A.tile([TG], F32, dsem=False) for _ in range(2)]
                self.KTs = [A.tile([NH, TG], BF16) for _ in range(1)]
                self.Vs = [A.tile([NH, TG // 128 if TG >= 128 else 1, 128], BF16) for _ in range(1)]
                self.i = 0

        def expand_group(xb_, tiles, T, KT_dst, KrT_dst, V_dst, hooks=None):
            i = xb_.i
            xb_.i += 1
            cT, krT, KTs, Vs = xb_.cT[i % 2], xb_.krT[i % 2], xb_.KTs[0], xb_.Vs[0]
            col = 0
            cols = []
            for ti, (ckv, kr, R) in enumerate(tiles):
                cb, krb = xb_.cb[ti % 2], xb_.krb[ti % 2]
                P.op(act, lambda e, cb=cb, ckv=ckv, R=R: e.copy(out=cb.t[0:R, :], in_=ckv.t[0:R, :]), reads=[ckv], writes=[cb])
                P.op(dve, [lambda e, krb=krb, kr=kr, R=R: e.tensor_copy(out=krb.t[0:R, 0:64], in_=kr.t[0:R, :]),
                           lambda e, krb=krb, kr=kr, R=R: e.tensor_copy(out=krb.t[0:R, 64:128], in_=kr.t[0:R, :])], reads=[kr], writes=[krb])
                for k0, kk, ps, pv in transpose_to(None, cb, R, 4, [], None):
                    P.op(dve, lambda e, pv=pv, col=col, R=R: e.tensor_copy(out=cT.t[:, 0:4, col:col + R], in_=pv[:, 0:512].rearrange("p (k n) -> p k n", k=4, n=128)[:, :, 0:R]),
                         reads=[ps], writes=[cT])
                for k0, kk, ps, pv in transpose_to(None, krb, R, 1, [], None):
                    P.op(dve, lambda e, pv=pv, col=col, R=R: e.tensor_copy(out=krT.t[:, col:col + R], in_=pv[:, 0:R]), reads=[ps], writes=[krT])
                cols.append((col, R))
                col += R
            assert col == T
            P.dma(pool, KrT_dst, krT.t[:, 0:T], reads=[krT])
            def k_mm(h):
                psk = psrot.get()
                P.op(pe, [lambda e, kc=kc: e.matmul(psk.t[:, 0:T], xb_.wuk.t[:, kc, h * 128:(h + 1) * 128], cT.t[:, kc, 0:T], start=(kc == 0), stop=(kc == 3)) for kc in range(4)],
                     reads=[xb_.wuk, cT], writes=[psk])
                return psk

            def k_rest(h, psk):
                sq, rs = xb_.sq[h % 2], xb_.rs[h % 2]
                P.op(act, lambda e: e.activation(out=sq.t[:, 0:T], in_=psk.t[:, 0:T], func=AF.Square), reads=[psk], writes=[sq])
                pss = psrot.get()
                P.op(pe, lambda e: e.matmul(pss.t[:, 0:T], ones_b.t, sq.t[:, 0:T], start=True, stop=True), reads=[ones_b, sq], writes=[pss])
                rstd_tile(pss, rs, T, 128)
                P.op(dve, lambda e: e.scalar_tensor_tensor(out=KTs.t[:, h, 0:T], in0=psk.t[:, 0:T], scalar=xb_.gk.t[:, 0:1], in1=rs.t[:, 0:T], op0=ALU.mult, op1=ALU.mult),
                     reads=[psk, rs, xb_.gk], writes=[KTs])

            LK = 2
            psks = {}
            for i in range(NH + LK):
                if i < NH:
                    psks[i] = k_mm(i)
                if i - LK >= 0:
                    k_rest(i - LK, psks.pop(i - LK))
                    if hooks and (i - LK) in hooks:
                        hooks[i - LK]()
            P.dma(pool, KT_dst, KTs.t[:, :, 0:T], reads=[KTs])
            Rmax = max(R for _, R in cols)
            for ti, (col, R) in enumerate(cols):
                for c4 in range(4):
                    psv = psrot.get()
                    P.op(pe, [lambda e, kc=kc, psv=psv, col=col, R=R, c4=c4: e.matmul(psv.t[0:R, :], cT.t[:, kc, col:col + R], xb_.wuv.t[:, kc, c4 * 512:(c4 + 1) * 512], start=(kc == 0), stop=(kc == 3)) for kc in range(4)],
                         reads=[xb_.wuv, cT], writes=[psv])
                    P.op(act, lambda e, psv=psv, ti=ti, R=R, c4=c4: e.copy(out=Vs.t[0:R, c4 * 4:(c4 + 1) * 4, ti, :], in_=psv.t[0:R, :].rearrange("p (h c) -> p h c", h=4, c=128)),
                         reads=[psv], writes=[Vs])
            P.dma(pool, V_dst, Vs.t[0:Rmax, :, 0:len(cols), :], reads=[Vs])

        fb = FrontBufs()
        G1, S1 = make_front_rows(g_norm1, 0, D, False, fb.tmp)
        kb = KvBufs()
        Wkv = A.tile([16, KVL + ROPE], BF16)
        P.dma(pool, Wkv.t, wview(w_in, OFF_KV, OFF_GATE), writes=[Wkv])
        eb = ExpBufs(512)
        xts = [A.tile([D], F32) for _ in range(2)]
        hTs = [A.tile([16, 512], BF16) for _ in range(1)]
        ckvs = [A.tile([KVL], F32) for _ in range(4)]
        krs = [A.tile([ROPE], F32) for _ in range(4)]
        xi = [0]
        hT = hTs[0]

        pend_hb = {}

        def a_front_a(g, t):
            xt = xts[xi[0] % 2]
            xi[0] += 1
            r0 = g * 512 + t * 128
            P.dma(sp, xt.t, xb[r0:r0 + 128, :], writes=[xt])
            pend_hb[t] = front_a(fb, xt, 128, G1, S1)

        def a_front_b(g, t):
            front_b(pend_hb.pop(t), 128, hT, t * 128)

        def a_front(g, t):
            a_front_a(g, t)
            a_front_b(g, t)

        def a_zkv(g):
            tl = []
            for t in range(4):
                ck, kr_ = ckvs[t], krs[t]
                r0 = g * 512 + t * 128
                zkv_tile(kb, Wkv, hT, t * 128, 128, ropetm[r0:r0 + 128, :], ck, kr_)
                tl.append((ck, kr_, 128))
            return tl

        for t in range(4):
            a_front(0, t)
        tl = a_zkv(0)
        for g in range(NB):
            hooks = None
            if g + 1 < NB:
                hooks = {4 * t: (lambda t=t, g=g: a_front_a(g + 1, t)) for t in range(4)}
                hooks.update({4 * t + 3: (lambda t=t, g=g: a_front_b(g + 1, t)) for t in range(4)})
            expand_group(eb, tl, 512,
                         KT_d.rearrange("h d t -> d h t")[:, :, g * 512:(g + 1) * 512],
                         KrT_d[:, g * 512:(g + 1) * 512],
                         V_d.rearrange("h p t c -> p h t c")[:, :, g * 4:(g + 1) * 4, :], hooks=hooks)
            if g + 1 < NB:
                tl = a_zkv(g + 1)
        P.barrier()
        A.reset(PERSIST)

        gn1 = A.tile([D], F32)
        G1, S1 = make_front_rows(g_norm1, 0, D, False, gn1)
        fb = FrontBufs()
        xts = [A.tile([D], F32) for _ in range(2)]
        hTs = [A.tile([16, 512], BF16) for _ in range(2)]
        xi = 0
        for (bi, row0, T, nt, R, is_s) in blocks():
            if is_s:
                reload_front_rows(G1, S1, g_norm1, 0, D, gn1)
            hT = hTs[bi % 2]
            for t in range(nt):
                xt = xts[xi % 2]
                xi += 1
                P.dma(sp, xt.t[0:R, :], xo[row0 + t * 128:row0 + t * 128 + R, :], writes=[xt])
                front_tile(fb, xt, R, G1, S1, hT, t * 128)
            P.dma(pool, hT_d[:, :, row0:row0 + T], hT.t[:, :, 0:T], reads=[hT])
        P.barrier()
        A.reset(PERSIST)

        def fm_stage(c_base, func, dst_d):
            W = A.tile([16, D], BF16)
            P.dma(pool, W.t[:, :, 0:1024], wview(w_in, c_base, c_base + 1024), writes=[W])
            P.dma(pool, W.t[:, :, 1024:2048], wview(w_in, c_base + 1024, c_base + 2048), writes=[W])
            hTs = [A.tile([16, 512], BF16) for _ in range(2)]
            outs = [A.tile([16, 512], BF16) for _ in range(2)]
            for (bi, row0, T, nt, R, is_s) in blocks():
                hT, o = hTs[bi % 2], outs[bi % 2]
                P.dma(sp, hT.t[:, :, 0:T], hT_d[:, :, row0:row0 + T], writes=[hT])
                for j in range(16):
                    ps = psrot.get()
                    P.op(pe, [lambda e, k=k, j=j, ps=ps, hT=hT: e.matmul(ps.t[:, 0:T], W.t[:, k, j * 128:(j + 1) * 128], hT.t[:, k, 0:T], start=(k == 0), stop=(k == 15)) for k in range(16)],
                         reads=[W, hT], writes=[ps])
                    P.op(act, lambda e, ps=ps, o=o, j=j: e.activation(out=o.t[:, j, 0:T], in_=ps.t[:, 0:T], func=func), reads=[ps], writes=[o])
                P.dma(pool, dst_d[:, :, row0:row0 + T], o.t[:, :, 0:T], reads=[o])
            P.barrier()
            A.reset(PERSIST)

        fm_stage(0, AF.Gelu, uT_d)
        fm_stage(OFF_GATE, AF.Sigmoid, gaT_d)
        fm_stage(OFF_GATE + D, AF.Sigmoid, gbT_d)

        W = A.tile([16, D], BF16)
        P.dma(pool, W.t[:, :, 0:1024], wview(w_in, D, D + 1024), writes=[W])
        P.dma(pool, W.t[:, :, 1024:2048], wview(w_in, D + 1024, 2 * D), writes=[W])
        gsg = A.tile([D], F32)
        P.dma(sp, gsg.t, g_sg[0:1, :].partition_broadcast(128), writes=[gsg])
        hTs = [A.tile([16, 512], BF16) for _ in range(2)]
        vfs = [A.tile([D], F32) for _ in range(2)]
        vbs = [A.tile([D], BF16) for _ in range(2)]
        vjunk = A.tile([D], BF16, dsem=False)
        vss = [A.tile([1], F32, dsem=False) for _ in range(2)]
        vrs = [A.tile([1], F32, dsem=False) for _ in range(2)]
        vi = 0
        for (bi, row0, T, nt, R, is_s) in blocks():
            hT = hTs[bi % 2]
            P.dma(sp, hT.t[:, :, 0:T], hT_d[:, :, row0:row0 + T], writes=[hT])
            for t in range(nt):
                vf, vb, ss, rs = vfs[vi % 2], vbs[vi % 2], vss[vi % 2], vrs[vi % 2]
                vi += 1
                for c4 in range(4):
                    ps = psrot.get()
                    P.op(pe, [lambda e, k=k, ps=ps, hT=hT, t=t, c4=c4: e.matmul(ps.t[0:R, :], hT.t[:, k, t * 128:t * 128 + R], W.t[:, k, c4 * 512:(c4 + 1) * 512], start=(k == 0), stop=(k == 15)) for k in range(16)],
                         reads=[W, hT], writes=[ps])
                    P.op(act, lambda e, ps=ps, vf=vf, c4=c4: e.activation(out=vf.t[0:R, c4 * 512:(c4 + 1) * 512], in_=ps.t[0:R, :], func=AF.Gelu), reads=[ps], writes=[vf])
                P.op(dve, lambda e, ss=ss: e.memset(ss.t[0:R, :], 0.0), writes=[ss])
                P.op(act, lambda e, vf=vf, ss=ss: e.activation(out=vjunk.t[0:R, :], in_=vf.t[0:R, :], func=AF.Square, accum_out=ss.t[0:R, :]), reads=[vf, ss], writes=[vjunk, ss])
                rstd_col(ss, rs, R, D)
                if is_s:
                    P.op(dve, lambda e, vf=vf, rs=rs: e.scalar_tensor_tensor(out=vf.t[0:R, :], in0=vf.t[0:R, :], scalar=rs.t[0:R, :], in1=gsg.t[0:R, :], op0=ALU.mult, op1=ALU.mult),
                         reads=[vf, rs, gsg], writes=[vf])
                    P.dma(pool, v_all[:, :], vf.t[0:R, :], reads=[vf])
                    P.op(dve, lambda e, vf=vf, vb=vb: e.tensor_copy(out=vb.t[0:R, :], in_=vf.t[0:R, :]), reads=[vf], writes=[vb])
                else:
                    P.op(dve, lambda e, vf=vf, rs=rs, vb=vb: e.scalar_tensor_tensor(out=vb.t[0:R, :], in0=vf.t[0:R, :], scalar=rs.t[0:R, :], in1=gsg.t[0:R, :], op0=ALU.mult, op1=ALU.mult),
                         reads=[vf, rs, gsg], writes=[vb])
                P.dma(pool, v_d[row0 + t * 128:row0 + t * 128 + R, :], vb.t[0:R, :], reads=[vb])
        P.barrier()
        A.reset(PERSIST)

        Wq = A.tile([16, QL], BF16)
        P.dma(pool, Wq.t, wview(w_in, OFF_Q, OFF_KV), writes=[Wq])
        Wkv = A.tile([16, KVL + ROPE], BF16)
        P.dma(pool, Wkv.t, wview(w_in, OFF_KV, OFF_GATE), writes=[Wkv])
        Wuq = A.tile([4, NH * QKH], BF16)
        P.dma(pool, Wuq.t, wview(w_uq, 0, NH * QKH), writes=[Wuq])
        Wr2 = A.tile([4, NH, 128], BF16, dsem=False)
        wq3 = Wuq.t.rearrange("p k (h c) -> p k h c", h=NH, c=QKH)
        for kc in range(4):
            P.op(dve, [lambda e, kc=kc: e.tensor_copy(out=Wr2.t[:, kc, :, 0:64], in_=wq3[:, kc, :, 128:192]),
                       lambda e, kc=kc: e.tensor_copy(out=Wr2.t[:, kc, :, 64:96], in_=wq3[:, kc, :, 160:192]),
                       lambda e, kc=kc: e.tensor_copy(out=Wr2.t[:, kc, :, 96:128], in_=wq3[:, kc, :, 128:160])],
                 reads=[Wuq], writes=[Wr2])
        gqa = A.tile([4], F32)
        P.dma(sp, gqa.t, g_q_a.rearrange("o (k p) -> p (o k)", p=128), writes=[gqa])
        gqn = A.tile([1], F32)
        P.dma(sp, gqn.t, g_q_nope.rearrange("o d -> d o"), writes=[gqn])
        gq2 = A.tile([1], F32)
        gqr_col = g_q_rope.rearrange("o d -> d o")
        P.dma(sp, gq2.t[0:64, :], gqr_col[0:64, :], writes=[gq2])
        P.dma(sp, gq2.t[64:96, :], gqr_col[32:64, :], writes=[gq2])
        P.dma(sp, gq2.t[96:128, :], gqr_col[0:32, :], writes=[gq2])
        P.op(dve, lambda e: e.tensor_scalar(out=gqn.t, in0=gqn.t, scalar1=SCALE, scalar2=0.0, op0=ALU.mult, op1=ALU.add), reads=[gqn], writes=[gqn])
        P.op(dve, lambda e: e.tensor_scalar(out=gq2.t, in0=gq2.t, scalar1=SCALE, scalar2=0.0, op0=ALU.mult, op1=ALU.add), reads=[gq2], writes=[gq2])
        kb = KvBufs()
        hTs = [A.tile([16, 512], BF16) for _ in range(2)]
        sqs = [A.tile([512], BF16, dsem=False) for _ in range(2)]
        rss = [A.tile([512], F32, dsem=False) for _ in range(2)]
        tmps = [A.tile([512], F32, dsem=False) for _ in range(2)]
        sqs4 = [A.tile([512], BF16, dsem=False) for _ in range(4)]
        rss4 = [A.tile([512], F32, dsem=False) for _ in range(4)]
        qaT = A.tile([4, 512], BF16, dsem=False)
        QTs = [A.tile([NH, 512], BF16) for _ in range(1)]
        QrTs = [A.tile([NH, 512], BF16) for _ in range(1)]
        tws = [A.tile([512], F32) for _ in range(2)]
        cko = [A.tile([KVL], F32) for _ in range(2)]
        kro = [A.tile([ROPE], F32) for _ in range(2)]
        oi = 0
        qrot = Rot(PS[0:4])
        orot = Rot(PS[4:8])
        for (bi, row0, T, nt, R, is_s) in blocks():
            hT, QT, QrT, tw = hTs[bi % 2], QTs[0], QrTs[0], tws[bi % 2]
            P.dma(sp, hT.t[:, :, 0:T], hT_d[:, :, row0:row0 + T], writes=[hT])
            P.dma(sp, tw.t[:, 0:T], town[:, row0:row0 + T], writes=[tw])
            for t in range(nt):
                ck, kr_ = cko[oi % 2], kro[oi % 2]
                oi += 1
                r0 = row0 + t * 128
                zkv_tile(kb, Wkv, hT, t * 128, R, ropeo[r0:r0 + R, :], ck, kr_)
                P.dma(pool, lat_all[r0:r0 + R, :], ck.t[0:R, :], reads=[ck])
                P.dma(pool, kr_all[r0:r0 + R, :], kr_.t[0:R, :], reads=[kr_])
            pq = [qrot.get() for _ in range(4)]
            pss = orot.get()
            for kc in range(4):
                P.op(pe, [lambda e, k=k, kc=kc: e.matmul(pq[kc].t[:, 0:T], Wq.t[:, k, kc * 128:(kc + 1) * 128], hT.t[:, k, 0:T], start=(k == 0), stop=(k == 15)) for k in range(16)],
                     reads=[Wq, hT], writes=[pq[kc]])
            for kc in range(4):
                sq = sqs[kc % 2]
                P.op(act, lambda e, kc=kc, sq=sq: e.activation(out=sq.t[:, 0:T], in_=pq[kc].t[:, 0:T], func=AF.Square), reads=[pq[kc]], writes=[sq])
                if kc == 0:
                    P.deps(pe, [], [pss])
                P.op(pe, lambda e, kc=kc, sq=sq: e.matmul(pss.t[:, 0:T], ones_b.t, sq.t[:, 0:T], start=(kc == 0), stop=(kc == 3)), reads=[ones_b, sq], writes=[] if kc < 3 else [pss])
            rs = rss[0]
            rstd_tile(pss, rs, T, QL)
            for kc in range(4):
                P.op(dve, lambda e, kc=kc: e.scalar_tensor_tensor(out=qaT.t[:, kc, 0:T], in0=pq[kc].t[:, 0:T], scalar=gqa.t[:, kc:kc + 1], in1=rs.t[:, 0:T], op0=ALU.mult, op1=ALU.mult),
                     reads=[pq[kc], gqa, rs], writes=[qaT])
            def q_mm(h):
                pa = qrot.get()
                pab = qrot.get()
                P.op(pe, [lambda e, kc=kc: e.matmul(pa.t[:, 0:T], Wuq.t[:, kc, h * QKH:h * QKH + 128], qaT.t[:, kc, 0:T], start=(kc == 0), stop=(kc == 3)) for kc in range(4)],
                     reads=[Wuq, qaT], writes=[pa])
                P.op(pe, [lambda e, kc=kc: e.matmul(pab.t[:, 0:T], Wr2.t[:, kc, h, :], qaT.t[:, kc, 0:T], start=(kc == 0), stop=(kc == 3)) for kc in range(4)],
                     reads=[Wr2, qaT], writes=[pab])
                return pa, pab

            def q_rest(h, pa, pab):
                sq, sq2, rs1, rs2, tmp = sqs4[(h % 2) * 2], sqs4[(h % 2) * 2 + 1], rss4[(h % 2) * 2], rss4[(h % 2) * 2 + 1], tmps[h % 2]
                P.op(act, lambda e: e.activation(out=sq.t[:, 0:T], in_=pa.t[:, 0:T], func=AF.Square), reads=[pa], writes=[sq])
                P.op(act, lambda e: e.activation(out=sq2.t[0:64, 0:T], in_=pab.t[0:64, 0:T], func=AF.Square), reads=[pab], writes=[sq2])
                ps1 = orot.get()
                ps2 = orot.get()
                P.op(pe, lambda e: e.matmul(ps1.t[:, 0:T], ones_b.t, sq.t[:, 0:T], start=True, stop=True), reads=[ones_b, sq], writes=[ps1])
                P.op(pe, lambda e: e.matmul(ps2.t[:, 0:T], ones_b.t[0:64, :], sq2.t[0:64, 0:T], start=True, stop=True), reads=[ones_b, sq2], writes=[ps2])
                rstd_tile(ps1, rs1, T, 128)
                rstd_tile(ps2, rs2, T, ROPE)
                P.op(dve, lambda e: e.scalar_tensor_tensor(out=QT.t[:, h, 0:T], in0=pa.t[:, 0:T], scalar=gqn.t[:, 0:1], in1=rs1.t[:, 0:T], op0=ALU.mult, op1=ALU.mult),
                     reads=[pa, gqn, rs1], writes=[QT])
                P.op(dve, lambda e: e.scalar_tensor_tensor(out=tmp.t[:, 0:T], in0=pab.t[:, 0:T], scalar=gq2.t[:, 0:1], in1=rs2.t[:, 0:T], op0=ALU.mult, op1=ALU.mult),
                     reads=[pab, gq2, rs2], writes=[tmp])
                P.op(dve, lambda e: e.tensor_tensor(out=QrT.t[:, h, 0:T], in0=tmp.t[:, 0:T], in1=tw.t[:, 0:T], op=ALU.mult),
                     reads=[tmp, tw], writes=[QrT])

            pend = {}
            for i in range(NH + 1):
                if i < NH:
                    pend[i] = q_mm(i)
                if i - 1 >= 0:
                    q_rest(i - 1, *pend.pop(i - 1))
            P.dma(pool, QT_d.rearrange("h d t -> d h t")[:, :, row0:row0 + T], QT.t[:, :, 0:T], reads=[QT])
            P.dma(pool, QrT_d.rearrange("h d t -> d h t")[:, :, row0:row0 + T], QrT.t[:, :, 0:T], reads=[QrT])
        P.barrier()
        A.reset(PERSIST)

        eb = ExpBufs(512)
        cks = [A.tile([KVL], F32) for _ in range(8)]
        krs_ = [A.tile([ROPE], F32) for _ in range(8)]
        gi = 0
        for i in range(SB):
            for t0 in range(0, NKT, 4):
                ntl = min(4, NKT - t0)
                tl = []
                for t in range(ntl):
                    ck, kr_ = cks[(gi % 2) * 4 + t], krs_[(gi % 2) * 4 + t]
                    r0 = (t0 + t) * 128
                    P.dma(sp, ck.t, clat[i, r0:r0 + 128, :], writes=[ck])
                    P.dma(sp, kr_.t, ckr[i, r0:r0 + 128, :], writes=[kr_])
                    tl.append((ck, kr_, 128))
                gi += 1
                T = ntl * 128
                expand_group(eb, tl, T,
                             KTs_d[i].rearrange("h d t -> d h t")[:, :, t0 * 128:t0 * 128 + T],
                             KrTs_d[i][:, t0 * 128:t0 * 128 + T],
                             Vs_d[i].rearrange("h p t c -> p h t c")[:, :, t0:t0 + ntl, :])
            ck, kr_ = cks[(gi % 2) * 4], krs_[(gi % 2) * 4]
            gi += 1
            r0 = NOWN + i * TS
            P.dma(sp, ck.t[0:TS, :], lat_all[r0:r0 + TS, :], writes=[ck])
            P.dma(sp, kr_.t[0:TS, :], kr_all[r0:r0 + TS, :], writes=[kr_])
            expand_group(eb, [(ck, kr_, TS)], TS,
                         KTs_d[i].rearrange("h d t -> d h t")[:, :, PAST:PASTP],
                         KrTs_d[i][:, PAST:PASTP],
                         Vs_d[i].rearrange("h p t c -> p h t c")[0:TS, :, NKT:NKT + 1, :])
        P.barrier()
        A.reset(PERSIST)

        wsT = A.tile([NH, 128], BF16, dsem=False)
        wsblk = A.tile([NH, NS], BF16)
        bsb = A.tile([NH, 128], F32)
        bss = A.tile([NH, NS], F32, dsem=False)
        trl = A.tile([128], F32)
        wsf = [A.tile([128], F32) for _ in range(2)]
        wsm = [A.tile([128], BF16, dsem=False) for _ in range(2)]
        P.dma(sp, trl.t, trild[:, :], writes=[trl])
        P.dma(sp, bsb.t.rearrange("p g i -> p (g i)"), b_s[0:1, :].partition_broadcast(128), writes=[bsb])
        for g in range(NH):
            wf, wm = wsf[g % 2], wsm[g % 2]
            P.dma(sp, wf.t, w_s[g, :, :], writes=[wf])
            P.op(dve, lambda e, wf=wf, wm=wm: e.tensor_tensor(out=wm.t, in0=wf.t, in1=trl.t, op=ALU.mult), reads=[wf, trl], writes=[wm])
            for k0, kk, ps, pv in transpose_to(None, wm, 128, 1, [], None):
                P.op(dve, lambda e, pv=pv, g=g: e.tensor_copy(out=wsT.t[:, g, :], in_=pv[:, 0:128]), reads=[ps], writes=[wsT])
        P.op(dve, lambda e: e.memset(wsblk.t, 0.0), writes=[wsblk])
        for i in range(SB):
            P.dma(sp, wsblk.t[16 * i:16 * i + 16, :, 16 * i:16 * i + 16], wsT.t[0:16, :, 0:16], reads=[wsT], writes=[wsblk])
            P.op(dve, lambda e, i=i: e.tensor_copy(out=bss.t[:, :, 16 * i:16 * i + 16], in_=bsb.t[:, :, 0:16]), reads=[bsb], writes=[bss])
        uTs = [A.tile([16, 512], BF16) for _ in range(2)]
        vts = [A.tile([4, D], BF16) for _ in range(2)]
        osg = [A.tile([16, 512], BF16) for _ in range(2)]
        tmpf = [A.tile([512], F32, dsem=False) for _ in range(2)]
        for (bi, row0, T, nt, R, is_s) in blocks():
            uT, vt, o = uTs[bi % 2], vts[bi % 2], osg[bi % 2]
            P.dma(sp, uT.t[:, :, 0:T], uT_d[:, :, row0:row0 + T], writes=[uT])
            for t in range(nt):
                P.dma(sp, vt.t[0:R, t, :], v_d[row0 + t * 128:row0 + t * 128 + R, :], writes=[vt])
            for g in range(NH):
                ps = psrot.get()
                tm = tmpf[g % 2]
                if not is_s:
                    P.op(pe, [lambda e, t=t, g=g, ps=ps, vt=vt: e.matmul(ps.t[:, t * 128:(t + 1) * 128], vt.t[:, t, g * 128:(g + 1) * 128], wsT.t[:, g, :], start=True, stop=True) for t in range(4)],
                         reads=[vt, wsT], writes=[ps])
                    P.op(dve, lambda e, ps=ps, tm=tm, g=g: e.tensor_tensor(out=tm.t.rearrange("p (t i) -> p t i", t=4, i=128), in0=ps.t.rearrange("p (t i) -> p t i", t=4, i=128),
                                                                     in1=bsb.t[:, g, :].unsqueeze(1).to_broadcast([128, 4, 128]), op=ALU.add), reads=[ps, bsb], writes=[tm])
                else:
                    P.op(pe, lambda e, g=g, ps=ps, vt=vt: e.matmul(ps.t[:, 0:NS], vt.t[0:NS, 0, g * 128:(g + 1) * 128], wsblk.t[0:NS, g, :], start=True, stop=True),
                         reads=[vt, wsblk], writes=[ps])
                    P.op(dve, lambda e, ps=ps, tm=tm, g=g: e.tensor_tensor(out=tm.t[:, 0:NS], in0=ps.t[:, 0:NS], in1=bss.t[:, g, :], op=ALU.add), reads=[ps, bss], writes=[tm])
                P.op(dve, lambda e, tm=tm, g=g, uT=uT, o=o: e.tensor_tensor(out=o.t[:, g, 0:T], in0=tm.t[:, 0:T], in1=uT.t[:, g, 0:T], op=ALU.mult), reads=[tm, uT], writes=[o])
            P.dma(pool, osgT_d[:, :, row0:row0 + T], o.t[:, :, 0:T], reads=[o])
        P.barrier()
        A.reset(PERSIST)

        msk = A.tile([16, 8], F32)
        mskb = A.tile([16, 8], BF16, dsem=False)
        P.dma(sp, msk.t.rearrange("p a b -> p (a b)"), maskc[:, :], writes=[msk])
        P.op(dve, lambda e: e.tensor_copy(out=mskb.t, in_=msk.t), reads=[msk], writes=[mskb])
        CH = 8
        Kc = [A.tile([CH * 128], BF16) for _ in range(4)]
        Krc = [A.tile([CH * 128], BF16) for _ in range(4)]
        Vc = [A.tile([CH, 128], BF16) for _ in range(4)]
        Qs = [A.tile([512], BF16) for _ in range(2)]
        Qrs = [A.tile([512], BF16) for _ in range(2)]
        pTs = [A.tile([512], BF16, dsem=False) for _ in range(4)]
        ptmp = [A.tile([512], BF16, dsem=False) for _ in range(4)]
        acc = [A.tile([512], F32, dsem=False) for _ in range(4)]
        rinv = [A.tile([512], F32, dsem=False) for _ in range(2)]
        omla = [A.tile([NH, 512], BF16) for _ in range(2)]
        srot = Rot(PS[0:5])
        orot2 = Rot(PS[5:7])
        rrot = Rot(PS[7:8])
        ci = 0
        hi = 0
        pi = 0

        LOOK = 3
        jobs = []
        for (bi, row0, T, nt, R, is_s) in blocks():
            if not is_s:
                ctx = 4 * (4 * bi + 4)
                for h in range(NH):
                    jobs.append(dict(bi=bi, row0=row0, T=T, QT=QT_d[h], QrT=QrT_d[h], q0=row0, NQ=512, ctx=ctx, last=128,
                                     KT=KT_d[h], KrT=KrT_d, V=V_d[h], h=h, ocol=0, tail=ctx - 16, store=(h == NH - 1)))
            else:
                for i in range(SB):
                    for h in range(NH):
                        jobs.append(dict(bi=bi, row0=row0, T=T, QT=QT_d[h], QrT=QrT_d[h], q0=row0 + i * TS, NQ=TS, ctx=NKT + 1, last=TS,
                                         KT=KTs_d[i, h], KrT=KrTs_d[i], V=Vs_d[i, h], h=h, ocol=i * TS, tail=None,
                                         store=(i == SB - 1 and h == NH - 1)))
        chunks = []
        tiles = []
        for ji, jb in enumerate(jobs):
            for c0 in range(0, jb["ctx"], CH):
                nt_ = min(CH, jb["ctx"] - c0)
                chunks.append(dict(job=ji, c0=c0, nt=nt_, first=(c0 == 0)))
                for t in range(nt_):
                    kt = c0 + t
                    tiles.append(dict(job=ji, chunk=len(chunks) - 1, t=t, kt=kt, RK=(jb["last"] if kt == jb["ctx"] - 1 else 128),
                                      firstc=(t == 0), first=(kt == 0), last=(kt == jb["ctx"] - 1)))

        def prefetch(cj):
            if cj >= len(chunks):
                return
            ch = chunks[cj]
            jb = jobs[ch["job"]]
            if ch["first"]:
                jb["Q"], jb["Qr"] = Qs[ch["job"] % 2], Qrs[ch["job"] % 2]
                jb["acs"] = (acc[(ch["job"] % 2) * 2], acc[(ch["job"] % 2) * 2 + 1])
                jb["ri"] = rinv[ch["job"] % 2]
                jb["nacc"] = [0, 0]
                jb["om"] = omla[jb["bi"] % 2]
                P.dma(sp, jb["Q"].t[:, 0:jb["NQ"]], jb["QT"][:, jb["q0"]:jb["q0"] + jb["NQ"]], writes=[jb["Q"]])
                P.dma(sp, jb["Qr"].t[:, 0:jb["NQ"]], jb["QrT"][:, jb["q0"]:jb["q0"] + jb["NQ"]], writes=[jb["Qr"]])
            kc_, krc_, vc_ = Kc[cj % 4], Krc[cj % 4], Vc[cj % 4]
            ch["bufs"] = (kc_, krc_, vc_)
            c0, nt_ = ch["c0"], ch["nt"]
            lastc = (c0 + nt_ == jb["ctx"])
            nkeys = (nt_ - 1) * 128 + (jb["last"] if lastc else 128)
            P.dma(sp, kc_.t[:, 0:nkeys], jb["KT"][:, c0 * 128:c0 * 128 + nkeys], writes=[kc_])
            P.dma(sp, krc_.t[:, 0:nkeys], jb["KrT"][:, c0 * 128:c0 * 128 + nkeys], writes=[krc_])
            if lastc and jb["last"] < 128:
                if nt_ > 1:
                    P.dma(sp, vc_.t[:, 0:nt_ - 1, :], jb["V"][:, c0:c0 + nt_ - 1, :], writes=[vc_])
                P.dma(sp, vc_.t[0:jb["last"], nt_ - 1:nt_, :], jb["V"][0:jb["last"], c0 + nt_ - 1:c0 + nt_, :], writes=[vc_])
            else:
                P.dma(sp, vc_.t[:, 0:nt_, :], jb["V"][:, c0:c0 + nt_, :], writes=[vc_])

        def emit_S(ti_):
            tl = tiles[ti_]
            jb = jobs[tl["job"]]
            if tl["firstc"]:
                if tl["chunk"] == 0:
                    prefetch(0)
                prefetch(tl["chunk"] + 1)
            kc_, krc_, vc_ = chunks[tl["chunk"]]["bufs"]
            NQ, RK, t = jb["NQ"], tl["RK"], tl["t"]
            pss_ = srot.get()
            tl["pss"] = pss_
            Q, Qr = jb["Q"], jb["Qr"]
            P.op(pe, [lambda e: e.matmul(pss_.t[0:RK, 0:NQ], kc_.t[:, t * 128:t * 128 + RK], Q.t[:, 0:NQ], start=True, stop=False),
                      lambda e: e.matmul(pss_.t[0:RK, 0:NQ], krc_.t[:, t * 128:t * 128 + RK], Qr.t[:, 0:NQ], start=False, stop=True)],
                 reads=[kc_, krc_, Q, Qr], writes=[pss_])

        pcount = [0]

        def emit_rest(ti_):
            tl = tiles[ti_]
            jb = jobs[tl["job"]]
            kc_, krc_, vc_ = chunks[tl["chunk"]]["bufs"]
            NQ, RK, t, kt = jb["NQ"], tl["RK"], tl["t"], tl["kt"]
            pss_ = tl["pss"]
            pT = pTs[pcount[0] % 4]
            pcount[0] += 1
            P.op(act, lambda e: e.activation(out=pT.t[0:RK, 0:NQ], in_=pss_.t[0:RK, 0:NQ], func=AF.Exp), reads=[pss_], writes=[pT])
            if jb["tail"] is not None and kt >= jb["tail"]:
                kbt = kt - jb["tail"]
                P.op(dve, lambda e: e.tensor_tensor(out=pT.t.rearrange("p (a b) -> p a b", a=8, b=64), in0=pT.t.rearrange("p (a b) -> p a b", a=8, b=64),
                                                    in1=mskb.t[:, kbt, :].unsqueeze(2).to_broadcast([128, 8, 64]), op=ALU.mult), reads=[pT, mskb], writes=[pT])
            if tl["first"]:
                jb["pso"] = orot2.get()
                P.deps(pe, [], [jb["pso"]])
            pso = jb["pso"]
            first, last = tl["first"], tl["last"]
            P.op(pe, lambda e: e.matmul(pso.t[:, 0:NQ], vc_.t[0:RK, t, :], pT.t[0:RK, 0:NQ], start=first, stop=last),
                 reads=[vc_, pT], writes=[pso] if last else [])
            if tl["first"]:
                for ac0 in jb["acs"]:
                    P.op(dve, lambda e, ac0=ac0: e.memset(ac0.t[:, 0:NQ], 0.0), writes=[ac0])

            def acc_add(src, pidx):
                ac = jb["acs"][pidx % 2]
                rr = 128 if src is not pT else RK
                P.op(dve, lambda e: e.tensor_tensor(out=ac.t[0:rr, 0:NQ], in0=ac.t[0:rr, 0:NQ], in1=src.t[0:rr, 0:NQ], op=ALU.add), reads=[src, ac], writes=[ac])
                jb["nacc"][pidx % 2] += 1

            if kt % 2 == 0:
                if last or RK < 128:
                    acc_add(pT, kt // 2)
                else:
                    jb["prevpT"] = pT
            else:
                if RK < 128:
                    acc_add(jb["prevpT"], kt // 2)
                    acc_add(pT, kt // 2 + 1)
                else:
                    pp = jb["prevpT"]
                    tb = ptmp[(pcount[0] // 2) % 4]
                    P.op(dve, lambda e: e.tensor_tensor(out=tb.t[:, 0:NQ], in0=pp.t[:, 0:NQ], in1=pT.t[:, 0:NQ], op=ALU.add), reads=[pp, pT], writes=[tb])
                    acc_add(tb, kt // 2)
            if last:
                acs, ri, om, h, ocol = jb["acs"], jb["ri"], jb["om"], jb["h"], jb["ocol"]
                psr_ = rrot.get()
                P.op(pe, [lambda e: e.matmul(psr_.t[:, 0:NQ], ones_f.t, acs[0].t[:, 0:NQ], start=True, stop=False),
                          lambda e: e.matmul(psr_.t[:, 0:NQ], ones_f.t, acs[1].t[:, 0:NQ], start=False, stop=True)], reads=[ones_f, acs[0], acs[1]], writes=[psr_])
                P.op(act, lambda e: e.activation(out=ri.t[:, 0:NQ], in_=psr_.t[:, 0:NQ], func=AF.Ln), reads=[psr_], writes=[ri])
                P.op(act, lambda e: e.activation(out=ri.t[:, 0:NQ], in_=ri.t[:, 0:NQ], func=AF.Exp, scale=-1.0), reads=[ri], writes=[ri])
                P.op(dve, lambda e: e.tensor_tensor(out=om.t[:, h, ocol:ocol + NQ], in0=pso.t[:, 0:NQ], in1=ri.t[:, 0:NQ], op=ALU.mult), reads=[pso, ri], writes=[om])
                if jb["store"]:
                    P.dma(pool, omlaT_d[:, :, jb["row0"]:jb["row0"] + jb["T"]], om.t[:, :, 0:jb["T"]], reads=[om])

        NTL = len(tiles)
        for i in range(NTL + LOOK):
            if i < NTL:
                emit_S(i)
            if i - LOOK >= 0:
                emit_rest(i - LOOK)
        P.barrier()
        A.reset(PERSIST)

        for grp in range(2):
            Wa = A.tile([16, 1024], BF16)
            Wb = A.tile([16, 1024], BF16)
            P.dma(pool, Wa.t, wview(w_pa, grp * 1024, (grp + 1) * 1024), writes=[Wa])
            P.dma(pool, Wb.t, wview(w_pb, grp * 1024, (grp + 1) * 1024), writes=[Wb])
            ia = [A.tile([16, 512], BF16) for _ in range(2)]
            ib = [A.tile([16, 512], BF16) for _ in range(2)]
            ga = [A.tile([8, 512], BF16) for _ in range(2)]
            gb = [A.tile([8, 512], BF16) for _ in range(2)]
            mo = [A.tile([8, 512], BF16) for _ in range(2)]
            t1 = [A.tile([512], F32, dsem=False) for _ in range(2)]
            t2 = [A.tile([512], F32, dsem=False) for _ in range(2)]
            for (bi, row0, T, nt, R, is_s) in blocks():
                a_, b_, ga_, gb_, mo_ = ia[bi % 2], ib[bi % 2], ga[bi % 2], gb[bi % 2], mo[bi % 2]
                P.dma(sp, a_.t[:, :, 0:T], osgT_d[:, :, row0:row0 + T], writes=[a_])
                P.dma(sp, b_.t[:, :, 0:T], omlaT_d[:, :, row0:row0 + T], writes=[b_])
                P.dma(sp, ga_.t[:, :, 0:T], gaT_d[:, grp * 8:(grp + 1) * 8, row0:row0 + T], writes=[ga_])
                P.dma(sp, gb_.t[:, :, 0:T], gbT_d[:, grp * 8:(grp + 1) * 8, row0:row0 + T], writes=[gb_])
                for j in range(8):
                    pa = psrot.get()
                    pb = psrot.get()
                    P.op(pe, [lambda e, k=k, j=j, pa=pa, a_=a_: e.matmul(pa.t[:, 0:T], Wa.t[:, k, j * 128:(j + 1) * 128], a_.t[:, k, 0:T], start=(k == 0), stop=(k == 15)) for k in range(16)],
                         reads=[Wa, a_], writes=[pa])
                    P.op(pe, [lambda e, k=k, j=j, pb=pb, b_=b_: e.matmul(pb.t[:, 0:T], Wb.t[:, k, j * 128:(j + 1) * 128], b_.t[:, k, 0:T], start=(k == 0), stop=(k == 15)) for k in range(16)],
                         reads=[Wb, b_], writes=[pb])
                    x1_, x2_ = t1[j % 2], t2[j % 2]
                    P.op(dve, lambda e, pa=pa, x1_=x1_, ga_=ga_, j=j: e.tensor_tensor(out=x1_.t[:, 0:T], in0=pa.t[:, 0:T], in1=ga_.t[:, j, 0:T], op=ALU.mult), reads=[pa, ga_], writes=[x1_])
                    P.op(dve, lambda e, pb=pb, x2_=x2_, gb_=gb_, j=j: e.tensor_tensor(out=x2_.t[:, 0:T], in0=pb.t[:, 0:T], in1=gb_.t[:, j, 0:T], op=ALU.mult), reads=[pb, gb_], writes=[x2_])
                    P.op(dve, lambda e, x1_=x1_, x2_=x2_, mo_=mo_, j=j: e.tensor_tensor(out=mo_.t[:, j, 0:T], in0=x1_.t[:, 0:T], in1=x2_.t[:, 0:T], op=ALU.add), reads=[x1_, x2_], writes=[mo_])
                P.dma(pool, mT_d[:, grp * 8:(grp + 1) * 8, row0:row0 + T], mo_.t[:, :, 0:T], reads=[mo_])
            P.barrier()
            A.reset(PERSIST)

        Wo = A.tile([16, D], BF16)
        P.dma(pool, Wo.t[:, :, 0:1024], wview(w_o, 0, 1024), writes=[Wo])
        P.dma(pool, Wo.t[:, :, 1024:2048], wview(w_o, 1024, 2048), writes=[Wo])
        gn2 = A.tile([D], F32)
        G2, S2 = make_front_rows(g_norm2, 3 * D, 4 * D, False, gn2)
        g1r = A.tile([D], F32)
        load_rows_prompt_or_sample(g1r, mod_d[0:1, 2 * D:3 * D], None, False, D)
        fb = FrontBufs()
        mTs = [A.tile([16, 512], BF16) for _ in range(1)]
        xts = [A.tile([D], F32) for _ in range(2)]
        h2s = [A.tile([16, 512], BF16) for _ in range(1)]
        tm5 = [A.tile([512], F32, dsem=False) for _ in range(2)]
        xi = 0
        for (bi, row0, T, nt, R, is_s) in blocks():
            if is_s:
                reload_front_rows(G2, S2, g_norm2, 3 * D, 4 * D, gn2)
                load_rows_prompt_or_sample(g1r, None, [mod_d[1 + i:2 + i, 2 * D:3 * D] for i in range(SB)], True, D)
            mT, h2 = mTs[0], h2s[0]
            P.dma(sp, mT.t[:, :, 0:T], mT_d[:, :, row0:row0 + T], writes=[mT])
            for t in range(nt):
                xt = xts[xi % 2]
                xi += 1
                r0 = row0 + t * 128
                P.dma(sp, xt.t[0:R, :], xo[r0:r0 + R, :], writes=[xt])
                for c4 in range(4):
                    ps = psrot.get()
                    tm = tm5[c4 % 2]
                    P.op(pe, [lambda e, k=k, ps=ps, mT=mT, t=t, c4=c4: e.matmul(ps.t[0:R, :], mT.t[:, k, t * 128:t * 128 + R], Wo.t[:, k, c4 * 512:(c4 + 1) * 512], start=(k == 0), stop=(k == 15)) for k in range(16)],
                         reads=[Wo, mT], writes=[ps])
                    P.op(dve, lambda e, ps=ps, tm=tm, c4=c4: e.tensor_tensor(out=tm.t[0:R, :], in0=ps.t[0:R, :], in1=g1r.t[0:R, c4 * 512:(c4 + 1) * 512], op=ALU.mult), reads=[ps, g1r], writes=[tm])
                    P.op(dve, lambda e, tm=tm, xt=xt, c4=c4: e.tensor_tensor(out=xt.t[0:R, c4 * 512:(c4 + 1) * 512], in0=xt.t[0:R, c4 * 512:(c4 + 1) * 512], in1=tm.t[0:R, :], op=ALU.add), reads=[tm, xt], writes=[xt])
                P.dma(pool, x1_d[r0:r0 + R, :], xt.t[0:R, :], reads=[xt])
                front_tile(fb, xt, R, G2, S2, h2, t * 128)
            P.dma(pool, h2T_d[:, :, row0:row0 + T], h2.t[:, :, 0:T], reads=[h2])
        P.barrier()
        A.reset(PERSIST)

        for grp in range(4):
            Wu = A.tile([16, D], BF16)
            P.dma(pool, Wu.t[:, :, 0:1024], wview(w_up, grp * D, grp * D + 1024), writes=[Wu])
            P.dma(pool, Wu.t[:, :, 1024:2048], wview(w_up, grp * D + 1024, (grp + 1) * D), writes=[Wu])
            h2s = [A.tile([16, 512], BF16) for _ in range(2)]
            hid = [A.tile([16, 512], BF16) for _ in range(2)]
            sqf = [A.tile([512], F32, dsem=False) for _ in range(2)]
            for (bi, row0, T, nt, R, is_s) in blocks():
                h2, hd = h2s[bi % 2], hid[bi % 2]
                P.dma(sp, h2.t[:, :, 0:T], h2T_d[:, :, row0:row0 + T], writes=[h2])
                for j in range(16):
                    ps = psrot.get()
                    sq = sqf[j % 2]
                    P.op(pe, [lambda e, k=k, j=j, ps=ps, h2=h2: e.matmul(ps.t[:, 0:T], Wu.t[:, k, j * 128:(j + 1) * 128], h2.t[:, k, 0:T], start=(k == 0), stop=(k == 15)) for k in range(16)],
                         reads=[Wu, h2], writes=[ps])
                    P.op(act, lambda e, ps=ps, sq=sq: e.activation(out=sq.t[:, 0:T], in_=ps.t[:, 0:T], func=AF.Square), reads=[ps], writes=[sq])
                    P.op(dve, lambda e, ps=ps, sq=sq, hd=hd, j=j: e.scalar_tensor_tensor(out=hd.t[:, j, 0:T], in0=ps.t[:, 0:T], scalar=0.0, in1=sq.t[:, 0:T], op0=ALU.is_gt, op1=ALU.mult),
                         reads=[ps, sq], writes=[hd])
                P.dma(pool, hidT_d[:, grp * 16:(grp + 1) * 16, row0:row0 + T], hd.t[:, :, 0:T], reads=[hd])
            P.barrier()
            A.reset(PERSIST)

        g2r = A.tile([D], F32)
        load_rows_prompt_or_sample(g2r, mod_d[0:1, 5 * D:6 * D], None, False, D)
        g2s = A.tile([D], F32)
        load_rows_prompt_or_sample(g2s, None, [mod_d[1 + i:2 + i, 5 * D:6 * D] for i in range(SB)], True, D)
        WBASE = A.off
        for grp in range(4):
            A.reset(WBASE)
            Wd = A.tile([64, 512], BF16)
            for q in range(4):
                P.dma(pool, Wd.t[:, q * 16:(q + 1) * 16, :], wview(w_down, grp * 512, (grp + 1) * 512)[:, q * 16:(q + 1) * 16, :], writes=[Wd])
            hts = [A.tile([64, 256], BF16) for _ in range(2)]
            x1s = [A.tile([512], F32) for _ in range(2)]
            tm7 = [A.tile([512], F32, dsem=False) for _ in range(2)]
            ti = 0
            hi7 = 0
            for (bi, row0, T, nt, R, is_s) in blocks():
                gr = g2s if is_s else g2r
                TH = 256 if not is_s else NS
                for half in range(T // TH):
                    ht = hts[hi7 % 2]
                    hi7 += 1
                    c0 = row0 + half * TH
                    for q in range(2):
                        P.dma(sp, ht.t[:, q * 32:(q + 1) * 32, 0:TH], hidT_d[:, q * 32:(q + 1) * 32, c0:c0 + TH], writes=[ht])
                    for t in range(TH // R):
                        x1t, tm = x1s[ti % 2], tm7[ti % 2]
                        ti += 1
                        r0 = c0 + t * R
                        P.dma(sp, x1t.t[0:R, :], x1_d[r0:r0 + R, grp * 512:(grp + 1) * 512], writes=[x1t])
                        ps = psrot.get()
                        P.op(pe, [lambda e, k=k, ps=ps, ht=ht, t=t: e.matmul(ps.t[0:R, :], ht.t[:, k, t * R:(t + 1) * R], Wd.t[:, k, :], start=(k == 0), stop=(k == 63)) for k in range(64)],
                             reads=[Wd, ht], writes=[ps])
                        P.op(dve, lambda e, ps=ps, tm=tm, gr=gr, grp=grp: e.tensor_tensor(out=tm.t[0:R, :], in0=ps.t[0:R, :], in1=gr.t[0:R, grp * 512:(grp + 1) * 512], op=ALU.mult), reads=[ps, gr], writes=[tm])
                        P.op(dve, lambda e, tm=tm, x1t=x1t: e.tensor_tensor(out=x1t.t[0:R, :], in0=x1t.t[0:R, :], in1=tm.t[0:R, :], op=ALU.add), reads=[tm, x1t], writes=[x1t])
                        P.dma(pool, y_all[r0:r0 + R, grp * 512:(grp + 1) * 512], x1t.t[0:R, :], reads=[x1t])
            P.barrier()
        P.barrier()

        with nc.allow_non_contiguous_dma(reason="small per-partition column loads"), nc.Block() as block:
            @block.tensor
            def _(e):
                P.replay(pe, e)

            @block.scalar
            def _(e):
                P.replay(act, e)

            @block.vector
            def _(e):
                P.replay(dve, e)

            @block.gpsimd
            def _(e):
                P.replay(pool, e)

            @block.sync
            def _(e):
                P.replay(sp, e)
    return nc


_CACHE = {}


def _rope_tables(pos):
    inv = (np.float32(10000.0) ** (-(np.arange(0, ROPE, 2, dtype=np.float32) / np.float32(ROPE)))).astype(np.float32)
    ang = (pos.astype(np.float32)[:, None] * inv[None, :]).astype(np.float32)
    return np.cos(ang).astype(np.float32), np.sin(ang).astype(np.float32)


def kernel(x_prompt, x_sample, cache_kv_latent, cache_k_rope, c_prompt, c_sample,
           w_ada, b_ada, g_norm1, g_norm2, w_in, g_sg, w_s, b_s,
           g_q_a, w_uq, g_q_nope, g_q_rope, g_kv_a, g_k_rope, w_uk, g_k_nope, w_uv,
           w_pa, w_pb, w_o, w_up, w_down):
    f = lambda a: np.ascontiguousarray(np.asarray(a, dtype=np.float32))
    x_prompt, x_sample = f(x_prompt), f(x_sample)
    B, SEQ, _ = x_prompt.shape
    DB, TS_, _ = x_sample.shape
    PAST = cache_kv_latent.shape[2]
    assert B == 2 and DB == 32 and TS_ == TS
    NB = SEQ // 512
    NJ = NB // 4
    NOWN = NJ * 512
    NS = SB * TS
    NTOK = NOWN + NS
    key = (SEQ, PAST)
    if key not in _CACHE:
        _CACHE[key] = build_program(SEQ, PAST)
    nc = _CACHE[key]

    cosb, sinb = _rope_tables(np.arange(SEQ))
    ropetm = np.concatenate([cosb, sinb], axis=1)
    shared = {
        "w_ada": f(w_ada[0]), "b_ada": f(b_ada[0])[None, :], "g_norm1": f(g_norm1[0])[None, :], "g_norm2": f(g_norm2[0])[None, :],
        "w_in": f(w_in[0]), "g_sg": f(g_sg[0])[None, :], "w_s": f(w_s[0]), "b_s": f(b_s[0]).reshape(1, -1),
        "g_q_a": f(g_q_a[0])[None, :], "w_uq": f(w_uq[0]), "g_q_nope": f(g_q_nope[0])[None, :], "g_q_rope": f(g_q_rope[0])[None, :],
        "g_kv_a": f(g_kv_a[0])[None, :], "g_k_rope": f(g_k_rope[0])[None, :], "w_uk": f(w_uk[0]).reshape(KVL, NH * 128),
        "g_k_nope": f(g_k_nope[0])[None, :], "w_uv": f(w_uv[0]).reshape(KVL, NH * 128),
        "w_pa": f(w_pa[0]), "w_pb": f(w_pb[0]), "w_o": f(w_o[0]), "w_up": f(w_up[0]), "w_down": f(w_down[0]),
        "ropetm": ropetm, "ident": np.eye(128, dtype=np.float32), "tril": np.tril(np.ones((128, 128), np.float32)),
    }
    in_maps = []
    for k in range(8):
        b, c = k // 4, k % 4
        rows = np.concatenate([np.arange((4 * J + c) * 512, (4 * J + c + 1) * 512) for J in range(NJ)])
        xo = np.concatenate([x_prompt[b][rows], x_sample[4 * k:4 * k + 4].reshape(NS, D)], axis=0)
        pos_o = np.concatenate([rows, np.tile(PAST + np.arange(TS), SB)])
        co, so = _rope_tables(pos_o)
        ropeo = np.concatenate([co, so], axis=1)
        town = np.concatenate([co.T, co.T, -so.T, so.T], axis=0)
        p = np.arange(128)[:, None, None]
        kbt = np.arange(16)[None, :, None]
        qc = np.arange(8)[None, None, :]
        mask = ((kbt * 2 + p // 64) <= (c * 8 + qc)).astype(np.float32).reshape(128, 128)
        m = dict(shared)
        m.update({
            "xb": x_prompt[b], "xo": f(xo),
            "clat": f(cache_kv_latent[0, 4 * k:4 * k + 4]), "ckr": f(cache_k_rope[0, 4 * k:4 * k + 4]),
            "cvec": f(np.concatenate([np.asarray(c_prompt)[b:b + 1], np.asarray(c_sample)[4 * k:4 * k + 4]], axis=0)),
            "ropeo": f(ropeo), "town": f(town), "maskc": f(mask),
        })
        in_maps.append(m)
    res = run_bass_kernel_spmd(nc, in_maps, core_ids=list(range(8)))
    y_p = np.zeros((B, SEQ, D), np.float32)
    y_s = np.zeros((DB, TS, D), np.float32)
    lat_p = np.zeros((1, B, SEQ, KVL), np.float32)
    kr_p = np.zeros((1, B, SEQ, ROPE), np.float32)
    lat_s = np.zeros((1, DB, TS, KVL), np.float32)
    kr_s = np.zeros((1, DB, TS, ROPE), np.float32)
    v_s = np.zeros((1, DB, TS, D), np.float32)
    for k in range(8):
        b, c = k // 4, k % 4
        r = res.results[k]
        rows = np.concatenate([np.arange((4 * J + c) * 512, (4 * J + c + 1) * 512) for J in range(NJ)])
        y_p[b, rows] = r["y_all"][:NOWN]
        lat_p[0, b, rows] = r["lat_all"][:NOWN]
        kr_p[0, b, rows] = r["kr_all"][:NOWN]
        y_s[4 * k:4 * k + 4] = r["y_all"][NOWN:].reshape(SB, TS, D)
        lat_s[0, 4 * k:4 * k + 4] = r["lat_all"][NOWN:].reshape(SB, TS, KVL)
        kr_s[0, 4 * k:4 * k + 4] = r["kr_all"][NOWN:].reshape(SB, TS, ROPE)
        v_s[0, 4 * k:4 * k + 4] = r["v_all"].reshape(SB, TS, D)
    return (y_p, y_s, lat_p, kr_p, lat_s, kr_s, v_s)
```
